# Optimizing a Trainium2 kernel written in Bass

```python
import jax
import jax.numpy as jnp
from jax import lax
import numpy as np

D_MODEL = 1024
BATCH = 32
SEQ = 2048
DEPTH = 2

CHUNK = 64
Q_BLOCK = 128
EPS = 1e-6
ROPE_THETA = 10000.0

RET_HEADS = 4
RET_HEAD_DIM = 64
RET_WIDTH = RET_HEADS * RET_HEAD_DIM

MLA_HEADS = 8
MLA_NOPE = 64
MLA_ROPE = 32
MLA_V = 64
MLA_WIDTH = MLA_HEADS * MLA_V
MLA_Q_RANK = 256
MLA_KV_RANK = 128

GLA_HEADS = 4
GLA_DK = 32
GLA_DV = 64
GLA_KWIDTH = GLA_HEADS * GLA_DK
GLA_WIDTH = GLA_HEADS * GLA_DV
GLA_GATE_RANK = 16
GLA_TAU = 16.0

MIX_WIDTH = RET_WIDTH + MLA_WIDTH + GLA_WIDTH

RET_SPLIT = (RET_WIDTH, RET_WIDTH, RET_WIDTH, RET_WIDTH)
MLA_SPLIT = (MLA_Q_RANK, MLA_KV_RANK, MLA_ROPE, MLA_WIDTH)
GLA_SPLIT = (GLA_KWIDTH, GLA_KWIDTH, GLA_WIDTH, GLA_GATE_RANK, GLA_WIDTH)
IN_COLS = sum(RET_SPLIT) + sum(MLA_SPLIT) + sum(GLA_SPLIT)

kernel_name = "hybrid_retention_mla_gla_streaming_block"


def rms_norm(x, w=None):
    xf = x.astype(jnp.float32)
    y = xf * lax.rsqrt(jnp.mean(xf * xf, axis=-1, keepdims=True) + EPS)
    if w is not None:
        y = y * w.astype(jnp.float32)
    return y.astype(x.dtype)


def split_cols(t, sizes):
    cuts = [int(s) for s in np.cumsum(sizes)[:-1]]
    return jnp.split(t, cuts, axis=-1)


def to_heads(t, n_heads):
    b, s, _ = t.shape
    return t.reshape(b, s, n_heads, -1)


def rope(x, pos):
    half = x.shape[-1] // 2
    inv = ROPE_THETA ** (-jnp.arange(half, dtype=jnp.float32) / half)
    ang = pos.astype(jnp.float32)[..., None] * inv
    cos = jnp.cos(ang)[:, :, None, :]
    sin = jnp.sin(ang)[:, :, None, :]
    x1 = x[..., :half].astype(jnp.float32)
    x2 = x[..., half:].astype(jnp.float32)
    out = jnp.concatenate([x1 * cos - x2 * sin, x2 * cos + x1 * sin], axis=-1)
    return out.astype(x.dtype)


def chunk_states(u, a):
    u_t = jnp.moveaxis(u, 1, 0)
    a_t = jnp.moveaxis(a, 1, 0)

    def step(s, inp):
        ui, ai = inp
        return ai[..., None] * s + ui, s

    _, s_prev = lax.scan(step, jnp.zeros_like(u_t[0]), (u_t, a_t))
    return jnp.moveaxis(s_prev, 0, 1)


def retention(q, k, v):
    b, s, h, d = q.shape
    dv = v.shape[-1]
    nc = s // CHUNK
    f32 = jnp.float32
    log_gamma = jnp.log1p(-jnp.exp2(-5.0 - jnp.arange(h, dtype=f32)))
    q = q.astype(f32).reshape(b, nc, CHUNK, h, d)
    k = k.astype(f32).reshape(b, nc, CHUNK, h, d) * (d ** -0.5)
    v = v.astype(f32).reshape(b, nc, CHUNK, h, dv)
    idx = jnp.arange(CHUNK, dtype=f32)
    decay = jnp.exp(log_gamma[:, None, None] * jnp.abs(idx[:, None] - idx[None, :]))
    scores = jnp.einsum('bnihd,bnjhd->bnhij', q, k) * decay
    intra = jnp.einsum('bnhij,bnjhe->bnihe', scores, v)
    k_w = jnp.exp((CHUNK - 1.0 - idx)[:, None] * log_gamma[None, :])
    u = jnp.einsum('bnjhd,jh,bnjhe->bnhde', k, k_w, v)
    a = jnp.broadcast_to(jnp.exp(CHUNK * log_gamma)[:, None], (b, nc, h, d))
    s_prev = chunk_states(u, a)
    q_w = jnp.exp((idx + 1.0)[:, None] * log_gamma[None, :])
    inter = jnp.einsum('bnihd,ih,bnhde->bnihe', q, q_w, s_prev)
    return (intra + inter).reshape(b, s, h, dv)


def mla(q_lat, kv_lat, k_rope, q_norm_w, w_uq, kv_norm_w, w_ukv, pos):
    b, s, _ = q_lat.shape
    q = (rms_norm(q_lat, q_norm_w) @ w_uq).reshape(b, s, MLA_HEADS, MLA_NOPE + MLA_ROPE)
    q_nope = q[..., :MLA_NOPE]
    q_pe = rope(q[..., MLA_NOPE:], pos)
    kv = (rms_norm(kv_lat, kv_norm_w) @ w_ukv).reshape(b, s, MLA_HEADS, MLA_NOPE + MLA_V)
    k_nope = kv[..., :MLA_NOPE]
    v = kv[..., MLA_NOPE:]
    k_pe = rope(k_rope[:, :, None, :], pos)[:, :, 0, :]
    scale = (MLA_NOPE + MLA_ROPE) ** -0.5
    key_chunk = jnp.arange(s) // CHUNK
    nb = s // Q_BLOCK
    qn_b = jnp.moveaxis(q_nope.reshape(b, nb, Q_BLOCK, MLA_HEADS, MLA_NOPE), 1, 0)
    qp_b = jnp.moveaxis(q_pe.reshape(b, nb, Q_BLOCK, MLA_HEADS, MLA_ROPE), 1, 0)
    starts = jnp.arange(nb, dtype=jnp.int32) * Q_BLOCK

    def block(args):
        qn, qp, i0 = args
        sc = (jnp.einsum('bqhd,bkhd->bhqk', qn, k_nope)
              + jnp.einsum('bqhd,bkd->bhqk', qp, k_pe)).astype(jnp.float32) * scale
        q_chunk = (i0 + jnp.arange(Q_BLOCK)) // CHUNK
        mask = key_chunk[None, :] <= q_chunk[:, None]
        sc = jnp.where(mask, sc, -jnp.inf)
        p = jax.nn.softmax(sc, axis=-1).astype(v.dtype)
        return jnp.einsum('bhqk,bkhe->bqhe', p, v)

    out = lax.map(block, (qn_b, qp_b, starts))
    return jnp.moveaxis(out, 0, 1).reshape(b, s, MLA_WIDTH)


def gla(q, k, v, g_low, w_g2, b_g2):
    b, s, h, dk = q.shape
    dv = v.shape[-1]
    nc = s // CHUNK
    f32 = jnp.float32
    log_a = jax.nn.log_sigmoid((g_low @ w_g2 + b_g2).astype(f32)) / GLA_TAU
    cum = jnp.cumsum(log_a.reshape(b, nc, CHUNK, h, dk), axis=2)
    q = q.astype(f32).reshape(b, nc, CHUNK, h, dk)
    k = k.astype(f32).reshape(b, nc, CHUNK, h, dk) * (dk ** -0.5)
    v = v.astype(f32).reshape(b, nc, CHUNK, h, dv)
    e_pos = jnp.exp(cum)
    e_neg = jnp.exp(-cum)
    q_pos = q * e_pos
    past = jnp.einsum('bnihd,bnjhd->bnhij', q_pos, k * e_neg)
    fut = jnp.einsum('bnihd,bnjhd->bnhij', q * e_neg, k * e_pos)
    idx = jnp.arange(CHUNK)
    attn = jnp.where(idx[:, None] >= idx[None, :], past, fut)
    intra = jnp.einsum('bnhij,bnjhe->bnihe', attn, v)
    last = cum[:, :, -1]
    u = jnp.einsum('bnjhd,bnjhe->bnhde', k * jnp.exp(last[:, :, None] - cum), v)
    s_prev = chunk_states(u, jnp.exp(last))
    inter = jnp.einsum('bnihd,bnhde->bnihe', q_pos, s_prev)
    return (intra + inter).reshape(b, s, h, dv)


def setup_inputs(seed: int = 0) -> dict:
    key = jax.random.key(seed)
    ks = jax.random.split(key, 16)
    f32 = jnp.float32

    def normal(k, shape, scale):
        return jax.random.normal(k, shape, f32) * scale

    def gain(k, shape):
        return 1.0 + 0.02 * jax.random.normal(k, shape, f32)

    x = normal(ks[0], (BATCH, SEQ, D_MODEL), 1.0)
    c = normal(ks[1], (BATCH, D_MODEL), 1.0)
    offset = jax.random.randint(ks[2], (BATCH, 1), 0, 4096, dtype=jnp.int32)
    positions = offset + jnp.arange(SEQ, dtype=jnp.int32)[None, :]
    norm_w = gain(ks[3], (DEPTH, D_MODEL))
    ada_w = normal(ks[4], (DEPTH, D_MODEL, 3 * D_MODEL), 0.5 * D_MODEL ** -0.5)
    ada_b = normal(ks[5], (DEPTH, 3 * D_MODEL), 0.02)
    w_in = normal(ks[6], (DEPTH, D_MODEL, IN_COLS), D_MODEL ** -0.5)
    mla_q_norm = gain(ks[7], (DEPTH, MLA_Q_RANK))
    w_uq = normal(ks[8], (DEPTH, MLA_Q_RANK, MLA_HEADS * (MLA_NOPE + MLA_ROPE)), MLA_Q_RANK ** -0.5)
    mla_kv_norm = gain(ks[9], (DEPTH, MLA_KV_RANK))
    w_ukv = normal(ks[10], (DEPTH, MLA_KV_RANK, MLA_HEADS * (MLA_NOPE + MLA_V)), MLA_KV_RANK ** -0.5)
    gla_w_g2 = normal(ks[11], (DEPTH, GLA_GATE_RANK, GLA_KWIDTH), GLA_GATE_RANK ** -0.5)
    gla_b_g2 = normal(ks[12], (DEPTH, GLA_KWIDTH), 0.02)
    gla_norm = gain(ks[13], (DEPTH, GLA_DV))
    w_out = normal(ks[14], (DEPTH, MIX_WIDTH, D_MODEL), MIX_WIDTH ** -0.5)
    final_norm = gain(ks[15], (D_MODEL,))
    return {"x": x, "c": c, "positions": positions, "norm_w": norm_w, "ada_w": ada_w,
            "ada_b": ada_b, "w_in": w_in, "mla_q_norm": mla_q_norm, "w_uq": w_uq,
            "mla_kv_norm": mla_kv_norm, "w_ukv": w_ukv, "gla_w_g2": gla_w_g2,
            "gla_b_g2": gla_b_g2, "gla_norm": gla_norm, "w_out": w_out,
            "final_norm": final_norm}


def reference(x, c, positions, norm_w, ada_w, ada_b, w_in, mla_q_norm, w_uq, mla_kv_norm,
              w_ukv, gla_w_g2, gla_b_g2, gla_norm, w_out, final_norm):
    b, s, _ = x.shape
    c_act = jax.nn.silu(c)
    for l in range(DEPTH):
        shift, scale, gate = jnp.split(c_act @ ada_w[l] + ada_b[l], 3, axis=-1)
        h = rms_norm(x, norm_w[l]) * (1.0 + scale[:, None, :]) + shift[:, None, :]
        proj = h @ w_in[l]
        ret_p, mla_p, gla_p = split_cols(proj, (sum(RET_SPLIT), sum(MLA_SPLIT), sum(GLA_SPLIT)))

        rq, rk, rv, rz = split_cols(ret_p, RET_SPLIT)
        r_o = retention(rope(to_heads(rq, RET_HEADS), positions),
                        rope(to_heads(rk, RET_HEADS), positions),
                        to_heads(rv, RET_HEADS))
        r_o = rms_norm(r_o).astype(x.dtype).reshape(b, s, RET_WIDTH)

        mq, mkv, mkr, mz = split_cols(mla_p, MLA_SPLIT)
        m_o = mla(mq, mkv, mkr, mla_q_norm[l], w_uq[l], mla_kv_norm[l], w_ukv[l], positions)

        gq, gk, gv, gg, gz = split_cols(gla_p, GLA_SPLIT)
        g_o = gla(to_heads(gq, GLA_HEADS), to_heads(gk, GLA_HEADS), to_heads(gv, GLA_HEADS),
                  gg, gla_w_g2[l], gla_b_g2[l])
        g_o = rms_norm(g_o, gla_norm[l]).astype(x.dtype).reshape(b, s, GLA_WIDTH)

        mixed = jnp.concatenate([r_o * jax.nn.silu(rz), m_o * jax.nn.silu(mz),
                                 g_o * jax.nn.silu(gz)], axis=-1)
        x = x + gate[:, None, :] * (mixed @ w_out[l])
    return rms_norm(x, final_norm)
```

```python
import math
from contextlib import ExitStack

import numpy as np
import concourse.bass as bass
import concourse.mybir as mybir
from concourse.bass_utils import run_bass_kernel_spmd

F32 = mybir.dt.float32
BF16 = mybir.dt.bfloat16
I32 = mybir.dt.int32
AF = mybir.ActivationFunctionType
ALU = mybir.AluOpType
AX = mybir.AxisListType

D = 1024
INC = 2736
EPS = 1e-6
NCORES = 8
PI = math.pi
TWO_PI = 2.0 * math.pi
C1 = 6.28125
C2 = TWO_PI - 6.28125
PI_LO = 3.1415925

_off = {}
_cur = 0


def _reg(name, n):
    global _cur
    _off[name] = (_cur, _cur + n)
    _cur += n


_reg("ident", 128)
_reg("maskR", 512)
_reg("tri", 128)
_reg("ones", 128)
_reg("trimo", 128)
_reg("M1s", 128)
_reg("M2s", 128)
_reg("wqT", 256)
_reg("wk", 4)
_reg("g128", 2)
_reg("inv48", 48)
_reg("hm", 4)
NCONST_BASE = _cur


def _const_pack(L, norm_w, mla_q_norm, mla_kv_norm, gla_norm, b_g2):
    f = np.float32
    h = np.arange(4, dtype=f)
    log_gamma = np.log1p(-np.exp2(-5.0 - h)).astype(f)
    idx = np.arange(128, dtype=f)
    cj = (np.arange(128) // 64)
    maskR = np.zeros((128, 4, 128), f)
    for hh in range(4):
        dist = np.abs(idx[None, :] - idx[:, None])
        dec = np.exp(log_gamma[hh] * dist).astype(f)
        same = cj[:, None] == cj[None, :]
        past = cj[:, None] < cj[None, :]
        maskR[:, hh, :] = np.where(same | past, dec, 0.0) * f(0.125)
    tri = (np.arange(128)[:, None] <= np.arange(128)[None, :]).astype(f)
    ones = np.ones((128, 128), f)
    trimo = tri - ones
    sc = f(32 ** -0.5)
    M1s = tri * sc
    M2s = ((np.arange(128)[:, None] > np.arange(128)[None, :]) & (cj[:, None] == cj[None, :])).astype(f) * sc
    wqT = np.zeros((128, 2, 128), f)
    g128 = np.zeros((128, 2), f)
    for pair in range(2):
        for hh in range(2):
            hd = 2 * pair + hh
            wqT[64 * hh:64 * hh + 64, pair, :] = np.exp(log_gamma[hd] * (idx + 1.0))[None, :]
            g128[64 * hh:64 * hh + 64, pair] = np.exp(log_gamma[hd] * 128.0)
    wk = np.exp(log_gamma[None, :] * (127.0 - idx)[:, None]).astype(f) * f(0.125)
    inv_r = (10000.0 ** (-np.arange(32, dtype=f) / 32)).astype(f)
    inv_m = (10000.0 ** (-np.arange(16, dtype=f) / 16)).astype(f)
    inv48 = np.concatenate([inv_r, inv_m])[None, :].repeat(128, 0)
    parts = [np.eye(128, dtype=f), maskR.reshape(128, 512), tri, ones, trimo, M1s, M2s,
             wqT.reshape(128, 256), wk, g128, inv48,
             (np.arange(128)[:, None] // 32 == np.arange(4)[None, :]).astype(f)]
    nwT = norm_w.reshape(L, 8, 128).transpose(2, 0, 1).reshape(128, L * 8)
    qnT = mla_q_norm.reshape(L, 2, 128).transpose(2, 0, 1).reshape(128, L * 2)
    kvT = mla_kv_norm.reshape(L, 1, 128).transpose(2, 0, 1).reshape(128, L)
    gn = np.broadcast_to(gla_norm.reshape(1, L * 64), (128, L * 64))
    bg = np.broadcast_to(b_g2.reshape(1, L * 128), (128, L * 128))
    parts += [nwT, qnT, kvT, gn, bg]
    return np.ascontiguousarray(np.concatenate([p.astype(f) for p in parts], axis=1))


class Res:
    __slots__ = ("name", "w", "r", "rd", "excl")

    def __init__(self, name, excl=False):
        self.name = name
        self.excl = excl
        self.w = None
        self.r = {}
        self.rd = []


class Op:
    __slots__ = ("eng", "fn", "waits", "flag", "sem", "val", "dma")

    def __init__(self, eng, fn, dma):
        self.eng = eng
        self.fn = fn
        self.waits = []
        self.flag = False
        self.sem = None
        self.val = 0
        self.dma = dma


ENGS = ("pe", "act", "dve", "pool", "sp")
SYNC_SAME_ENGINE_WAR = True


class Sched:
    def __init__(self):
        self.ops = {e: [] for e in ENGS}
        self.all = []

    def add(self, eng, fn, reads=(), writes=(), dma=False):
        op = Op(eng, fn, dma)
        deps = []
        for r in reads:
            if r.w is not None:
                deps.append((r.w, "raw"))
            if r.excl:
                for en, o in r.r.items():
                    if en != eng:
                        deps.append((o, "war"))
        for r in writes:
            if r.w is not None:
                deps.append((r.w, "waw"))
            for o in r.r.values():
                deps.append((o, "war"))
            for o in r.rd:
                deps.append((o, "war"))
        seen = set()
        for d, kind in deps:
            if d is op or id(d) in seen:
                continue
            if d.eng == eng and not d.dma and not dma:
                if eng == "pe":
                    continue
                if kind != "raw" and not SYNC_SAME_ENGINE_WAR:
                    continue
            seen.add(id(d))
            op.waits.append(d)
            d.flag = True
        for r in reads:
            if dma:
                r.rd.append(op)
            else:
                r.r[eng] = op
        for r in writes:
            r.w = op
            r.r = {}
            r.rd = []
        self.ops[eng].append(op)
        self.all.append(op)
        return op

    def finalize(self, sems, dsems):
        cnt = {e: 0 for e in ENGS}
        pools = {"sp": dsems[:16], "pool": dsems[16:]}
        di = {"sp": 0, "pool": 0}
        last_on = {"sp": [None] * 16, "pool": [None] * (len(dsems) - 16)}
        for op in self.all:
            if op.dma:
                pl = pools[op.eng]
                nd = len(pl)
                j = di[op.eng] % nd
                op.sem = pl[j]
                op.val = 16 * (di[op.eng] // nd + 1)
                op.flag = True
                if last_on[op.eng][j] is not None:
                    op.waits.append(last_on[op.eng][j])
                last_on[op.eng][j] = op
                di[op.eng] += 1
            elif op.flag:
                cnt[op.eng] += 1
                op.sem = sems[op.eng]
                op.val = cnt[op.eng]
        return cnt

    def replay(self, eng, e):
        seen = {}
        for op in self.ops[eng]:
            for w in op.waits:
                key = id(w.sem)
                if seen.get(key, 0) >= w.val:
                    continue
                seen[key] = w.val
                e.wait_ge(w.sem, w.val)
            if op.fn is None:
                continue
            ins = op.fn(e)
            if op.flag:
                ins.then_inc(op.sem, 16 if op.dma else 1)


class StopBuild(Exception):
    pass


def build(NSEQ, NT, L):
    import os
    STOP = int(os.environ.get("KSTOP", "99"))
    PIPE = int(os.environ.get("KPIPE", "1"))

    def stage_(n):
        if STOP == n:
            raise StopBuild()
    S_LEN = NT * 128
    nc = bass.Bass("TRN2", target_bir_lowering=False)
    NCONST = NCONST_BASE + L * 8 + L * 2 + L + L * 64 + L * 128
    o_nw = NCONST_BASE
    o_qn = o_nw + L * 8
    o_kv = o_qn + L * 2
    o_gn = o_kv + L
    o_bg = o_gn + L * 64

    def din(name, shape, dt=F32):
        return nc.dram_tensor(name, list(shape), dt, kind="ExternalInput").ap()

    x_d = din("x", [NSEQ, S_LEN, D])
    cT_d = din("cT", [128, 8 * NSEQ])
    pos_d = din("pos", [128, NSEQ * NT], I32)
    consts_d = din("consts", [128, NCONST])
    fnorm_d = din("fnorm", [128, D])
    ada_w_d = din("ada_w", [L, D, 3 * D])
    ada_b_d = din("ada_b", [NSEQ, L * 3 * D])
    w_in_d = din("w_in", [L, D, INC])
    w_uq_d = din("w_uq", [L, 256, 768])
    w_ukv_d = din("w_ukv", [L, 128, 1024])
    w_g2_d = din("w_g2", [L, 16, 128])
    b_g2_d = din("b_g2", [L, 128])
    w_out_d = din("w_out", [L, D, D])
    y_d = nc.dram_tensor("y", [NSEQ, S_LEN, D], F32, kind="ExternalOutput").ap()
    mods_d = nc.dram_tensor("mods", [L, NSEQ, 3 * D], F32, kind="Internal").ap()

    S = Sched()
    with ExitStack() as es:
        def sb(name, shape, dt=F32):
            return es.enter_context(nc.sbuf_tensor("sb_" + name, list(shape), dt))

        def R(name):
            return Res(name)

        cst = sb("cst", [128, NCONST]); r_cst = R("cst")
        fnorm = sb("fnorm", [128, D]); r_fnorm = R("fnorm")
        ident_b = sb("ident_b", [128, 128], BF16); r_ident = R("ident")
        w_in_b = sb("w_in_b", [128, 8, INC], BF16); r_win = R("win")
        w_out_b = sb("w_out_b", [128, 8, D], BF16); r_wout = R("wout")
        w_uq_b = sb("w_uq_b", [128, 2, 768], BF16); r_wuq = R("wuq")
        w_ukv_b = sb("w_ukv_b", [128, 1024], BF16); r_wukv = R("wukv")
        w_g2_b = sb("w_g2_b", [32, 128], BF16); r_wg2 = R("wg2")
        KT = sb("KT", [96, 8, S_LEN], BF16); r_KT = [R("KT%d" % t) for t in range(NT)]
        VA = sb("VA", [128, NT, 8, 65], BF16); r_VA = [R("VA%d" % t) for t in range(NT)]
        stage = sb("stage", [128, 1024]); r_stage = R("stage")
        gate_bc = sb("gate_bc", [128, D], BF16); r_gate = R("gate")
        modc = sb("modc", [128, 16]); r_modc = R("modc")
        gp = sb("gp", [128, 8]); r_gp = R("gp")
        modg = sb("modg", [NSEQ, 128]); r_modg = R("modg")
        r_modsd = [[R("modsd%d_%d" % (l_, g_)) for g_ in range(24)] for l_ in range(L)]
        adab = sb("adab", [NSEQ, 128]); r_adab = R("adab")
        cact = sb("cact", [128, 8 * NSEQ]); r_cact = R("cact")
        ctmp = sb("ctmp", [128, 8 * NSEQ]); r_ctmp = R("ctmp")
        cin = sb("cin", [128, 8 * NSEQ]); r_cin = R("cin")
        pos_i = sb("pos_i", [128, NSEQ * NT], I32); r_posi = R("posi")
        CH = min(2, NT)
        pos_f = sb("pos_f", [128, NT]); r_posf = R("posf")
        ang = sb("ang", [128, CH, 48]); r_ang = R("ang")
        rr = sb("rr", [128, CH, 48]); r_rr = R("rr")
        kk_i = sb("kk_i", [128, CH, 48], I32); r_kki = R("kki")
        kk_f = sb("kk_f", [128, CH, 48]); r_kkf = R("kkf")
        mm_ = sb("mm_", [128, CH, 48]); r_mm = R("mm")
        rc = sb("rc", [128, CH, 48]); r_rc = R("rc")
        Tsin = sb("Tsin", [128, NT, 48]); r_Tsin = R("Tsin")
        Tcos = sb("Tcos", [128, NT, 48]); r_Tcos = R("Tcos")
        xt = [sb("xt%d" % i, [128, D]) for i in range(3)]; r_xt = [R("xt0"), R("xt1"), R("xt2")]
        st_ = sb("st_", [128, 16]); r_st = R("st")
        st_r = sb("st_r", [128, 16]); r_str = R("str")
        st_g = sb("st_g", [128, 16]); r_stg = R("stg")
        st_m = sb("st_m", [128, 16]); r_stm = R("stm")
        xn = sb("xn", [128, D], BF16); r_xn = R("xn")
        hTs = [sb("hT%d" % i, [128, 8, 128], BF16) for i in range(2)]
        r_hTs = [[R("hT%da" % i), R("hT%db" % i)] for i in range(2)]
        qkf = sb("qkf", [128, 512]); r_qkf = R("qkf")
        tA = sb("tA", [128, 256]); r_tA = R("tA")
        tB = sb("tB", [128, 256]); r_tB = R("tB")
        qkrot = sb("qkrot", [128, 8, 64], BF16); r_qkrot = R("qkrot")
        kdec = sb("kdec", [128, 4, 64], BF16); r_kdec = R("kdec")
        qT_m = sb("qT_m", [128, 4, 128], BF16); r_qTm = R("qTm")
        qTd_m = sb("qTd_m", [128, 4, 128], BF16); r_qTdm = R("qTdm")
        kTr = sb("kTr", [128, 2, 128], BF16); r_kTr = R("kTr")
        STm = sb("STm", [128, 4, 128], BF16); r_STm = R("STm")
        vb_r = sb("vb_r", [128, 256], BF16); r_vbr = R("vbr")
        S32 = sb("S32", [128, 2, 64]); r_S32 = R("S32")
        S_b = sb("S_b", [128, 4, 64], BF16); r_Sb = R("Sb")
        glow_b = sb("glow_b", [128, 32], BF16); r_glowb = R("glowb")
        glowT = sb("glowT", [32, 128], BF16); r_glowT = R("glowT")
        lf = sb("lf", [128, 128]); r_lf = R("lf")
        e3s = [sb("e3_%d" % i, [128, 3, 128]) for i in range(2)]; r_e3s = [R("e3_0"), R("e3_1")]
        qk4 = sb("qk4", [128, 4, 128], BF16); r_qk4 = R("qk4")
        kdg = sb("kdg", [128, 128], BF16); r_kdg = R("kdg")
        qpT_m = sb("qpT_m", [128, 4, 128], BF16); r_qpTm = R("qpTm")
        qnT_m = sb("qnT_m", [128, 4, 128], BF16); r_qnTm = R("qnTm")
        kTg = sb("kTg", [128, 2, 128], BF16); r_kTg = R("kTg")
        A1 = sb("A1", [128, 4, 128], BF16); r_A1 = R("A1")
        A2 = sb("A2", [128, 4, 128], BF16); r_A2 = R("A2")
        vb_g = sb("vb_g", [128, 256], BF16); r_vbg = R("vbg")
        Sg32 = sb("Sg32", [128, 64]); r_Sg32 = R("Sg32")
        Sg_b = sb("Sg_b", [128, 64], BF16); r_Sgb = R("Sgb")
        edecs = [sb("edec%d" % i, [128, 1]) for i in range(2)]; r_edecs = [R("edec0"), R("edec1")]
        latn = sb("latn", [128, 384], BF16); r_latn = R("latn")
        latT = sb("latT", [128, 3, 128], BF16); r_latT = R("latT")
        Qb = sb("Qb", [128, 8, 96], BF16); r_Qb = R("Qb")
        Kfull = sb("Kfull", [128, 8, 96], BF16); r_Kfull = R("Kfull")
        kpe = sb("kpe", [128, 32], BF16); r_kpe = R("kpe")
        krf = sb("krf", [128, 32]); r_krf = R("krf")
        QT2 = [sb("QT%d" % i, [96, 8, 128], BF16) for i in range(2)]; r_QT2 = [R("QT0"), R("QT1")]
        PT = [sb("PT%d" % i, [128, 8, 128], BF16) for i in range(2)]; r_PT = [[R("PT0a"), R("PT0b")], [R("PT1a"), R("PT1b")]]
        rs = sb("rs", [128, 8]); r_rs = R("rs")
        eg = sb("eg", [128, 256]); r_eg = R("eg")
        sg = sb("sg", [128, 256]); r_sg = R("sg")
        gA = sb("gA", [128, 256], BF16); r_gA = R("gA")
        gA_g = sb("gA_g", [128, 256], BF16); r_gAg = R("gAg")
        gB = sb("gB", [128, 256], BF16); r_gB = R("gB")
        gB_gs = [sb("gB_g%d" % i, [128, 256], BF16) for i in range(2)]; r_gBgs = [R("gBg0"), R("gBg1")]
        mixed2 = [sb("mixed%d" % i, [128, D], BF16) for i in range(2)]; r_mixed2 = [R("mixed0"), R("mixed1")]
        gBm = [sb("gBm%d" % i, [128, 512], BF16) for i in range(2)]; r_gBm = [R("gBm0"), R("gBm1")]
        gAm = sb("gAm", [128, 512]); r_gAm = R("gAm")
        st_b = sb("st_b", [128, 4]); r_stb = R("stb")
        mixT = sb("mixT", [128, 8, 128], BF16); r_mixT = R("mixT")

        PP = [es.enter_context(nc.psum_tensor("pp%d" % i, [128, 1024], F32)) for i in range(4)]
        r_bank = [[Res("b%d_%d" % (i, j), excl=True) for j in range(2)] for i in range(4)]
        gen_banks = [(0, 0), (0, 1), (2, 0), (2, 1)]
        gstate = {"i": 0, "sc": 0}

        def gbank():
            i, j = gen_banks[gstate["i"] % 4]
            gstate["i"] += 1
            return PP[i][:, j * 512:(j + 1) * 512], r_bank[i][j]

        sems = {e: es.enter_context(nc.semaphore("s_" + e)) for e in ENGS}
        dsems = [es.enter_context(nc.semaphore("d%d" % i)) for i in range(24)]

        def C(name):
            a, b = _off[name]
            return cst[:, a:b]

        def dma(out, in_, reads, writes, eng="sp", **kw):
            return S.add(eng, lambda e: e.dma_start(out=out, in_=in_, **kw), reads, writes, dma=True)

        def mm(out, lhsT, rhs, start, reads, writes):
            return S.add("pe", lambda e: e.matmul(out, lhsT=lhsT, rhs=rhs, start=start, stop=True,
                                                  skip_group_check=True), reads, writes)

        def tr(out, in_, reads, writes):
            return S.add("pe", lambda e: e.transpose(out, in_, ident_b[:]), list(reads) + [r_ident], writes)

        def act(out, in_, func, reads, writes, scale=1.0, bias=0.0, accum=None, eng="act"):
            if accum is None:
                return S.add(eng, lambda e: e.activation(out=out, in_=in_, func=func, scale=scale, bias=bias),
                             reads, writes)
            return S.add(eng, lambda e: e.activation(out=out, in_=in_, func=func, scale=scale, bias=bias,
                                                     accum_out=accum), reads, writes)

        def tt(out, a, b, op, reads, writes, eng="dve"):
            return S.add(eng, lambda e: e.tensor_tensor(out=out, in0=a, in1=b, op=op), reads, writes)

        def ts(out, a, s1, s2, op0, op1, reads, writes, eng="dve"):
            if op1 is None:
                return S.add(eng, lambda e: e.tensor_scalar(out=out, in0=a, scalar1=s1, scalar2=None, op0=op0),
                             reads, writes)
            return S.add(eng, lambda e: e.tensor_scalar(out=out, in0=a, scalar1=s1, scalar2=s2, op0=op0, op1=op1),
                         reads, writes)

        def stt(out, a, scalar, b, op0, op1, reads, writes):
            return S.add("dve", lambda e: e.scalar_tensor_tensor(out=out, in0=a, scalar=scalar, in1=b,
                                                                 op0=op0, op1=op1), reads, writes)

        def cp(out, in_, reads, writes, eng="dve"):
            if eng == "act":
                return act(out, in_, AF.Copy, reads, writes)
            return S.add(eng, lambda e: e.tensor_copy(out=out, in_=in_), reads, writes)

        def memset(ap, val, writes, eng="dve"):
            return S.add(eng, lambda e: e.memset(ap, val), (), writes)

        def rstd_from(ssq_ap, out_ap, n, reads, writes_tmp, tmp_ap):
            act(tmp_ap, ssq_ap, AF.Ln, reads, writes_tmp, scale=1.0 / n, bias=EPS)
            act(out_ap, tmp_ap, AF.Exp, writes_tmp, writes_tmp, scale=-0.5)

        def sigmoid_from(z_ap, z_reads, width):
            act(eg[:, 0:width], z_ap, AF.Exp, z_reads, [r_eg], scale=-1.0)
            act(sg[:, 0:width], eg[:, 0:width], AF.Ln, [r_eg], [r_sg], bias=1.0)
            act(sg[:, 0:width], sg[:, 0:width], AF.Exp, [r_sg], [r_sg], scale=-1.0)

        try:
            dma(cst[:], consts_d[:, :], [], [r_cst])
            dma(fnorm[:], fnorm_d[:, :], [], [r_fnorm])
            dma(cin[:], cT_d[:, :], [], [r_cin])
            dma(pos_i[:], pos_d[:, :], [], [r_posi])
            cp(ident_b[:], C("ident"), [r_cst], [r_ident])
            for buf, rb in ((qT_m, r_qTm), (qTd_m, r_qTdm), (qpT_m, r_qpTm), (qnT_m, r_qnTm), (S_b, r_Sb)):
                memset(buf[:], 0.0, [rb])
            memset(VA[:], 1.0, r_VA)
            memset(w_g2_b[:], 0.0, [r_wg2])
            act(ctmp[:], cin[:], AF.Exp, [r_cin], [r_ctmp], scale=-1.0)
            ts(ctmp[:], ctmp[:], 1.0, None, ALU.add, None, [r_ctmp], [r_ctmp])
            S.add("dve", lambda e: e.reciprocal(out=ctmp[:], in_=ctmp[:]), [r_ctmp], [r_ctmp])
            tt(cact[:], cin[:], ctmp[:], ALU.mult, [r_cin, r_ctmp], [r_cact])
            cact3 = cact[:].rearrange("p (k s) -> p k s", s=NSEQ)
            def ada_group(l, g, bk, rbk):
                st3 = stage[:, 0:1024].rearrange("p (k c) -> p k c", c=128)
                dma(st3, ada_w_d[l, :, g * 128:(g + 1) * 128].rearrange("(k p) c -> p k c", p=128),
                    [], [r_stage])
                dma(adab[:], ada_b_d[:, l * 3 * D + g * 128: l * 3 * D + (g + 1) * 128], [], [r_adab])
                for k in range(8):
                    mm(bk[0:NSEQ, 0:128], cact3[:, k, :], st3[:, k, :], k == 0, [r_cact, r_stage], [rbk])
                tt(modg[:], bk[0:NSEQ, 0:128], adab[:], ALU.add, [rbk, r_adab], [r_modg])
                dma(mods_d[l, :, g * 128:(g + 1) * 128], modg[:], [r_modg], [r_modsd[l][g]])

            for g in range(24):
                bk, rbk = gbank()
                ada_group(0, g, bk, rbk)
            ada_todo = [(l_, g_) for l_ in range(1, L) for g_ in range(24)]

            stage_(1)
            r_y = [[R("y%d_%d" % (s, t)) for t in range(NT)] for s in range(NSEQ)]
            loaded_layer = {"l": None}
            xslot = {"i": 0}

            def rope_tables(s):
                cp(pos_f[:], pos_i[:, s * NT:(s + 1) * NT], [r_posi], [r_posf])
                for c0 in range(0, NT, CH):
                    tt(ang[:], pos_f[:, c0:c0 + CH].unsqueeze(2).to_broadcast([128, CH, 48]),
                       C("inv48").unsqueeze(1).to_broadcast([128, CH, 48]), ALU.mult, [r_posf, r_cst], [r_ang])
                    ts(rr[:], ang[:], 1.0 / TWO_PI, None, ALU.mult, None, [r_ang], [r_rr])
                    cp(kk_i[:], rr[:], [r_rr], [r_kki])
                    cp(kk_f[:], kk_i[:], [r_kki], [r_kkf])
                    stt(rr[:], kk_f[:], -C1, ang[:], ALU.mult, ALU.add, [r_kkf, r_ang], [r_rr])
                    stt(rr[:], kk_f[:], -C2, rr[:], ALU.mult, ALU.add, [r_kkf, r_rr], [r_rr])

                    def wrap(buf, rb):
                        ts(mm_[:], buf[:], PI, None, ALU.is_gt, None, [rb], [r_mm])
                        stt(buf[:], mm_[:], -TWO_PI, buf[:], ALU.mult, ALU.add, [r_mm, rb], [rb])
                        ts(mm_[:], buf[:], -PI, None, ALU.is_lt, None, [rb], [r_mm])
                        stt(buf[:], mm_[:], TWO_PI, buf[:], ALU.mult, ALU.add, [r_mm, rb], [rb])
                        ts(buf[:], buf[:], -PI_LO, PI_LO, ALU.max, ALU.min, [rb], [rb])

                    wrap(rr, r_rr)
                    ts(rc[:], rr[:], PI / 2, None, ALU.add, None, [r_rr], [r_rc])
                    wrap(rc, r_rc)
                    act(Tsin[:, c0:c0 + CH, :], rr[:], AF.Sin, [r_rr], [r_Tsin])
                    act(Tcos[:, c0:c0 + CH, :], rc[:], AF.Sin, [r_rc], [r_Tcos])

            for l in range(L):
                while ada_todo and ada_todo[0][0] <= l:
                    l_, g_ = ada_todo.pop(0)
                    bk, rbk = gbank()
                    ada_group(l_, g_, bk, rbk)
                for s in range(NSEQ):
                    rope_tables(s)
                    last_layer = (l == L - 1)
                    dma(modc[:, 0:8], mods_d[l, s, 0:D].rearrange("(k p) -> p k", p=128), r_modsd[l][0:8], [r_modc],
                        allow_slow_non_contiguous=True)
                    dma(modc[:, 8:16], mods_d[l, s, D:2 * D].rearrange("(k p) -> p k", p=128), r_modsd[l][8:16], [r_modc],
                        allow_slow_non_contiguous=True)
                    dma(stage[0:1, 0:D], mods_d[l, s:s + 1, 2 * D:3 * D], r_modsd[l][16:24], [r_stage])
                    stt(gp[:], modc[:, 8:16], 1.0, cst[:, o_nw + l * 8:o_nw + (l + 1) * 8], ALU.add, ALU.mult,
                        [r_modc, r_cst], [r_gp])
                    for hf in range(2):
                        bk, rbk = gbank()
                        mm(bk, C("ones")[0:1, :], stage[0:1, hf * 512:(hf + 1) * 512], True, [r_cst, r_stage], [rbk])
                        cp(gate_bc[:, hf * 512:(hf + 1) * 512], bk, [rbk], [r_gate], eng="act")
                    if loaded_layer["l"] != l:
                        loaded_layer["l"] = l
                        for k in range(8):
                            for (a, b) in ((0, 1368), (1368, INC)):
                                dma(w_in_b[:, k, a:b], w_in_d[l, k * 128:(k + 1) * 128, a:b], [], [r_win], eng="pool")
                            dma(w_out_b[:, k, :], w_out_d[l, k * 128:(k + 1) * 128, :], [], [r_wout], eng="pool")
                        dma(w_g2_b[0:16, :], w_g2_d[l, :, :], [], [r_wg2], eng="pool")
                        for k in range(2):
                            dma(stage[:, 0:768], w_uq_d[l, k * 128:(k + 1) * 128, :], [], [r_stage])
                            ts(w_uq_b[:, k, :], stage[:, 0:768], cst[:, o_qn + l * 2 + k:o_qn + l * 2 + k + 1], None,
                               ALU.mult, None, [r_stage, r_cst], [r_wuq])
                        dma(stage[:, 0:1024], w_ukv_d[l, :, :], [], [r_stage])
                        ts(w_ukv_b[:], stage[:, 0:1024], cst[:, o_kv + l:o_kv + l + 1], None, ALU.mult, None,
                           [r_stage, r_cst], [r_wukv])
                    memset(S32[:], 0.0, [r_S32])
                    for sl in range(4):
                        hh = sl % 2
                        memset(S_b[64 * hh:64 * hh + 64, sl, :], 0.0, [r_Sb])
                    memset(Sg32[:], 0.0, [r_Sg32])
                    memset(Sg_b[:], 0.0, [r_Sgb])

                    stage_(3)
                    bank_R = (PP[0][:, 0:512], r_bank[0][0])
                    bank_G = (PP[0][:, 512:1024], r_bank[0][1])
                    bank_M = [(PP[2][:, 0:512], r_bank[2][0]), (PP[2][:, 512:1024], r_bank[2][1])]
                    mstate = {"i": 0}

                    def nextM():
                        return bank_M[0]

                    def inproj_to(bk_, rbk_, c0, c1, t):
                        hT = hTs[t % 2]
                        for k in range(8):
                            mm(bk_[:, 0:c1 - c0], hT[:, k, :], w_in_b[:, k, c0:c1], k == 0, r_hTs[t % 2] + [r_win], [rbk_])

                    def norm_gate(bo, rbo, stx, rstx, gAx, rgAx, gBx, rgBx, out_ap, rout):
                        act(eg[:, 0:256], bo[:, 0:256], AF.Square, [rbo], [r_eg])
                        S.add("dve", lambda e: e.tensor_reduce(out=stx[:, 4:8], in_=eg[:, 0:256].rearrange("p (h d) -> p h d", d=64),
                                                                axis=AX.X, op=ALU.add), [r_eg], [rstx])
                        rstd_from(stx[:, 4:8], stx[:, 8:12], 64, [rstx], [rstx], stx[:, 12:16])
                        tt(gAx[:, 0:256].rearrange("p (h d) -> p h d", d=64), bo[:, 0:256].rearrange("p (h d) -> p h d", d=64),
                           stx[:, 8:12].unsqueeze(2).to_broadcast([128, 4, 64]), ALU.mult, [rbo, rstx], [rgAx])
                        tt(out_ap, gAx[:, 0:256], gBx[:, 0:256], ALU.mult, [rgAx, rgBx], [rout], eng="pool")

                    def hT_stage(t, X, rX):
                        hT = hTs[t % 2]
                        rh = r_hTs[t % 2]
                        src = x_d if l == 0 else y_d
                        dma(X[:], src[s, t * 128:(t + 1) * 128, :], [r_y[s][t]] if l > 0 else [], [rX])
                        act(xn[:], X[:], AF.Square, [rX], [r_xn, r_st], accum=st_[:, 0:1])
                        rstd_from(st_[:, 0:1], st_[:, 2:3], D, [r_st], [r_st], st_[:, 1:2])
                        S.add("pool", lambda e: e.tensor_scalar(out=xn[:], in0=X[:], scalar1=st_[:, 2:3], scalar2=1.0,
                                                                 op0=ALU.mult, op1=ALU.mult), [rX, r_st], [r_xn])
                        yield
                        bA, rbA = bank_R
                        bAb = bA.bitcast(BF16)
                        for k in range(8):
                            tr(bAb[:, k * 128:(k + 1) * 128], xn[:, k * 128:(k + 1) * 128], [r_xn], [rbA])
                        for k in range(4):
                            ts(hT[:, k, :], bAb[:, k * 128:(k + 1) * 128], gp[:, k:k + 1], modc[:, k:k + 1],
                               ALU.mult, ALU.add, [rbA, r_gp, r_modc], [rh[0]])
                        for k in range(4, 8):
                            act(hT[:, k, :], bAb[:, k * 128:(k + 1) * 128], AF.Identity, [rbA, r_gp, r_modc], [rh[1]],
                                scale=gp[:, k:k + 1], bias=modc[:, k:k + 1])

                    def ret_stream(t):
                        bR, rbR = bank_R
                        mx, rmx = mixed2[t % 2], r_mixed2[t % 2]
                        inproj_to(bR, rbR, 0, 512, t)
                        cp(qkf[:], bR, [rbR], [r_qkf], eng="act")
                        yield
                        q3 = qkf[:].rearrange("p (h d) -> p h d", d=64)
                        x1 = q3[:, :, 0:32]
                        x2 = q3[:, :, 32:64]
                        cosb = Tcos[:, t, 0:32].unsqueeze(1).to_broadcast([128, 8, 32])
                        sinb = Tsin[:, t, 0:32].unsqueeze(1).to_broadcast([128, 8, 32])
                        tA3 = tA[:].rearrange("p (h d) -> p h d", d=32)
                        tB3 = tB[:].rearrange("p (h d) -> p h d", d=32)
                        tt(tA3, x1, cosb, ALU.mult, [r_qkf, r_Tcos], [r_tA], eng="pool")
                        tt(tB3, x2, sinb, ALU.mult, [r_qkf, r_Tsin], [r_tB], eng="pool")
                        tt(qkrot[:, :, 0:32], tA3, tB3, ALU.subtract, [r_tA, r_tB], [r_qkrot], eng="pool")
                        tt(tA3, x2, cosb, ALU.mult, [r_qkf, r_Tcos], [r_tA], eng="pool")
                        tt(tB3, x1, sinb, ALU.mult, [r_qkf, r_Tsin], [r_tB], eng="pool")
                        tt(qkrot[:, :, 32:64], tA3, tB3, ALU.add, [r_tA, r_tB], [r_qkrot], eng="pool")
                        tt(kdec[:], qkrot[:, 4:8, :], C("wk").unsqueeze(2).to_broadcast([128, 4, 64]), ALU.mult,
                           [r_qkrot, r_cst], [r_kdec], eng="pool")
                        yield
                        qkflat = qkrot[:].rearrange("p h d -> p (h d)")
                        bRb = bR.bitcast(BF16)
                        for j in range(4):
                            tr(bRb[:, j * 128:(j + 1) * 128], qkflat[:, j * 128:(j + 1) * 128], [r_qkrot], [rbR])
                        for pair in range(2):
                            for hh in range(2):
                                cp(qT_m[64 * hh:64 * hh + 64, 2 * pair + hh, :],
                                   bRb[64 * hh:64 * hh + 64, pair * 128:(pair + 1) * 128], [rbR], [r_qTm], eng="act")
                        cp(kTr[:], bRb[:, 256:512].rearrange("p (a b) -> p a b", b=128), [rbR], [r_kTr])
                        wq3 = C("wqT").rearrange("p (a b) -> p a b", b=128)
                        for pair in range(2):
                            for hh in range(2):
                                tt(qTd_m[64 * hh:64 * hh + 64, 2 * pair + hh, :], qT_m[64 * hh:64 * hh + 64, 2 * pair + hh, :],
                                   wq3[64 * hh:64 * hh + 64, pair, :], ALU.mult, [r_qTm, r_cst], [r_qTdm], eng="pool")
                        yield
                        for hd in range(4):
                            mm(bR[:, hd * 128:(hd + 1) * 128], kTr[:, hd // 2, :], qT_m[:, hd, :], hd == 0,
                               [r_kTr, r_qTm], [rbR])
                        tt(STm[:].rearrange("p a b -> p (a b)"), bR, C("maskR"), ALU.mult, [rbR, r_cst], [r_STm])
                        yield
                        inproj_to(bR, rbR, 512, 1024, t)
                        cp(vb_r[:], bR[:, 0:256], [rbR], [r_vbr], eng="act")
                        sigmoid_from(bR[:, 256:512], [rbR], 256)
                        tt(gB[:, 0:256], bR[:, 256:512], sg[:, 0:256], ALU.mult, [rbR, r_sg], [r_gB])
                        yield
                        for hd in range(4):
                            mm(bR[:, hd * 64:(hd + 1) * 64], STm[:, hd, :], vb_r[:, hd * 64:(hd + 1) * 64], hd == 0,
                               [r_STm, r_vbr], [rbR])
                            mm(bR[:, hd * 64:(hd + 1) * 64], qTd_m[:, hd, :], S_b[:, hd, :], False, [r_qTdm, r_Sb], [rbR])
                        norm_gate(bR, rbR, st_r, r_str, gA, r_gA, gB, r_gB, mx[:, 0:256], rmx)
                        yield
                        kd2 = kdec[:].rearrange("p h d -> p (h d)")
                        for pair in range(2):
                            mm(bR[:, pair * 128:(pair + 1) * 128], kd2[:, pair * 128:(pair + 1) * 128],
                               vb_r[:, pair * 128:(pair + 1) * 128], pair == 0, [r_kdec, r_vbr], [rbR])
                        for pair in range(2):
                            for hh in range(2):
                                p0 = 64 * hh
                                stt(S32[p0:p0 + 64, pair, :], S32[p0:p0 + 64, pair, :],
                                    cst[p0:p0 + 64, _off["g128"][0] + pair:_off["g128"][0] + pair + 1],
                                    bR[p0:p0 + 64, pair * 128 + hh * 64: pair * 128 + hh * 64 + 64],
                                    ALU.mult, ALU.add, [r_S32, r_cst, rbR], [r_S32])
                                cp(S_b[p0:p0 + 64, 2 * pair + hh, :], S32[p0:p0 + 64, pair, :], [r_S32], [r_Sb], eng="pool")

                    def gla_gate(t):
                        bG, rbG = bank_G
                        bGb = bG.bitcast(BF16)
                        e3, r_e3 = e3s[t % 2], r_e3s[t % 2]
                        edec, r_edec = edecs[t % 2], r_edecs[t % 2]
                        gB_g, r_gBg = gB_gs[t % 2], r_gBgs[t % 2]
                        inproj_to(bG, rbG, 2464, 2736, t)
                        cp(glow_b[:], bG[:, 0:32], [rbG], [r_glowb], eng="act")
                        sigmoid_from(bG[:, 16:272], [rbG], 256)
                        tt(gB_g[:, 0:256], bG[:, 16:272], sg[:, 0:256], ALU.mult, [rbG, r_sg], [r_gBg])
                        tt(gB_g[:, 0:256].rearrange("p (h d) -> p h d", d=64), gB_g[:, 0:256].rearrange("p (h d) -> p h d", d=64),
                           cst[:, o_gn + l * 64:o_gn + (l + 1) * 64].unsqueeze(1).to_broadcast([128, 4, 64]), ALU.mult,
                           [r_gBg, r_cst], [r_gBg], eng="pool")
                        yield
                        tr(bGb[0:32, 0:128], glow_b[:], [r_glowb], [rbG])
                        cp(glowT[:], bGb[0:32, 0:128], [rbG], [r_glowT])
                        yield
                        mm(bG[:, 0:128], glowT[:], w_g2_b[:], True, [r_glowT, r_wg2], [rbG])
                        tt(lf[:], bG[:, 0:128], cst[:, o_bg + l * 128:o_bg + (l + 1) * 128], ALU.add, [rbG, r_cst], [r_lf])
                        act(lf[:], lf[:], AF.Exp, [r_lf], [r_lf], scale=-1.0)
                        act(lf[:], lf[:], AF.Ln, [r_lf], [r_lf], bias=1.0)
                        yield
                        mm(bG[:, 0:128], C("tri"), lf[:], True, [r_cst, r_lf], [rbG])
                        mm(bG[:, 128:256], C("trimo"), lf[:], False, [r_cst, r_lf], [rbG])
                        mm(bG[:, 256:258], lf[:], C("ones")[:, 0:2], False, [r_cst, r_lf], [rbG])
                        act(e3[:, 0, :], bG[:, 0:128], AF.Exp, [rbG], [r_e3], scale=-1.0 / 16)
                        act(e3[:, 1, :], bG[:, 0:128], AF.Exp, [rbG], [r_e3], scale=1.0 / 16)
                        act(e3[:, 2, :], bG[:, 128:256], AF.Exp, [rbG], [r_e3], scale=1.0 / 16,
                            bias=float(math.log(32 ** -0.5)))
                        act(edec[:], bG[:, 256:257], AF.Exp, [rbG], [r_edec], scale=-1.0 / 16)

                    def gla_stream(t):
                        bG, rbG = bank_G
                        mx, rmx = mixed2[t % 2], r_mixed2[t % 2]
                        bGb = bG.bitcast(BF16)
                        hm0 = _off["hm"][0]
                        e3, r_e3 = e3s[t % 2], r_e3s[t % 2]
                        edec, r_edec = edecs[t % 2], r_edecs[t % 2]
                        gB_g, r_gBg = gB_gs[t % 2], r_gBgs[t % 2]
                        inproj_to(bG, rbG, 1952, 2464, t)
                        cp(vb_g[:], bG[:, 256:512], [rbG], [r_vbg], eng="act")
                        tt(qk4[:, 0, :], bG[:, 0:128], e3[:, 0, :], ALU.mult, [rbG, r_e3], [r_qk4])
                        tt(qk4[:, 1, :], bG[:, 0:128], e3[:, 1, :], ALU.mult, [rbG, r_e3], [r_qk4])
                        tt(qk4[:, 2, :], bG[:, 128:256], e3[:, 1, :], ALU.mult, [rbG, r_e3], [r_qk4])
                        tt(qk4[:, 3, :], bG[:, 128:256], e3[:, 0, :], ALU.mult, [rbG, r_e3], [r_qk4])
                        tt(kdg[:], bG[:, 128:256], e3[:, 2, :], ALU.mult, [rbG, r_e3], [r_kdg])
                        yield
                        for j in range(4):
                            tr(bGb[:, j * 128:(j + 1) * 128], qk4[:, j, :], [r_qk4], [rbG])
                        for hd in range(4):
                            ts(qpT_m[:, hd, :], bGb[:, 0:128], cst[:, hm0 + hd:hm0 + hd + 1], None, ALU.mult, None,
                               [rbG, r_cst], [r_qpTm])
                            ts(qnT_m[:, hd, :], bGb[:, 128:256], cst[:, hm0 + hd:hm0 + hd + 1], None, ALU.mult, None,
                               [rbG, r_cst], [r_qnTm])
                        cp(kTg[:], bGb[:, 256:512].rearrange("p (a b) -> p a b", b=128), [rbG], [r_kTg])
                        yield
                        m1b = C("M1s").unsqueeze(1).to_broadcast([128, 4, 128])
                        m2b = C("M2s").unsqueeze(1).to_broadcast([128, 4, 128])
                        for hd in range(4):
                            mm(bG[:, hd * 128:(hd + 1) * 128], kTg[:, 0, :], qpT_m[:, hd, :], hd == 0, [r_kTg, r_qpTm], [rbG])
                        tt(A1[:], bG.rearrange("p (a b) -> p a b", b=128), m1b, ALU.mult, [rbG, r_cst], [r_A1])
                        yield
                        for hd in range(4):
                            mm(bG[:, hd * 128:(hd + 1) * 128], kTg[:, 1, :], qnT_m[:, hd, :], hd == 0, [r_kTg, r_qnTm], [rbG])
                        tt(A2[:], bG.rearrange("p (a b) -> p a b", b=128), m2b, ALU.mult, [rbG, r_cst], [r_A2])
                        yield
                        for hd in range(4):
                            o_ = bG[:, hd * 64:(hd + 1) * 64]
                            v_ = vb_g[:, hd * 64:(hd + 1) * 64]
                            mm(o_, A1[:, hd, :], v_, hd == 0, [r_A1, r_vbg], [rbG])
                            mm(o_, A2[:, hd, :], v_, False, [r_A2, r_vbg], [rbG])
                            mm(o_, qpT_m[:, hd, :], Sg_b[:], False, [r_qpTm, r_Sgb], [rbG])
                        norm_gate(bG, rbG, st_g, r_stg, gA_g, r_gAg, gB_g, r_gBg, mx[:, 768:1024], rmx)
                        yield
                        mm(bG[:, 0:256], kdg[:], vb_g[:], True, [r_kdg, r_vbg], [rbG])
                        ts(Sg32[:], Sg32[:], edec[:, 0:1], None, ALU.mult, None, [r_Sg32, r_edec], [r_Sg32])
                        for hd in range(4):
                            stt(Sg32[:], bG[:, hd * 64:(hd + 1) * 64], cst[:, hm0 + hd:hm0 + hd + 1], Sg32[:],
                                ALU.mult, ALU.add, [r_Sg32, r_cst, rbG], [r_Sg32])
                        cp(Sg_b[:], Sg32[:], [r_Sg32], [r_Sgb], eng="pool")

                    def mla_stream(t):
                        par = t % 2
                        g3, rg3 = nextM()
                        inproj_to(g3, rg3, 1440, 1952, t)
                        for hz in range(2):
                            sigmoid_from(g3[:, hz * 256:(hz + 1) * 256], [rg3], 256)
                            tt(gBm[par][:, hz * 256:(hz + 1) * 256], g3[:, hz * 256:(hz + 1) * 256], sg[:, 0:256], ALU.mult,
                               [rg3, r_sg], [r_gBm[par]])
                        yield
                        g2, rg2 = nextM()
                        inproj_to(g2, rg2, 1024, 1440, t)
                        act(latn[:, 0:256], g2[:, 0:256], AF.Square, [rg2], [r_latn, r_stm], accum=st_m[:, 4:5])
                        act(latn[:, 256:384], g2[:, 256:384], AF.Square, [rg2], [r_latn, r_stm], accum=st_m[:, 5:6])
                        cp(krf[:], g2[:, 384:416], [rg2], [r_krf], eng="act")
                        rstd_from(st_m[:, 4:5], st_m[:, 8:9], 256, [r_stm], [r_stm], st_m[:, 12:13])
                        rstd_from(st_m[:, 5:6], st_m[:, 9:10], 128, [r_stm], [r_stm], st_m[:, 13:14])
                        ts(latn[:, 0:256], g2[:, 0:256], st_m[:, 8:9], None, ALU.mult, None, [rg2, r_stm], [r_latn])
                        ts(latn[:, 256:384], g2[:, 256:384], st_m[:, 9:10], None, ALU.mult, None, [rg2, r_stm], [r_latn])
                        cm = Tcos[:, t, 32:48]
                        sm = Tsin[:, t, 32:48]
                        tt(tA[:, 0:16], krf[:, 0:16], cm, ALU.mult, [r_krf, r_Tcos], [r_tA], eng="pool")
                        tt(tB[:, 0:16], krf[:, 16:32], sm, ALU.mult, [r_krf, r_Tsin], [r_tB], eng="pool")
                        tt(kpe[:, 0:16], tA[:, 0:16], tB[:, 0:16], ALU.subtract, [r_tA, r_tB], [r_kpe], eng="pool")
                        tt(tA[:, 0:16], krf[:, 16:32], cm, ALU.mult, [r_krf, r_Tcos], [r_tA], eng="pool")
                        tt(tB[:, 0:16], krf[:, 0:16], sm, ALU.mult, [r_krf, r_Tsin], [r_tB], eng="pool")
                        tt(kpe[:, 16:32], tA[:, 0:16], tB[:, 0:16], ALU.add, [r_tA, r_tB], [r_kpe], eng="pool")
                        cp(Kfull[:, :, 64:96], kpe[:].unsqueeze(1).to_broadcast([128, 8, 32]), [r_kpe], [r_Kfull], eng="pool")
                        yield
                        bk, rbk = nextM()
                        bkb = bk.bitcast(BF16)
                        for j in range(3):
                            tr(bkb[:, j * 128:(j + 1) * 128], latn[:, j * 128:(j + 1) * 128], [r_latn], [rbk])
                        cp(latT[:], bkb[:, 0:384].rearrange("p (a b) -> p a b", b=128), [rbk], [r_latT])
                        yield
                        for hf in range(2):
                            bq, rbq = nextM()
                            for k in range(2):
                                mm(bq[:, 0:384], latT[:, k, :], w_uq_b[:, k, hf * 384:(hf + 1) * 384], k == 0,
                                   [r_latT, r_wuq], [rbq])
                            q3 = bq[:, 0:384].rearrange("p (h d) -> p h d", d=96)
                            Qh = Qb[:, hf * 4:(hf + 1) * 4, :]
                            cp(Qh[:, :, 0:64], q3[:, :, 0:64], [rbq], [r_Qb], eng="act")
                            cm4 = cm.unsqueeze(1).to_broadcast([128, 4, 16])
                            sm4 = sm.unsqueeze(1).to_broadcast([128, 4, 16])
                            tA4 = tA[:, 0:64].rearrange("p (h d) -> p h d", d=16)
                            tB4 = tB[:, 0:64].rearrange("p (h d) -> p h d", d=16)
                            tt(tA4, q3[:, :, 64:80], cm4, ALU.mult, [rbq, r_Tcos], [r_tA])
                            tt(tB4, q3[:, :, 80:96], sm4, ALU.mult, [rbq, r_Tsin], [r_tB])
                            tt(Qh[:, :, 64:80], tA4, tB4, ALU.subtract, [r_tA, r_tB], [r_Qb])
                            tt(tA4, q3[:, :, 80:96], cm4, ALU.mult, [rbq, r_Tcos], [r_tA])
                            tt(tB4, q3[:, :, 64:80], sm4, ALU.mult, [rbq, r_Tsin], [r_tB])
                            tt(Qh[:, :, 80:96], tA4, tB4, ALU.add, [r_tA, r_tB], [r_Qb])
                        yield
                        bk, rbk = nextM()
                        bkb = bk.bitcast(BF16)
                        for hd in range(8):
                            tr(bkb[0:96, hd * 128:(hd + 1) * 128], Qb[:, hd, :], [r_Qb], [rbk])
                        cp(QT2[par][:], bkb[0:96, :].rearrange("p (a b) -> p a b", b=128), [rbk], [r_QT2[par]])
                        yield
                        for hf in range(2):
                            bq, rbq = nextM()
                            mm(bq, latT[:, 2, :], w_ukv_b[:, hf * 512:(hf + 1) * 512], True, [r_latT, r_wukv], [rbq])
                            kv3 = bq.rearrange("p (h d) -> p h d", d=128)
                            cp(Kfull[:, hf * 4:(hf + 1) * 4, 0:64], kv3[:, :, 0:64], [rbq], [r_Kfull], eng="act")
                            cp(VA[:, t, hf * 4:(hf + 1) * 4, 0:64], kv3[:, :, 64:128], [rbq], [r_VA[t]], eng="act")
                        yield
                        bk, rbk = nextM()
                        bkb = bk.bitcast(BF16)
                        for hd in range(8):
                            tr(bkb[0:96, hd * 128:(hd + 1) * 128], Kfull[:, hd, :], [r_Kfull], [rbk])
                        cp(KT[:, :, t * 128:(t + 1) * 128], bkb[0:96, :].rearrange("p (a b) -> p a b", b=128), [rbk], [r_KT[t]],
                           eng="act")
                        if ada_todo:
                            yield
                            l_, g_ = ada_todo.pop(0)
                            bk, rbk = nextM()
                            ada_group(l_, g_, bk, rbk)

                    def phaseB(t, X, rX):
                        par = t % 2
                        QTt, rQTt = QT2[par], r_QT2[par]
                        mx, rmx = mixed2[par], r_mixed2[par]
                        sc_scale = float(96 ** -0.5)
                        acc = PP[1]
                        racc = r_bank[1]
                        scp = PP[3]
                        rsc = r_bank[3]
                        rot = [(scp[:, 0:512], rsc[0]), (scp[:, 512:1024], rsc[1]), bank_M[1]]

                        def qk_exp(j):
                            kt, hf = j // 2, j % 2
                            sb_, rsb_ = rot[j % 3]
                            P_ = PT[kt % 2]
                            rP = r_PT[kt % 2]
                            for i4 in range(4):
                                hd = 4 * hf + i4
                                mm(sb_[:, i4 * 128:(i4 + 1) * 128], KT[:, hd, kt * 128:(kt + 1) * 128], QTt[:, hd, :],
                                   i4 == 0, [r_KT[kt], rQTt], [rsb_])
                            act(P_[:, 4 * hf:4 * hf + 4, :].rearrange("p a b -> p (a b)"), sb_, AF.Exp, [rsb_], [rP[hf]],
                                scale=sc_scale)
                            if kt == t:
                                memset(P_[64:128, 4 * hf:4 * hf + 4, 0:64], 0.0, [rP[hf]], eng="pool")

                        def pv(j):
                            kt, hf = j // 2, j % 2
                            P_ = PT[kt % 2]
                            rP = r_PT[kt % 2]
                            for i4 in range(4):
                                hd = 4 * hf + i4
                                o_ = acc[:, hf * 512 + i4 * 65: hf * 512 + i4 * 65 + 65]
                                mm(o_, P_[:, hd, :], VA[:, kt, hd, :], (kt == 0 and i4 == 0), [rP[hf], r_VA[kt]], [racc[hf]])

                        nj = 2 * (t + 1)
                        qk_exp(0)
                        qk_exp(1)
                        for j in range(nj):
                            if j + 2 < nj:
                                qk_exp(j + 2)
                            pv(j)
                            if j % 2 == 1:
                                yield
                        for hf in range(2):
                            a3 = acc[:, hf * 512: hf * 512 + 260].rearrange("p (h d) -> p h d", d=65)
                            S.add("dve", lambda e, a3=a3, hf=hf: e.reciprocal(out=rs[:, hf * 4:(hf + 1) * 4].unsqueeze(2),
                                                                             in_=a3[:, :, 64:65]), [racc[hf]], [r_rs])
                            tt(gAm[:, hf * 256:(hf + 1) * 256].rearrange("p (h d) -> p h d", d=64), a3[:, :, 0:64],
                               rs[:, hf * 4:(hf + 1) * 4].unsqueeze(2).to_broadcast([128, 4, 64]), ALU.mult,
                               [racc[hf], r_rs], [r_gAm])
                        tt(mx[:, 256:768], gAm[:], gBm[par][:], ALU.mult, [r_gAm, r_gBm[par]], [rmx], eng="pool")
                        yield
                        stage_(8)
                        bk = scp[:, 0:512]
                        rbk = rsc[0]
                        bkb = bk.bitcast(BF16)
                        for k in range(8):
                            tr(bkb[:, k * 128:(k + 1) * 128], mx[:, k * 128:(k + 1) * 128], [rmx], [rbk])
                        cp(mixT[:], bkb.rearrange("p (a b) -> p a b", b=128), [rbk], [r_mixT], eng="act")
                        yield
                        for hf in range(2):
                            bo = scp[:, (1 - hf) * 512:(2 - hf) * 512]
                            rbo = rsc[1 - hf]
                            for k in range(8):
                                mm(bo, mixT[:, k, :], w_out_b[:, k, hf * 512:(hf + 1) * 512], k == 0, [r_mixT, r_wout], [rbo])
                            tt(gAm[:], bo, gate_bc[:, hf * 512:(hf + 1) * 512], ALU.mult, [rbo, r_gate], [r_gAm])
                            tt(X[:, hf * 512:(hf + 1) * 512], X[:, hf * 512:(hf + 1) * 512], gAm[:], ALU.add,
                               [rX, r_gAm], [rX], eng="pool")
                            yield
                        if last_layer:
                            act(mixT[:].rearrange("p a b -> p (a b)"), X[:], AF.Square, [rX], [r_mixT, r_stb], accum=st_b[:, 0:1])
                            rstd_from(st_b[:, 0:1], st_b[:, 2:3], D, [r_stb], [r_stb], st_b[:, 1:2])
                            stt(X[:], X[:], st_b[:, 2:3], fnorm[:], ALU.mult, ALU.mult, [rX, r_stb, r_fnorm], [rX])
                        dma(y_d[s, t * 128:(t + 1) * 128, :], X[:], [rX], [r_y[s][t]])

                    def drain(g):
                        for _ in g:
                            pass

                    def interleave(ga, gb, na, nb):
                        ia = ib = 0
                        da = db = False
                        while not (da and db):
                            if db or (not da and ia * nb <= ib * na):
                                try:
                                    next(ga)
                                    ia += 1
                                except StopIteration:
                                    da = True
                            else:
                                try:
                                    next(gb)
                                    ib += 1
                                except StopIteration:
                                    db = True

                    def round_robin(gens, weights):
                        active = list(range(len(gens)))
                        while active:
                            for i in list(active):
                                for _ in range(weights[i]):
                                    try:
                                        next(gens[i])
                                    except StopIteration:
                                        active.remove(i)
                                        break

                    Xs = {}

                    def alloc_x(t):
                        xi = xslot["i"] % 3
                        xslot["i"] += 1
                        Xs[t] = (xt[xi], r_xt[xi])
                        return Xs[t]

                    def chain2(g1, g2):
                        for _ in g1:
                            yield
                        yield
                        for _ in g2:
                            yield

                    prevB = None
                    drain(hT_stage(0, *alloc_x(0)))
                    drain(gla_gate(0))
                    for t in range(NT):
                        gens = [gla_stream(t), mla_stream(t), ret_stream(t)]
                        weights = [1, 1, 1]
                        if not PIPE:
                            if prevB is not None:
                                drain(prevB)
                            for g_ in gens:
                                drain(g_)
                            if t + 1 < NT:
                                drain(hT_stage(t + 1, *alloc_x(t + 1)))
                                drain(gla_gate(t + 1))
                        else:
                            if t + 1 < NT:
                                gens.append(chain2(hT_stage(t + 1, *alloc_x(t + 1)), gla_gate(t + 1)))
                                weights.append(1)
                            if prevB is not None:
                                gens.append(prevB)
                                weights.append(max(1, -(-(t + 4) // 10)))
                            round_robin(gens, weights)
                        prevB = phaseB(t, *Xs[t])
                    drain(prevB)

        except StopBuild:
            pass
        S.add("sp", None, [r_y[s][t] for s in range(NSEQ) for t in range(NT)], [])

        if os.environ.get("KDBG"):
            print("SBUF remaining", nc.sbuf_bytes_remaining, "ops", {e: len(v) for e, v in S.ops.items()})
        S.finalize(sems, dsems)
        with nc.Block() as block:
            @block.sync
            def _(e):
                S.replay("sp", e)

            @block.gpsimd
            def _(e):
                S.replay("pool", e)

            @block.vector
            def _(e):
                S.replay("dve", e)

            @block.scalar
            def _(e):
                S.replay("act", e)

            @block.tensor
            def _(e):
                S.replay("pe", e)
    return nc


def run(inputs, NSEQ, NT, L, ncores=NCORES, trace=False):
    f = np.float32
    x = np.asarray(inputs["x"], f)
    c = np.asarray(inputs["c"], f)
    pos = np.asarray(inputs["positions"], np.int32)
    consts = _const_pack(L, np.asarray(inputs["norm_w"], f), np.asarray(inputs["mla_q_norm"], f),
                         np.asarray(inputs["mla_kv_norm"], f), np.asarray(inputs["gla_norm"], f),
                         np.asarray(inputs["gla_b_g2"], f))
    fnorm = np.ascontiguousarray(np.broadcast_to(np.asarray(inputs["final_norm"], f)[None, :], (128, D)))
    ada_b = np.asarray(inputs["ada_b"], f).reshape(1, L * 3 * D)
    ada_b_rep = np.ascontiguousarray(np.broadcast_to(ada_b, (NSEQ, L * 3 * D)))
    shared = {
        "consts": consts, "fnorm": fnorm, "ada_w": np.ascontiguousarray(inputs["ada_w"], f), "ada_b": ada_b_rep,
        "w_in": np.ascontiguousarray(inputs["w_in"], f), "w_uq": np.ascontiguousarray(inputs["w_uq"], f),
        "w_ukv": np.ascontiguousarray(inputs["w_ukv"], f), "w_g2": np.ascontiguousarray(inputs["gla_w_g2"], f),
        "b_g2": np.ascontiguousarray(inputs["gla_b_g2"], f), "w_out": np.ascontiguousarray(inputs["w_out"], f),
    }
    in_maps = []
    for ci in range(ncores):
        sl = slice(ci * NSEQ, (ci + 1) * NSEQ)
        xc = np.ascontiguousarray(x[sl])
        cc = c[sl]
        cT = np.ascontiguousarray(cc.reshape(NSEQ, 8, 128).transpose(2, 1, 0).reshape(128, 8 * NSEQ))
        pc = pos[sl]
        pT = np.ascontiguousarray(pc.reshape(NSEQ, NT, 128).transpose(2, 0, 1).reshape(128, NSEQ * NT))
        m = {"x": xc, "cT": cT, "pos": pT}
        m.update(shared)
        in_maps.append(m)
    nc = build(NSEQ, NT, L)
    res = run_bass_kernel_spmd(nc, in_maps, core_ids=list(range(ncores)), trace=trace)
    out = np.concatenate([r["y"] for r in res.results], axis=0)
    return out, res


def kernel(**inputs):
    out, _ = run(inputs, NSEQ=4, NT=16, L=2)
    return out.astype(np.float32)
```

```python
import math
from contextlib import ExitStack

import numpy as np
import concourse.bass as bass
import concourse.mybir as mybir
from concourse.bass_utils import run_bass_kernel_spmd

F32 = mybir.dt.float32
BF16 = mybir.dt.bfloat16
I32 = mybir.dt.int32
AF = mybir.ActivationFunctionType
ALU = mybir.AluOpType
AX = mybir.AxisListType

D = 1024
INC = 2736
EPS = 1e-6
NCORES = 8
PI = math.pi
TWO_PI = 2.0 * math.pi
C1 = 6.28125
C2 = TWO_PI - 6.28125
PI_LO = 3.1415925

_off = {}
_cur = 0


def _reg(name, n):
    global _cur
    _off[name] = (_cur, _cur + n)
    _cur += n


_reg("ident", 128)
_reg("maskR", 512)
_reg("tri", 128)
_reg("ones", 128)
_reg("trimo", 128)
_reg("M1s", 128)
_reg("M2s", 128)
_reg("wqT", 256)
_reg("wk", 4)
_reg("g128", 2)
_reg("inv48", 48)
_reg("hm", 4)
NCONST_BASE = _cur


def _const_pack(L, norm_w, mla_q_norm, mla_kv_norm, gla_norm, b_g2):
    f = np.float32
    h = np.arange(4, dtype=f)
    log_gamma = np.log1p(-np.exp2(-5.0 - h)).astype(f)
    idx = np.arange(128, dtype=f)
    cj = (np.arange(128) // 64)
    maskR = np.zeros((128, 4, 128), f)
    for hh in range(4):
        dist = np.abs(idx[None, :] - idx[:, None])
        dec = np.exp(log_gamma[hh] * dist).astype(f)
        same = cj[:, None] == cj[None, :]
        past = cj[:, None] < cj[None, :]
        maskR[:, hh, :] = np.where(same | past, dec, 0.0) * f(0.125)
    tri = (np.arange(128)[:, None] <= np.arange(128)[None, :]).astype(f)
    ones = np.ones((128, 128), f)
    trimo = tri - ones
    sc = f(32 ** -0.5)
    M1s = tri * sc
    M2s = ((np.arange(128)[:, None] > np.arange(128)[None, :]) & (cj[:, None] == cj[None, :])).astype(f) * sc
    wqT = np.zeros((128, 2, 128), f)
    g128 = np.zeros((128, 2), f)
    for pair in range(2):
        for hh in range(2):
            hd = 2 * pair + hh
            wqT[64 * hh:64 * hh + 64, pair, :] = np.exp(log_gamma[hd] * (idx + 1.0))[None, :]
            g128[64 * hh:64 * hh + 64, pair] = np.exp(log_gamma[hd] * 128.0)
    wk = np.exp(log_gamma[None, :] * (127.0 - idx)[:, None]).astype(f) * f(0.125)
    inv_r = (10000.0 ** (-np.arange(32, dtype=f) / 32)).astype(f)
    inv_m = (10000.0 ** (-np.arange(16, dtype=f) / 16)).astype(f)
    inv48 = np.concatenate([inv_r, inv_m])[None, :].repeat(128, 0)
    parts = [np.eye(128, dtype=f), maskR.reshape(128, 512), tri, ones, trimo, M1s, M2s,
             wqT.reshape(128, 256), wk, g128, inv48,
             (np.arange(128)[:, None] // 32 == np.arange(4)[None, :]).astype(f)]
    nwT = norm_w.reshape(L, 8, 128).transpose(2, 0, 1).reshape(128, L * 8)
    qnT = mla_q_norm.reshape(L, 2, 128).transpose(2, 0, 1).reshape(128, L * 2)
    kvT = mla_kv_norm.reshape(L, 1, 128).transpose(2, 0, 1).reshape(128, L)
    gn = np.broadcast_to(gla_norm.reshape(1, L * 64), (128, L * 64))
    bg = np.broadcast_to(b_g2.reshape(1, L * 128), (128, L * 128))
    parts += [nwT, qnT, kvT, gn, bg]
    return np.ascontiguousarray(np.concatenate([p.astype(f) for p in parts], axis=1))


class Res:
    __slots__ = ("name", "w", "r", "rd", "excl")

    def __init__(self, name, excl=False):
        self.name = name
        self.excl = excl
        self.w = None
        self.r = {}
        self.rd = []


class Op:
    __slots__ = ("eng", "fn", "waits", "flag", "sem", "val", "dma")

    def __init__(self, eng, fn, dma):
        self.eng = eng
        self.fn = fn
        self.waits = []
        self.flag = False
        self.sem = None
        self.val = 0
        self.dma = dma


ENGS = ("pe", "act", "dve", "pool", "sp")


class Sched:
    def __init__(self):
        self.ops = {e: [] for e in ENGS}
        self.all = []

    def add(self, eng, fn, reads=(), writes=(), dma=False):
        op = Op(eng, fn, dma)
        deps = []
        for r in reads:
            if r.w is not None:
                deps.append((r.w, "raw"))
            if r.excl:
                for en, o in r.r.items():
                    if en != eng:
                        deps.append((o, "war"))
        for r in writes:
            if r.w is not None:
                deps.append((r.w, "waw"))
            for o in r.r.values():
                deps.append((o, "war"))
            for o in r.rd:
                deps.append((o, "war"))
        seen = set()
        for d, kind in deps:
            if d is op or id(d) in seen:
                continue
            if d.eng == eng and not d.dma and not dma:
                if eng == "pe":
                    continue
                if kind != "raw":
                    continue
            seen.add(id(d))
            op.waits.append(d)
            d.flag = True
        for r in reads:
            if dma:
                r.rd.append(op)
            else:
                r.r[eng] = op
        for r in writes:
            r.w = op
            r.r = {}
            r.rd = []
        self.ops[eng].append(op)
        self.all.append(op)
        return op

    def finalize(self, sems, dsems):
        cnt = {e: 0 for e in ENGS}
        pools = {"sp": dsems[:16], "pool": dsems[16:]}
        di = {"sp": 0, "pool": 0}
        last_on = {"sp": [None] * 16, "pool": [None] * (len(dsems) - 16)}
        for op in self.all:
            if op.dma:
                pl = pools[op.eng]
                nd = len(pl)
                j = di[op.eng] % nd
                op.sem = pl[j]
                op.val = 16 * (di[op.eng] // nd + 1)
                op.flag = True
                if last_on[op.eng][j] is not None:
                    op.waits.append(last_on[op.eng][j])
                last_on[op.eng][j] = op
                di[op.eng] += 1
            elif op.flag:
                cnt[op.eng] += 1
                op.sem = sems[op.eng]
                op.val = cnt[op.eng]
        return cnt

    def replay(self, eng, e):
        seen = {}
        for op in self.ops[eng]:
            for w in op.waits:
                key = id(w.sem)
                if seen.get(key, 0) >= w.val:
                    continue
                seen[key] = w.val
                e.wait_ge(w.sem, w.val)
            if op.fn is None:
                continue
            ins = op.fn(e)
            if op.flag:
                ins.then_inc(op.sem, 16 if op.dma else 1)


class StopBuild(Exception):
    pass


def build(NSEQ, NT, L):
    import os
    STOP = int(os.environ.get("KSTOP", "99"))
    PIPE = int(os.environ.get("KPIPE", "1"))

    def stage_(n):
        if STOP == n:
            raise StopBuild()
    S_LEN = NT * 128
    nc = bass.Bass("TRN2", target_bir_lowering=False)
    NCONST = NCONST_BASE + L * 8 + L * 2 + L + L * 64 + L * 128
    o_nw = NCONST_BASE
    o_qn = o_nw + L * 8
    o_kv = o_qn + L * 2
    o_gn = o_kv + L
    o_bg = o_gn + L * 64

    def din(name, shape, dt=F32):
        return nc.dram_tensor(name, list(shape), dt, kind="ExternalInput").ap()

    x_d = din("x", [NSEQ, S_LEN, D])
    cT_d = din("cT", [128, 8 * NSEQ])
    pos_d = din("pos", [128, NSEQ * NT], I32)
    consts_d = din("consts", [128, NCONST])
    fnorm_d = din("fnorm", [128, D])
    ada_w_d = din("ada_w", [L, D, 3 * D])
    ada_b_d = din("ada_b", [NSEQ, L * 3 * D])
    w_in_d = din("w_in", [L, D, INC])
    w_uq_d = din("w_uq", [L, 256, 768])
    w_ukv_d = din("w_ukv", [L, 128, 1024])
    w_g2_d = din("w_g2", [L, 16, 128])
    b_g2_d = din("b_g2", [L, 128])
    w_out_d = din("w_out", [L, D, D])
    y_d = nc.dram_tensor("y", [NSEQ, S_LEN, D], F32, kind="ExternalOutput").ap()
    mods_d = nc.dram_tensor("mods", [L, NSEQ, 3 * D], F32, kind="Internal").ap()

    S = Sched()
    with ExitStack() as es:
        def sb(name, shape, dt=F32):
            return es.enter_context(nc.sbuf_tensor("sb_" + name, list(shape), dt))

        def R(name):
            return Res(name)

        cst = sb("cst", [128, NCONST]); r_cst = R("cst")
        fnorm = sb("fnorm", [128, D]); r_fnorm = R("fnorm")
        ident_b = sb("ident_b", [128, 128], BF16); r_ident = R("ident")
        w_in_b = sb("w_in_b", [128, 8, INC], BF16); r_win = R("win")
        w_out_b = sb("w_out_b", [128, 8, D], BF16); r_wout = R("wout")
        w_uq_b = sb("w_uq_b", [128, 2, 768], BF16); r_wuq = R("wuq")
        w_ukv_b = sb("w_ukv_b", [128, 1024], BF16); r_wukv = R("wukv")
        w_g2_b = sb("w_g2_b", [32, 128], BF16); r_wg2 = R("wg2")
        KT = sb("KT", [96, 8, S_LEN], BF16); r_KT = [R("KT%d" % t) for t in range(NT)]
        VA = sb("VA", [128, NT, 8, 65], BF16); r_VA = [R("VA%d" % t) for t in range(NT)]
        stage = sb("stage", [128, 1024]); r_stage = R("stage")
        gate_bc = sb("gate_bc", [128, D], BF16); r_gate = R("gate")
        modc = sb("modc", [128, 16]); r_modc = R("modc")
        gp = sb("gp", [128, 8]); r_gp = R("gp")
        modg = sb("modg", [NSEQ, 128]); r_modg = R("modg")
        r_modsd = [[R("modsd%d_%d" % (l_, g_)) for g_ in range(24)] for l_ in range(L)]
        adab = sb("adab", [NSEQ, 128]); r_adab = R("adab")
        cact = sb("cact", [128, 8 * NSEQ]); r_cact = R("cact")
        ctmp = sb("ctmp", [128, 8 * NSEQ]); r_ctmp = R("ctmp")
        cin = sb("cin", [128, 8 * NSEQ]); r_cin = R("cin")
        pos_i = sb("pos_i", [128, NSEQ * NT], I32); r_posi = R("posi")
        CH = min(2, NT)
        pos_f = sb("pos_f", [128, NT]); r_posf = R("posf")
        ang = sb("ang", [128, CH, 48]); r_ang = R("ang")
        rr = sb("rr", [128, CH, 48]); r_rr = R("rr")
        kk_i = sb("kk_i", [128, CH, 48], I32); r_kki = R("kki")
        kk_f = sb("kk_f", [128, CH, 48]); r_kkf = R("kkf")
        mm_ = sb("mm_", [128, CH, 48]); r_mm = R("mm")
        rc = sb("rc", [128, CH, 48]); r_rc = R("rc")
        Tsin = sb("Tsin", [128, NT, 48]); r_Tsin = R("Tsin")
        Tcos = sb("Tcos", [128, NT, 48]); r_Tcos = R("Tcos")
        xt = [sb("xt%d" % i, [128, D]) for i in range(3)]; r_xt = [R("xt0"), R("xt1"), R("xt2")]
        st_ = sb("st_", [128, 16]); r_st = R("st")
        st_r = sb("st_r", [128, 16]); r_str = R("str")
        st_g = sb("st_g", [128, 16]); r_stg = R("stg")
        st_m = sb("st_m", [128, 16]); r_stm = R("stm")
        xn = sb("xn", [128, D], BF16); r_xn = R("xn")
        hTs = [sb("hT%d" % i, [128, 8, 128], BF16) for i in range(2)]
        r_hTs = [[R("hT%da" % i), R("hT%db" % i)] for i in range(2)]
        qkf = sb("qkf", [128, 512]); r_qkf = R("qkf")
        tA = sb("tA", [128, 256]); r_tA = R("tA")
        tB = sb("tB", [128, 256]); r_tB = R("tB")
        qkrot = sb("qkrot", [128, 8, 64], BF16); r_qkrot = R("qkrot")
        kdec = sb("kdec", [128, 4, 64], BF16); r_kdec = R("kdec")
        qT_m = sb("qT_m", [128, 4, 128], BF16); r_qTm = R("qTm")
        qTd_m = sb("qTd_m", [128, 4, 128], BF16); r_qTdm = R("qTdm")
        kTr = sb("kTr", [128, 2, 128], BF16); r_kTr = R("kTr")
        STm = sb("STm", [128, 4, 128], BF16); r_STm = R("STm")
        vb_r = sb("vb_r", [128, 256], BF16); r_vbr = R("vbr")
        S32 = sb("S32", [128, 2, 64]); r_S32 = R("S32")
        S_b = sb("S_b", [128, 4, 64], BF16); r_Sb = R("Sb")
        glow_b = sb("glow_b", [128, 32], BF16); r_glowb = R("glowb")
        glowT = sb("glowT", [32, 128], BF16); r_glowT = R("glowT")
        lf = sb("lf", [128, 128]); r_lf = R("lf")
        e3s = [sb("e3_%d" % i, [128, 3, 128]) for i in range(2)]; r_e3s = [R("e3_0"), R("e3_1")]
        qk4 = sb("qk4", [128, 4, 128], BF16); r_qk4 = R("qk4")
        kdg = sb("kdg", [128, 128], BF16); r_kdg = R("kdg")
        qpT_m = sb("qpT_m", [128, 4, 128], BF16); r_qpTm = R("qpTm")
        qnT_m = sb("qnT_m", [128, 4, 128], BF16); r_qnTm = R("qnTm")
        kTg = sb("kTg", [128, 2, 128], BF16); r_kTg = R("kTg")
        A1 = sb("A1", [128, 4, 128], BF16); r_A1 = R("A1")
        A2 = sb("A2", [128, 4, 128], BF16); r_A2 = R("A2")
        vb_g = sb("vb_g", [128, 256], BF16); r_vbg = R("vbg")
        Sg32 = sb("Sg32", [128, 64]); r_Sg32 = R("Sg32")
        Sg_b = sb("Sg_b", [128, 64], BF16); r_Sgb = R("Sgb")
        edecs = [sb("edec%d" % i, [128, 1]) for i in range(2)]; r_edecs = [R("edec0"), R("edec1")]
        latn = sb("latn", [128, 384], BF16); r_latn = R("latn")
        latT = sb("latT", [128, 3, 128], BF16); r_latT = R("latT")
        Qb = sb("Qb", [128, 8, 96], BF16); r_Qb = R("Qb")
        Kfull = sb("Kfull", [128, 8, 96], BF16); r_Kfull = R("Kfull")
        kpe = sb("kpe", [128, 32], BF16); r_kpe = R("kpe")
        krf = sb("krf", [128, 32]); r_krf = R("krf")
        QT2 = [sb("QT%d" % i, [96, 8, 128], BF16) for i in range(2)]; r_QT2 = [R("QT0"), R("QT1")]
        PT = [sb("PT%d" % i, [128, 8, 128], BF16) for i in range(2)]; r_PT = [[R("PT0a"), R("PT0b")], [R("PT1a"), R("PT1b")]]
        rs = sb("rs", [128, 8]); r_rs = R("rs")
        eg = sb("eg", [128, 256]); r_eg = R("eg")
        sg = sb("sg", [128, 256]); r_sg = R("sg")
        gA = sb("gA", [128, 256], BF16); r_gA = R("gA")
        gA_g = sb("gA_g", [128, 256], BF16); r_gAg = R("gAg")
        gB = sb("gB", [128, 256], BF16); r_gB = R("gB")
        gB_gs = [sb("gB_g%d" % i, [128, 256], BF16) for i in range(2)]; r_gBgs = [R("gBg0"), R("gBg1")]
        mixed2 = [sb("mixed%d" % i, [128, D], BF16) for i in range(2)]; r_mixed2 = [R("mixed0"), R("mixed1")]
        gBm = [sb("gBm%d" % i, [128, 512], BF16) for i in range(2)]; r_gBm = [R("gBm0"), R("gBm1")]
        gAm = sb("gAm", [128, 512]); r_gAm = R("gAm")
        st_b = sb("st_b", [128, 4]); r_stb = R("stb")
        mixT = sb("mixT", [128, 8, 128], BF16); r_mixT = R("mixT")

        PP = [es.enter_context(nc.psum_tensor("pp%d" % i, [128, 1024], F32)) for i in range(4)]
        r_bank = [[Res("b%d_%d" % (i, j), excl=True) for j in range(2)] for i in range(4)]
        gen_banks = [(0, 0), (0, 1), (2, 0), (2, 1)]
        gstate = {"i": 0, "sc": 0}

        def gbank():
            i, j = gen_banks[gstate["i"] % 4]
            gstate["i"] += 1
            return PP[i][:, j * 512:(j + 1) * 512], r_bank[i][j]

        sems = {e: es.enter_context(nc.semaphore("s_" + e)) for e in ENGS}
        dsems = [es.enter_context(nc.semaphore("d%d" % i)) for i in range(24)]

        def C(name):
            a, b = _off[name]
            return cst[:, a:b]

        def dma(out, in_, reads, writes, eng="sp", **kw):
            return S.add(eng, lambda e: e.dma_start(out=out, in_=in_, **kw), reads, writes, dma=True)

        def mm(out, lhsT, rhs, start, reads, writes):
            return S.add("pe", lambda e: e.matmul(out, lhsT=lhsT, rhs=rhs, start=start, stop=True,
                                                  skip_group_check=True), reads, writes)

        def tr(out, in_, reads, writes):
            return S.add("pe", lambda e: e.transpose(out, in_, ident_b[:]), list(reads) + [r_ident], writes)

        def act(out, in_, func, reads, writes, scale=1.0, bias=0.0, accum=None, eng="act"):
            if accum is None:
                return S.add(eng, lambda e: e.activation(out=out, in_=in_, func=func, scale=scale, bias=bias),
                             reads, writes)
            return S.add(eng, lambda e: e.activation(out=out, in_=in_, func=func, scale=scale, bias=bias,
                                                     accum_out=accum), reads, writes)

        def tt(out, a, b, op, reads, writes, eng="dve"):
            return S.add(eng, lambda e: e.tensor_tensor(out=out, in0=a, in1=b, op=op), reads, writes)

        def ts(out, a, s1, s2, op0, op1, reads, writes, eng="dve"):
            if op1 is None:
                return S.add(eng, lambda e: e.tensor_scalar(out=out, in0=a, scalar1=s1, scalar2=None, op0=op0),
                             reads, writes)
            return S.add(eng, lambda e: e.tensor_scalar(out=out, in0=a, scalar1=s1, scalar2=s2, op0=op0, op1=op1),
                         reads, writes)

        def stt(out, a, scalar, b, op0, op1, reads, writes):
            return S.add("dve", lambda e: e.scalar_tensor_tensor(out=out, in0=a, scalar=scalar, in1=b,
                                                                 op0=op0, op1=op1), reads, writes)

        def cp(out, in_, reads, writes, eng="dve"):
            if eng == "act":
                return act(out, in_, AF.Copy, reads, writes)
            return S.add(eng, lambda e: e.tensor_copy(out=out, in_=in_), reads, writes)

        def memset(ap, val, writes, eng="dve"):
            return S.add(eng, lambda e: e.memset(ap, val), (), writes)

        def rstd_from(ssq_ap, out_ap, n, reads, writes_tmp, tmp_ap):
            act(tmp_ap, ssq_ap, AF.Ln, reads, writes_tmp, scale=1.0 / n, bias=EPS)
            act(out_ap, tmp_ap, AF.Exp, writes_tmp, writes_tmp, scale=-0.5)

        def sigmoid_from(z_ap, z_reads, width):
            act(eg[:, 0:width], z_ap, AF.Exp, z_reads, [r_eg], scale=-1.0)
            act(sg[:, 0:width], eg[:, 0:width], AF.Ln, [r_eg], [r_sg], bias=1.0)
            act(sg[:, 0:width], sg[:, 0:width], AF.Exp, [r_sg], [r_sg], scale=-1.0)

        try:
            dma(cst[:], consts_d[:, :], [], [r_cst])
            dma(fnorm[:], fnorm_d[:, :], [], [r_fnorm])
            dma(cin[:], cT_d[:, :], [], [r_cin])
            dma(pos_i[:], pos_d[:, :], [], [r_posi])
            cp(ident_b[:], C("ident"), [r_cst], [r_ident])
            for buf, rb in ((qT_m, r_qTm), (qTd_m, r_qTdm), (qpT_m, r_qpTm), (qnT_m, r_qnTm), (S_b, r_Sb)):
                memset(buf[:], 0.0, [rb])
            memset(VA[:], 1.0, r_VA)
            memset(w_g2_b[:], 0.0, [r_wg2])
            act(ctmp[:], cin[:], AF.Exp, [r_cin], [r_ctmp], scale=-1.0)
            ts(ctmp[:], ctmp[:], 1.0, None, ALU.add, None, [r_ctmp], [r_ctmp])
            S.add("dve", lambda e: e.reciprocal(out=ctmp[:], in_=ctmp[:]), [r_ctmp], [r_ctmp])
            tt(cact[:], cin[:], ctmp[:], ALU.mult, [r_cin, r_ctmp], [r_cact])
            cact3 = cact[:].rearrange("p (k s) -> p k s", s=NSEQ)
            def ada_group(l, g, bk, rbk):
                st3 = stage[:, 0:1024].rearrange("p (k c) -> p k c", c=128)
                dma(st3, ada_w_d[l, :, g * 128:(g + 1) * 128].rearrange("(k p) c -> p k c", p=128),
                    [], [r_stage])
                dma(adab[:], ada_b_d[:, l * 3 * D + g * 128: l * 3 * D + (g + 1) * 128], [], [r_adab])
                for k in range(8):
                    mm(bk[0:NSEQ, 0:128], cact3[:, k, :], st3[:, k, :], k == 0, [r_cact, r_stage], [rbk])
                tt(modg[:], bk[0:NSEQ, 0:128], adab[:], ALU.add, [rbk, r_adab], [r_modg])
                dma(mods_d[l, :, g * 128:(g + 1) * 128], modg[:], [r_modg], [r_modsd[l][g]])

            for g in range(24):
                bk, rbk = gbank()
                ada_group(0, g, bk, rbk)
            ada_todo = [(l_, g_) for l_ in range(1, L) for g_ in range(24)]

            stage_(1)
            r_y = [[R("y%d_%d" % (s, t)) for t in range(NT)] for s in range(NSEQ)]
            loaded_layer = {"l": None}
            xslot = {"i": 0}

            def rope_tables(s):
                cp(pos_f[:], pos_i[:, s * NT:(s + 1) * NT], [r_posi], [r_posf])
                for c0 in range(0, NT, CH):
                    tt(ang[:], pos_f[:, c0:c0 + CH].unsqueeze(2).to_broadcast([128, CH, 48]),
                       C("inv48").unsqueeze(1).to_broadcast([128, CH, 48]), ALU.mult, [r_posf, r_cst], [r_ang])
                    ts(rr[:], ang[:], 1.0 / TWO_PI, None, ALU.mult, None, [r_ang], [r_rr])
                    cp(kk_i[:], rr[:], [r_rr], [r_kki])
                    cp(kk_f[:], kk_i[:], [r_kki], [r_kkf])
                    stt(rr[:], kk_f[:], -C1, ang[:], ALU.mult, ALU.add, [r_kkf, r_ang], [r_rr])
                    stt(rr[:], kk_f[:], -C2, rr[:], ALU.mult, ALU.add, [r_kkf, r_rr], [r_rr])

                    def wrap(buf, rb):
                        ts(mm_[:], buf[:], PI, None, ALU.is_gt, None, [rb], [r_mm])
                        stt(buf[:], mm_[:], -TWO_PI, buf[:], ALU.mult, ALU.add, [r_mm, rb], [rb])
                        ts(mm_[:], buf[:], -PI, None, ALU.is_lt, None, [rb], [r_mm])
                        stt(buf[:], mm_[:], TWO_PI, buf[:], ALU.mult, ALU.add, [r_mm, rb], [rb])
                        ts(buf[:], buf[:], -PI_LO, PI_LO, ALU.max, ALU.min, [rb], [rb])

                    wrap(rr, r_rr)
                    ts(rc[:], rr[:], PI / 2, None, ALU.add, None, [r_rr], [r_rc])
                    wrap(rc, r_rc)
                    act(Tsin[:, c0:c0 + CH, :], rr[:], AF.Sin, [r_rr], [r_Tsin])
                    act(Tcos[:, c0:c0 + CH, :], rc[:], AF.Sin, [r_rc], [r_Tcos])

            for l in range(L):
                while ada_todo and ada_todo[0][0] <= l:
                    l_, g_ = ada_todo.pop(0)
                    bk, rbk = gbank()
                    ada_group(l_, g_, bk, rbk)
                for s in range(NSEQ):
                    rope_tables(s)
                    last_layer = (l == L - 1)
                    dma(modc[:, 0:8], mods_d[l, s, 0:D].rearrange("(k p) -> p k", p=128), r_modsd[l][0:8], [r_modc],
                        allow_slow_non_contiguous=True)
                    dma(modc[:, 8:16], mods_d[l, s, D:2 * D].rearrange("(k p) -> p k", p=128), r_modsd[l][8:16], [r_modc],
                        allow_slow_non_contiguous=True)
                    dma(stage[0:1, 0:D], mods_d[l, s:s + 1, 2 * D:3 * D], r_modsd[l][16:24], [r_stage])
                    stt(gp[:], modc[:, 8:16], 1.0, cst[:, o_nw + l * 8:o_nw + (l + 1) * 8], ALU.add, ALU.mult,
                        [r_modc, r_cst], [r_gp])
                    for hf in range(2):
                        bk, rbk = gbank()
                        mm(bk, C("ones")[0:1, :], stage[0:1, hf * 512:(hf + 1) * 512], True, [r_cst, r_stage], [rbk])
                        cp(gate_bc[:, hf * 512:(hf + 1) * 512], bk, [rbk], [r_gate], eng="act")
                    if loaded_layer["l"] != l:
                        loaded_layer["l"] = l
                        for k in range(8):
                            for (a, b) in ((0, 1368), (1368, INC)):
                                dma(w_in_b[:, k, a:b], w_in_d[l, k * 128:(k + 1) * 128, a:b], [], [r_win], eng="pool")
                            dma(w_out_b[:, k, :], w_out_d[l, k * 128:(k + 1) * 128, :], [], [r_wout], eng="pool")
                        dma(w_g2_b[0:16, :], w_g2_d[l, :, :], [], [r_wg2], eng="pool")
                        for k in range(2):
                            dma(stage[:, 0:768], w_uq_d[l, k * 128:(k + 1) * 128, :], [], [r_stage])
                            ts(w_uq_b[:, k, :], stage[:, 0:768], cst[:, o_qn + l * 2 + k:o_qn + l * 2 + k + 1], None,
                               ALU.mult, None, [r_stage, r_cst], [r_wuq])
                        dma(stage[:, 0:1024], w_ukv_d[l, :, :], [], [r_stage])
                        ts(w_ukv_b[:], stage[:, 0:1024], cst[:, o_kv + l:o_kv + l + 1], None, ALU.mult, None,
                           [r_stage, r_cst], [r_wukv])
                    memset(S32[:], 0.0, [r_S32])
                    for sl in range(4):
                        hh = sl % 2
                        memset(S_b[64 * hh:64 * hh + 64, sl, :], 0.0, [r_Sb])
                    memset(Sg32[:], 0.0, [r_Sg32])
                    memset(Sg_b[:], 0.0, [r_Sgb])

                    stage_(3)
                    bank_R = (PP[0][:, 0:512], r_bank[0][0])
                    bank_G = (PP[0][:, 512:1024], r_bank[0][1])
                    bank_M = [(PP[2][:, 0:512], r_bank[2][0]), (PP[2][:, 512:1024], r_bank[2][1])]
                    mstate = {"i": 0}

                    def nextM():
                        return bank_M[0]

                    def inproj_to(bk_, rbk_, c0, c1, t):
                        hT = hTs[t % 2]
                        for k in range(8):
                            mm(bk_[:, 0:c1 - c0], hT[:, k, :], w_in_b[:, k, c0:c1], k == 0, r_hTs[t % 2] + [r_win], [rbk_])

                    def norm_gate(bo, rbo, stx, rstx, gAx, rgAx, gBx, rgBx, out_ap, rout):
                        act(eg[:, 0:256], bo[:, 0:256], AF.Square, [rbo], [r_eg])
                        S.add("dve", lambda e: e.tensor_reduce(out=stx[:, 4:8], in_=eg[:, 0:256].rearrange("p (h d) -> p h d", d=64),
                                                                axis=AX.X, op=ALU.add), [r_eg], [rstx])
                        rstd_from(stx[:, 4:8], stx[:, 8:12], 64, [rstx], [rstx], stx[:, 12:16])
                        tt(gAx[:, 0:256].rearrange("p (h d) -> p h d", d=64), bo[:, 0:256].rearrange("p (h d) -> p h d", d=64),
                           stx[:, 8:12].unsqueeze(2).to_broadcast([128, 4, 64]), ALU.mult, [rbo, rstx], [rgAx])
                        tt(out_ap, gAx[:, 0:256], gBx[:, 0:256], ALU.mult, [rgAx, rgBx], [rout], eng="pool")

                    def hT_stage(t, X, rX):
                        hT = hTs[t % 2]
                        rh = r_hTs[t % 2]
                        src = x_d if l == 0 else y_d
                        dma(X[:], src[s, t * 128:(t + 1) * 128, :], [r_y[s][t]] if l > 0 else [], [rX])
                        act(xn[:], X[:], AF.Square, [rX], [r_xn, r_st], accum=st_[:, 0:1])
                        rstd_from(st_[:, 0:1], st_[:, 2:3], D, [r_st], [r_st], st_[:, 1:2])
                        S.add("pool", lambda e: e.tensor_scalar(out=xn[:], in0=X[:], scalar1=st_[:, 2:3], scalar2=1.0,
                                                                 op0=ALU.mult, op1=ALU.mult), [rX, r_st], [r_xn])
                        yield
                        bA, rbA = bank_R
                        bAb = bA.bitcast(BF16)
                        for k in range(8):
                            tr(bAb[:, k * 128:(k + 1) * 128], xn[:, k * 128:(k + 1) * 128], [r_xn], [rbA])
                        for k in range(4):
                            ts(hT[:, k, :], bAb[:, k * 128:(k + 1) * 128], gp[:, k:k + 1], modc[:, k:k + 1],
                               ALU.mult, ALU.add, [rbA, r_gp, r_modc], [rh[0]])
                        for k in range(4, 8):
                            act(hT[:, k, :], bAb[:, k * 128:(k + 1) * 128], AF.Identity, [rbA, r_gp, r_modc], [rh[1]],
                                scale=gp[:, k:k + 1], bias=modc[:, k:k + 1])

                    def ret_stream(t):
                        bR, rbR = bank_R
                        mx, rmx = mixed2[t % 2], r_mixed2[t % 2]
                        inproj_to(bR, rbR, 0, 512, t)
                        cp(qkf[:], bR, [rbR], [r_qkf], eng="act")
                        yield
                        q3 = qkf[:].rearrange("p (h d) -> p h d", d=64)
                        x1 = q3[:, :, 0:32]
                        x2 = q3[:, :, 32:64]
                        cosb = Tcos[:, t, 0:32].unsqueeze(1).to_broadcast([128, 8, 32])
                        sinb = Tsin[:, t, 0:32].unsqueeze(1).to_broadcast([128, 8, 32])
                        tA3 = tA[:].rearrange("p (h d) -> p h d", d=32)
                        tB3 = tB[:].rearrange("p (h d) -> p h d", d=32)
                        tt(tA3, x1, cosb, ALU.mult, [r_qkf, r_Tcos], [r_tA], eng="pool")
                        tt(tB3, x2, sinb, ALU.mult, [r_qkf, r_Tsin], [r_tB], eng="pool")
                        tt(qkrot[:, :, 0:32], tA3, tB3, ALU.subtract, [r_tA, r_tB], [r_qkrot], eng="pool")
                        tt(tA3, x2, cosb, ALU.mult, [r_qkf, r_Tcos], [r_tA], eng="pool")
                        tt(tB3, x1, sinb, ALU.mult, [r_qkf, r_Tsin], [r_tB], eng="pool")
                        tt(qkrot[:, :, 32:64], tA3, tB3, ALU.add, [r_tA, r_tB], [r_qkrot], eng="pool")
                        tt(kdec[:], qkrot[:, 4:8, :], C("wk").unsqueeze(2).to_broadcast([128, 4, 64]), ALU.mult,
                           [r_qkrot, r_cst], [r_kdec], eng="pool")
                        yield
                        qkflat = qkrot[:].rearrange("p h d -> p (h d)")
                        bRb = bR.bitcast(BF16)
                        for j in range(4):
                            tr(bRb[:, j * 128:(j + 1) * 128], qkflat[:, j * 128:(j + 1) * 128], [r_qkrot], [rbR])
                        for pair in range(2):
                            for hh in range(2):
                                cp(qT_m[64 * hh:64 * hh + 64, 2 * pair + hh, :],
                                   bRb[64 * hh:64 * hh + 64, pair * 128:(pair + 1) * 128], [rbR], [r_qTm], eng="act")
                        cp(kTr[:], bRb[:, 256:512].rearrange("p (a b) -> p a b", b=128), [rbR], [r_kTr])
                        wq3 = C("wqT").rearrange("p (a b) -> p a b", b=128)
                        for pair in range(2):
                            for hh in range(2):
                                tt(qTd_m[64 * hh:64 * hh + 64, 2 * pair + hh, :], qT_m[64 * hh:64 * hh + 64, 2 * pair + hh, :],
                                   wq3[64 * hh:64 * hh + 64, pair, :], ALU.mult, [r_qTm, r_cst], [r_qTdm], eng="pool")
                        yield
                        for hd in range(4):
                            mm(bR[:, hd * 128:(hd + 1) * 128], kTr[:, hd // 2, :], qT_m[:, hd, :], hd == 0,
                               [r_kTr, r_qTm], [rbR])
                        tt(STm[:].rearrange("p a b -> p (a b)"), bR, C("maskR"), ALU.mult, [rbR, r_cst], [r_STm])
                        yield
                        inproj_to(bR, rbR, 512, 1024, t)
                        cp(vb_r[:], bR[:, 0:256], [rbR], [r_vbr], eng="act")
                        sigmoid_from(bR[:, 256:512], [rbR], 256)
                        tt(gB[:, 0:256], bR[:, 256:512], sg[:, 0:256], ALU.mult, [rbR, r_sg], [r_gB])
                        yield
                        for hd in range(4):
                            mm(bR[:, hd * 64:(hd + 1) * 64], STm[:, hd, :], vb_r[:, hd * 64:(hd + 1) * 64], hd == 0,
                               [r_STm, r_vbr], [rbR])
                            mm(bR[:, hd * 64:(hd + 1) * 64], qTd_m[:, hd, :], S_b[:, hd, :], False, [r_qTdm, r_Sb], [rbR])
                        norm_gate(bR, rbR, st_r, r_str, gA, r_gA, gB, r_gB, mx[:, 0:256], rmx)
                        yield
                        kd2 = kdec[:].rearrange("p h d -> p (h d)")
                        for pair in range(2):
                            mm(bR[:, pair * 128:(pair + 1) * 128], kd2[:, pair * 128:(pair + 1) * 128],
                               vb_r[:, pair * 128:(pair + 1) * 128], pair == 0, [r_kdec, r_vbr], [rbR])
                        for pair in range(2):
                            for hh in range(2):
                                p0 = 64 * hh
                                stt(S32[p0:p0 + 64, pair, :], S32[p0:p0 + 64, pair, :],
                                    cst[p0:p0 + 64, _off["g128"][0] + pair:_off["g128"][0] + pair + 1],
                                    bR[p0:p0 + 64, pair * 128 + hh * 64: pair * 128 + hh * 64 + 64],
                                    ALU.mult, ALU.add, [r_S32, r_cst, rbR], [r_S32])
                                cp(S_b[p0:p0 + 64, 2 * pair + hh, :], S32[p0:p0 + 64, pair, :], [r_S32], [r_Sb], eng="pool")

                    def gla_gate(t):
                        bG, rbG = bank_G
                        bGb = bG.bitcast(BF16)
                        e3, r_e3 = e3s[t % 2], r_e3s[t % 2]
                        edec, r_edec = edecs[t % 2], r_edecs[t % 2]
                        gB_g, r_gBg = gB_gs[t % 2], r_gBgs[t % 2]
                        inproj_to(bG, rbG, 2464, 2736, t)
                        cp(glow_b[:], bG[:, 0:32], [rbG], [r_glowb], eng="act")
                        sigmoid_from(bG[:, 16:272], [rbG], 256)
                        tt(gB_g[:, 0:256], bG[:, 16:272], sg[:, 0:256], ALU.mult, [rbG, r_sg], [r_gBg])
                        tt(gB_g[:, 0:256].rearrange("p (h d) -> p h d", d=64), gB_g[:, 0:256].rearrange("p (h d) -> p h d", d=64),
                           cst[:, o_gn + l * 64:o_gn + (l + 1) * 64].unsqueeze(1).to_broadcast([128, 4, 64]), ALU.mult,
                           [r_gBg, r_cst], [r_gBg], eng="pool")
                        yield
                        tr(bGb[0:32, 0:128], glow_b[:], [r_glowb], [rbG])
                        cp(glowT[:], bGb[0:32, 0:128], [rbG], [r_glowT])
                        yield
                        mm(bG[:, 0:128], glowT[:], w_g2_b[:], True, [r_glowT, r_wg2], [rbG])
                        tt(lf[:], bG[:, 0:128], cst[:, o_bg + l * 128:o_bg + (l + 1) * 128], ALU.add, [rbG, r_cst], [r_lf])
                        act(lf[:], lf[:], AF.Exp, [r_lf], [r_lf], scale=-1.0)
                        act(lf[:], lf[:], AF.Ln, [r_lf], [r_lf], bias=1.0)
                        yield
                        mm(bG[:, 0:128], C("tri"), lf[:], True, [r_cst, r_lf], [rbG])
                        mm(bG[:, 128:256], C("trimo"), lf[:], False, [r_cst, r_lf], [rbG])
                        mm(bG[:, 256:258], lf[:], C("ones")[:, 0:2], False, [r_cst, r_lf], [rbG])
                        act(e3[:, 0, :], bG[:, 0:128], AF.Exp, [rbG], [r_e3], scale=-1.0 / 16)
                        act(e3[:, 1, :], bG[:, 0:128], AF.Exp, [rbG], [r_e3], scale=1.0 / 16)
                        act(e3[:, 2, :], bG[:, 128:256], AF.Exp, [rbG], [r_e3], scale=1.0 / 16,
                            bias=float(math.log(32 ** -0.5)))
                        act(edec[:], bG[:, 256:257], AF.Exp, [rbG], [r_edec], scale=-1.0 / 16)

                    def gla_stream(t):
                        bG, rbG = bank_G
                        mx, rmx = mixed2[t % 2], r_mixed2[t % 2]
                        bGb = bG.bitcast(BF16)
                        hm0 = _off["hm"][0]
                        e3, r_e3 = e3s[t % 2], r_e3s[t % 2]
                        edec, r_edec = edecs[t % 2], r_edecs[t % 2]
                        gB_g, r_gBg = gB_gs[t % 2], r_gBgs[t % 2]
                        inproj_to(bG, rbG, 1952, 2464, t)
                        cp(vb_g[:], bG[:, 256:512], [rbG], [r_vbg], eng="act")
                        tt(qk4[:, 0, :], bG[:, 0:128], e3[:, 0, :], ALU.mult, [rbG, r_e3], [r_qk4])
                        tt(qk4[:, 1, :], bG[:, 0:128], e3[:, 1, :], ALU.mult, [rbG, r_e3], [r_qk4])
                        tt(qk4[:, 2, :], bG[:, 128:256], e3[:, 1, :], ALU.mult, [rbG, r_e3], [r_qk4])
                        tt(qk4[:, 3, :], bG[:, 128:256], e3[:, 0, :], ALU.mult, [rbG, r_e3], [r_qk4])
                        tt(kdg[:], bG[:, 128:256], e3[:, 2, :], ALU.mult, [rbG, r_e3], [r_kdg])
                        yield
                        for j in range(4):
                            tr(bGb[:, j * 128:(j + 1) * 128], qk4[:, j, :], [r_qk4], [rbG])
                        for hd in range(4):
                            ts(qpT_m[:, hd, :], bGb[:, 0:128], cst[:, hm0 + hd:hm0 + hd + 1], None, ALU.mult, None,
                               [rbG, r_cst], [r_qpTm])
                            ts(qnT_m[:, hd, :], bGb[:, 128:256], cst[:, hm0 + hd:hm0 + hd + 1], None, ALU.mult, None,
                               [rbG, r_cst], [r_qnTm])
                        cp(kTg[:], bGb[:, 256:512].rearrange("p (a b) -> p a b", b=128), [rbG], [r_kTg])
                        yield
                        m1b = C("M1s").unsqueeze(1).to_broadcast([128, 4, 128])
                        m2b = C("M2s").unsqueeze(1).to_broadcast([128, 4, 128])
                        for hd in range(4):
                            mm(bG[:, hd * 128:(hd + 1) * 128], kTg[:, 0, :], qpT_m[:, hd, :], hd == 0, [r_kTg, r_qpTm], [rbG])
                        tt(A1[:], bG.rearrange("p (a b) -> p a b", b=128), m1b, ALU.mult, [rbG, r_cst], [r_A1])
                        yield
                        for hd in range(4):
                            mm(bG[:, hd * 128:(hd + 1) * 128], kTg[:, 1, :], qnT_m[:, hd, :], hd == 0, [r_kTg, r_qnTm], [rbG])
                        tt(A2[:], bG.rearrange("p (a b) -> p a b", b=128), m2b, ALU.mult, [rbG, r_cst], [r_A2])
                        yield
                        for hd in range(4):
                            o_ = bG[:, hd * 64:(hd + 1) * 64]
                            v_ = vb_g[:, hd * 64:(hd + 1) * 64]
                            mm(o_, A1[:, hd, :], v_, hd == 0, [r_A1, r_vbg], [rbG])
                            mm(o_, A2[:, hd, :], v_, False, [r_A2, r_vbg], [rbG])
                            mm(o_, qpT_m[:, hd, :], Sg_b[:], False, [r_qpTm, r_Sgb], [rbG])
                        norm_gate(bG, rbG, st_g, r_stg, gA_g, r_gAg, gB_g, r_gBg, mx[:, 768:1024], rmx)
                        yield
                        mm(bG[:, 0:256], kdg[:], vb_g[:], True, [r_kdg, r_vbg], [rbG])
                        ts(Sg32[:], Sg32[:], edec[:, 0:1], None, ALU.mult, None, [r_Sg32, r_edec], [r_Sg32])
                        for hd in range(4):
                            stt(Sg32[:], bG[:, hd * 64:(hd + 1) * 64], cst[:, hm0 + hd:hm0 + hd + 1], Sg32[:],
                                ALU.mult, ALU.add, [r_Sg32, r_cst, rbG], [r_Sg32])
                        cp(Sg_b[:], Sg32[:], [r_Sg32], [r_Sgb], eng="pool")

                    def mla_stream(t):
                        par = t % 2
                        g3, rg3 = nextM()
                        inproj_to(g3, rg3, 1440, 1952, t)
                        for hz in range(2):
                            sigmoid_from(g3[:, hz * 256:(hz + 1) * 256], [rg3], 256)
                            tt(gBm[par][:, hz * 256:(hz + 1) * 256], g3[:, hz * 256:(hz + 1) * 256], sg[:, 0:256], ALU.mult,
                               [rg3, r_sg], [r_gBm[par]])
                        yield
                        g2, rg2 = nextM()
                        inproj_to(g2, rg2, 1024, 1440, t)
                        act(latn[:, 0:256], g2[:, 0:256], AF.Square, [rg2], [r_latn, r_stm], accum=st_m[:, 4:5])
                        act(latn[:, 256:384], g2[:, 256:384], AF.Square, [rg2], [r_latn, r_stm], accum=st_m[:, 5:6])
                        cp(krf[:], g2[:, 384:416], [rg2], [r_krf], eng="act")
                        rstd_from(st_m[:, 4:5], st_m[:, 8:9], 256, [r_stm], [r_stm], st_m[:, 12:13])
                        rstd_from(st_m[:, 5:6], st_m[:, 9:10], 128, [r_stm], [r_stm], st_m[:, 13:14])
                        ts(latn[:, 0:256], g2[:, 0:256], st_m[:, 8:9], None, ALU.mult, None, [rg2, r_stm], [r_latn])
                        ts(latn[:, 256:384], g2[:, 256:384], st_m[:, 9:10], None, ALU.mult, None, [rg2, r_stm], [r_latn])
                        cm = Tcos[:, t, 32:48]
                        sm = Tsin[:, t, 32:48]
                        tt(tA[:, 0:16], krf[:, 0:16], cm, ALU.mult, [r_krf, r_Tcos], [r_tA], eng="pool")
                        tt(tB[:, 0:16], krf[:, 16:32], sm, ALU.mult, [r_krf, r_Tsin], [r_tB], eng="pool")
                        tt(kpe[:, 0:16], tA[:, 0:16], tB[:, 0:16], ALU.subtract, [r_tA, r_tB], [r_kpe], eng="pool")
                        tt(tA[:, 0:16], krf[:, 16:32], cm, ALU.mult, [r_krf, r_Tcos], [r_tA], eng="pool")
                        tt(tB[:, 0:16], krf[:, 0:16], sm, ALU.mult, [r_krf, r_Tsin], [r_tB], eng="pool")
                        tt(kpe[:, 16:32], tA[:, 0:16], tB[:, 0:16], ALU.add, [r_tA, r_tB], [r_kpe], eng="pool")
                        cp(Kfull[:, :, 64:96], kpe[:].unsqueeze(1).to_broadcast([128, 8, 32]), [r_kpe], [r_Kfull], eng="pool")
                        yield
                        bk, rbk = nextM()
                        bkb = bk.bitcast(BF16)
                        for j in range(3):
                            tr(bkb[:, j * 128:(j + 1) * 128], latn[:, j * 128:(j + 1) * 128], [r_latn], [rbk])
                        cp(latT[:], bkb[:, 0:384].rearrange("p (a b) -> p a b", b=128), [rbk], [r_latT])
                        yield
                        for hf in range(2):
                            bq, rbq = nextM()
                            for k in range(2):
                                mm(bq[:, 0:384], latT[:, k, :], w_uq_b[:, k, hf * 384:(hf + 1) * 384], k == 0,
                                   [r_latT, r_wuq], [rbq])
                            q3 = bq[:, 0:384].rearrange("p (h d) -> p h d", d=96)
                            Qh = Qb[:, hf * 4:(hf + 1) * 4, :]
                            cp(Qh[:, :, 0:64], q3[:, :, 0:64], [rbq], [r_Qb], eng="act")
                            cm4 = cm.unsqueeze(1).to_broadcast([128, 4, 16])
                            sm4 = sm.unsqueeze(1).to_broadcast([128, 4, 16])
                            tA4 = tA[:, 0:64].rearrange("p (h d) -> p h d", d=16)
                            tB4 = tB[:, 0:64].rearrange("p (h d) -> p h d", d=16)
                            tt(tA4, q3[:, :, 64:80], cm4, ALU.mult, [rbq, r_Tcos], [r_tA])
                            tt(tB4, q3[:, :, 80:96], sm4, ALU.mult, [rbq, r_Tsin], [r_tB])
                            tt(Qh[:, :, 64:80], tA4, tB4, ALU.subtract, [r_tA, r_tB], [r_Qb])
                            tt(tA4, q3[:, :, 80:96], cm4, ALU.mult, [rbq, r_Tcos], [r_tA])
                            tt(tB4, q3[:, :, 64:80], sm4, ALU.mult, [rbq, r_Tsin], [r_tB])
                            tt(Qh[:, :, 80:96], tA4, tB4, ALU.add, [r_tA, r_tB], [r_Qb])
                        yield
                        bk, rbk = nextM()
                        bkb = bk.bitcast(BF16)
                        for hd in range(8):
                            tr(bkb[0:96, hd * 128:(hd + 1) * 128], Qb[:, hd, :], [r_Qb], [rbk])
                        cp(QT2[par][:], bkb[0:96, :].rearrange("p (a b) -> p a b", b=128), [rbk], [r_QT2[par]])
                        yield
                        for hf in range(2):
                            bq, rbq = nextM()
                            mm(bq, latT[:, 2, :], w_ukv_b[:, hf * 512:(hf + 1) * 512], True, [r_latT, r_wukv], [rbq])
                            kv3 = bq.rearrange("p (h d) -> p h d", d=128)
                            cp(Kfull[:, hf * 4:(hf + 1) * 4, 0:64], kv3[:, :, 0:64], [rbq], [r_Kfull], eng="act")
                            cp(VA[:, t, hf * 4:(hf + 1) * 4, 0:64], kv3[:, :, 64:128], [rbq], [r_VA[t]])
                        yield
                        bk, rbk = nextM()
                        bkb = bk.bitcast(BF16)
                        for hd in range(8):
                            tr(bkb[0:96, hd * 128:(hd + 1) * 128], Kfull[:, hd, :], [r_Kfull], [rbk])
                        cp(KT[:, :, t * 128:(t + 1) * 128], bkb[0:96, :].rearrange("p (a b) -> p a b", b=128), [rbk], [r_KT[t]],
                           eng="act")
                        if ada_todo:
                            yield
                            l_, g_ = ada_todo.pop(0)
                            bk, rbk = nextM()
                            ada_group(l_, g_, bk, rbk)

                    def phaseB(t, X, rX):
                        par = t % 2
                        QTt, rQTt = QT2[par], r_QT2[par]
                        mx, rmx = mixed2[par], r_mixed2[par]
                        sc_scale = float(96 ** -0.5)
                        acc = PP[1]
                        racc = r_bank[1]
                        scp = PP[3]
                        rsc = r_bank[3]
                        rot = [(scp[:, 0:512], rsc[0]), (scp[:, 512:1024], rsc[1]), bank_M[1]]

                        def qk_exp(j):
                            kt, hf = j // 2, j % 2
                            sb_, rsb_ = rot[j % 3]
                            P_ = PT[kt % 2]
                            rP = r_PT[kt % 2]
                            for i4 in range(4):
                                hd = 4 * hf + i4
                                mm(sb_[:, i4 * 128:(i4 + 1) * 128], KT[:, hd, kt * 128:(kt + 1) * 128], QTt[:, hd, :],
                                   i4 == 0, [r_KT[kt], rQTt], [rsb_])
                            act(P_[:, 4 * hf:4 * hf + 4, :].rearrange("p a b -> p (a b)"), sb_, AF.Exp, [rsb_], [rP[hf]],
                                scale=sc_scale)
                            if kt == t:
                                memset(P_[64:128, 4 * hf:4 * hf + 4, 0:64], 0.0, [rP[hf]], eng="pool")

                        def pv(j):
                            kt, hf = j // 2, j % 2
                            P_ = PT[kt % 2]
                            rP = r_PT[kt % 2]
                            for i4 in range(4):
                                hd = 4 * hf + i4
                                o_ = acc[:, hf * 512 + i4 * 65: hf * 512 + i4 * 65 + 65]
                                mm(o_, P_[:, hd, :], VA[:, kt, hd, :], (kt == 0 and i4 == 0), [rP[hf], r_VA[kt]], [racc[hf]])

                        nj = 2 * (t + 1)
                        qk_exp(0)
                        qk_exp(1)
                        for j in range(nj):
                            if j + 2 < nj:
                                qk_exp(j + 2)
                            pv(j)
                            if j % 2 == 1:
                                yield
                        for hf in range(2):
                            a3 = acc[:, hf * 512: hf * 512 + 260].rearrange("p (h d) -> p h d", d=65)
                            S.add("dve", lambda e, a3=a3, hf=hf: e.reciprocal(out=rs[:, hf * 4:(hf + 1) * 4].unsqueeze(2),
                                                                             in_=a3[:, :, 64:65]), [racc[hf]], [r_rs])
                            tt(gAm[:, hf * 256:(hf + 1) * 256].rearrange("p (h d) -> p h d", d=64), a3[:, :, 0:64],
                               rs[:, hf * 4:(hf + 1) * 4].unsqueeze(2).to_broadcast([128, 4, 64]), ALU.mult,
                               [racc[hf], r_rs], [r_gAm])
                        tt(mx[:, 256:768], gAm[:], gBm[par][:], ALU.mult, [r_gAm, r_gBm[par]], [rmx], eng="pool")
                        yield
                        stage_(8)
                        bk = scp[:, 0:512]
                        rbk = rsc[0]
                        bkb = bk.bitcast(BF16)
                        for k in range(8):
                            tr(bkb[:, k * 128:(k + 1) * 128], mx[:, k * 128:(k + 1) * 128], [rmx], [rbk])
                        cp(mixT[:], bkb.rearrange("p (a b) -> p a b", b=128), [rbk], [r_mixT])
                        yield
                        for hf in range(2):
                            bo = scp[:, (1 - hf) * 512:(2 - hf) * 512]
                            rbo = rsc[1 - hf]
                            for k in range(8):
                                mm(bo, mixT[:, k, :], w_out_b[:, k, hf * 512:(hf + 1) * 512], k == 0, [r_mixT, r_wout], [rbo])
                            tt(gAm[:], bo, gate_bc[:, hf * 512:(hf + 1) * 512], ALU.mult, [rbo, r_gate], [r_gAm])
                            tt(X[:, hf * 512:(hf + 1) * 512], X[:, hf * 512:(hf + 1) * 512], gAm[:], ALU.add,
                               [rX, r_gAm], [rX], eng="pool")
                            yield
                        if last_layer:
                            act(mixT[:].rearrange("p a b -> p (a b)"), X[:], AF.Square, [rX], [r_mixT, r_stb], accum=st_b[:, 0:1])
                            rstd_from(st_b[:, 0:1], st_b[:, 2:3], D, [r_stb], [r_stb], st_b[:, 1:2])
                            stt(X[:], X[:], st_b[:, 2:3], fnorm[:], ALU.mult, ALU.mult, [rX, r_stb, r_fnorm], [rX])
                        dma(y_d[s, t * 128:(t + 1) * 128, :], X[:], [rX], [r_y[s][t]])

                    def drain(g):
                        for _ in g:
                            pass

                    def interleave(ga, gb, na, nb):
                        ia = ib = 0
                        da = db = False
                        while not (da and db):
                            if db or (not da and ia * nb <= ib * na):
                                try:
                                    next(ga)
                                    ia += 1
                                except StopIteration:
                                    da = True
                            else:
                                try:
                                    next(gb)
                                    ib += 1
                                except StopIteration:
                                    db = True

                    def round_robin(gens, weights):
                        active = list(range(len(gens)))
                        while active:
                            for i in list(active):
                                for _ in range(weights[i]):
                                    try:
                                        next(gens[i])
                                    except StopIteration:
                                        active.remove(i)
                                        break

                    Xs = {}

                    def alloc_x(t):
                        xi = xslot["i"] % 3
                        xslot["i"] += 1
                        Xs[t] = (xt[xi], r_xt[xi])
                        return Xs[t]

                    def chain2(g1, g2):
                        for _ in g1:
                            yield
                        yield
                        for _ in g2:
                            yield

                    prevB = None
                    drain(hT_stage(0, *alloc_x(0)))
                    drain(gla_gate(0))
                    for t in range(NT):
                        gens = [gla_stream(t), mla_stream(t), ret_stream(t)]
                        weights = [1, 1, 1]
                        if not PIPE:
                            if prevB is not None:
                                drain(prevB)
                            for g_ in gens:
                                drain(g_)
                            if t + 1 < NT:
                                drain(hT_stage(t + 1, *alloc_x(t + 1)))
                                drain(gla_gate(t + 1))
                        else:
                            if t + 1 < NT:
                                gens.append(chain2(hT_stage(t + 1, *alloc_x(t + 1)), gla_gate(t + 1)))
                                weights.append(1)
                            if prevB is not None:
                                gens.append(prevB)
                                weights.append(max(1, -(-(t + 4) // 10)))
                            round_robin(gens, weights)
                        prevB = phaseB(t, *Xs[t])
                    drain(prevB)

        except StopBuild:
            pass
        S.add("sp", None, [r_y[s][t] for s in range(NSEQ) for t in range(NT)], [])

        if os.environ.get("KDBG"):
            print("SBUF remaining", nc.sbuf_bytes_remaining, "ops", {e: len(v) for e, v in S.ops.items()})
        S.finalize(sems, dsems)
        with nc.Block() as block:
            @block.sync
            def _(e):
                S.replay("sp", e)

            @block.gpsimd
            def _(e):
                S.replay("pool", e)

            @block.vector
            def _(e):
                S.replay("dve", e)

            @block.scalar
            def _(e):
                S.replay("act", e)

            @block.tensor
            def _(e):
                S.replay("pe", e)
    return nc


def run(inputs, NSEQ, NT, L, ncores=NCORES, trace=False):
    f = np.float32
    x = np.asarray(inputs["x"], f)
    c = np.asarray(inputs["c"], f)
    pos = np.asarray(inputs["positions"], np.int32)
    consts = _const_pack(L, np.asarray(inputs["norm_w"], f), np.asarray(inputs["mla_q_norm"], f),
                         np.asarray(inputs["mla_kv_norm"], f), np.asarray(inputs["gla_norm"], f),
                         np.asarray(inputs["gla_b_g2"], f))
    fnorm = np.ascontiguousarray(np.broadcast_to(np.asarray(inputs["final_norm"], f)[None, :], (128, D)))
    ada_b = np.asarray(inputs["ada_b"], f).reshape(1, L * 3 * D)
    ada_b_rep = np.ascontiguousarray(np.broadcast_to(ada_b, (NSEQ, L * 3 * D)))
    shared = {
        "consts": consts, "fnorm": fnorm, "ada_w": np.ascontiguousarray(inputs["ada_w"], f), "ada_b": ada_b_rep,
        "w_in": np.ascontiguousarray(inputs["w_in"], f), "w_uq": np.ascontiguousarray(inputs["w_uq"], f),
        "w_ukv": np.ascontiguousarray(inputs["w_ukv"], f), "w_g2": np.ascontiguousarray(inputs["gla_w_g2"], f),
        "b_g2": np.ascontiguousarray(inputs["gla_b_g2"], f), "w_out": np.ascontiguousarray(inputs["w_out"], f),
    }
    in_maps = []
    for ci in range(ncores):
        sl = slice(ci * NSEQ, (ci + 1) * NSEQ)
        xc = np.ascontiguousarray(x[sl])
        cc = c[sl]
        cT = np.ascontiguousarray(cc.reshape(NSEQ, 8, 128).transpose(2, 1, 0).reshape(128, 8 * NSEQ))
        pc = pos[sl]
        pT = np.ascontiguousarray(pc.reshape(NSEQ, NT, 128).transpose(2, 0, 1).reshape(128, NSEQ * NT))
        m = {"x": xc, "cT": cT, "pos": pT}
        m.update(shared)
        in_maps.append(m)
    nc = build(NSEQ, NT, L)
    res = run_bass_kernel_spmd(nc, in_maps, core_ids=list(range(ncores)), trace=trace)
    out = np.concatenate([r["y"] for r in res.results], axis=0)
    return out, res


def kernel(**inputs):
    out, _ = run(inputs, NSEQ=4, NT=16, L=2)
    return out.astype(np.float32)
```

```python
import math
from contextlib import ExitStack

import numpy as np
import concourse.bass as bass
import concourse.mybir as mybir
from concourse.bass_utils import run_bass_kernel_spmd

F32 = mybir.dt.float32
BF16 = mybir.dt.bfloat16
I32 = mybir.dt.int32
AF = mybir.ActivationFunctionType
ALU = mybir.AluOpType
AX = mybir.AxisListType

D = 1024
INC = 2736
EPS = 1e-6
NCORES = 8
PI = math.pi
TWO_PI = 2.0 * math.pi
C1 = 6.28125
C2 = TWO_PI - 6.28125
PI_LO = 3.1415925

_off = {}
_cur = 0


def _reg(name, n):
    global _cur
    _off[name] = (_cur, _cur + n)
    _cur += n


_reg("ident", 128)
_reg("maskR", 512)
_reg("tri", 128)
_reg("ones", 128)
_reg("trimo", 128)
_reg("M1s", 128)
_reg("M2s", 128)
_reg("wqT", 256)
_reg("wk", 4)
_reg("g128", 2)
_reg("inv48", 48)
_reg("hm", 4)
NCONST_BASE = _cur


def _const_pack(L, norm_w, mla_q_norm, mla_kv_norm, gla_norm, b_g2):
    f = np.float32
    h = np.arange(4, dtype=f)
    log_gamma = np.log1p(-np.exp2(-5.0 - h)).astype(f)
    idx = np.arange(128, dtype=f)
    cj = (np.arange(128) // 64)
    maskR = np.zeros((128, 4, 128), f)
    for hh in range(4):
        dist = np.abs(idx[None, :] - idx[:, None])
        dec = np.exp(log_gamma[hh] * dist).astype(f)
        same = cj[:, None] == cj[None, :]
        past = cj[:, None] < cj[None, :]
        maskR[:, hh, :] = np.where(same | past, dec, 0.0) * f(0.125)
    tri = (np.arange(128)[:, None] <= np.arange(128)[None, :]).astype(f)
    ones = np.ones((128, 128), f)
    trimo = tri - ones
    sc = f(32 ** -0.5)
    M1s = tri * sc
    M2s = ((np.arange(128)[:, None] > np.arange(128)[None, :]) & (cj[:, None] == cj[None, :])).astype(f) * sc
    wqT = np.zeros((128, 2, 128), f)
    g128 = np.zeros((128, 2), f)
    for pair in range(2):
        for hh in range(2):
            hd = 2 * pair + hh
            wqT[64 * hh:64 * hh + 64, pair, :] = np.exp(log_gamma[hd] * (idx + 1.0))[None, :]
            g128[64 * hh:64 * hh + 64, pair] = np.exp(log_gamma[hd] * 128.0)
    wk = np.exp(log_gamma[None, :] * (127.0 - idx)[:, None]).astype(f) * f(0.125)
    inv_r = (10000.0 ** (-np.arange(32, dtype=f) / 32)).astype(f)
    inv_m = (10000.0 ** (-np.arange(16, dtype=f) / 16)).astype(f)
    inv48 = np.concatenate([inv_r, inv_m])[None, :].repeat(128, 0)
    parts = [np.eye(128, dtype=f), maskR.reshape(128, 512), tri, ones, trimo, M1s, M2s,
             wqT.reshape(128, 256), wk, g128, inv48,
             (np.arange(128)[:, None] // 32 == np.arange(4)[None, :]).astype(f)]
    nwT = norm_w.reshape(L, 8, 128).transpose(2, 0, 1).reshape(128, L * 8)
    qnT = mla_q_norm.reshape(L, 2, 128).transpose(2, 0, 1).reshape(128, L * 2)
    kvT = mla_kv_norm.reshape(L, 1, 128).transpose(2, 0, 1).reshape(128, L)
    gn = np.broadcast_to(gla_norm.reshape(1, L * 64), (128, L * 64))
    bg = np.broadcast_to(b_g2.reshape(1, L * 128), (128, L * 128))
    parts += [nwT, qnT, kvT, gn, bg]
    return np.ascontiguousarray(np.concatenate([p.astype(f) for p in parts], axis=1))


class Res:
    __slots__ = ("name", "w", "r", "rd", "excl")

    def __init__(self, name, excl=False):
        self.name = name
        self.excl = excl
        self.w = None
        self.r = {}
        self.rd = []


class Op:
    __slots__ = ("eng", "fn", "waits", "flag", "sem", "val", "dma")

    def __init__(self, eng, fn, dma):
        self.eng = eng
        self.fn = fn
        self.waits = []
        self.flag = False
        self.sem = None
        self.val = 0
        self.dma = dma


ENGS = ("pe", "act", "dve", "pool", "sp")


class Sched:
    def __init__(self):
        self.ops = {e: [] for e in ENGS}
        self.all = []

    def add(self, eng, fn, reads=(), writes=(), dma=False):
        op = Op(eng, fn, dma)
        deps = []
        for r in reads:
            if r.w is not None:
                deps.append((r.w, "raw"))
            if r.excl:
                for en, o in r.r.items():
                    if en != eng:
                        deps.append((o, "war"))
        for r in writes:
            if r.w is not None:
                deps.append((r.w, "waw"))
            for o in r.r.values():
                deps.append((o, "war"))
            for o in r.rd:
                deps.append((o, "war"))
        seen = set()
        for d, kind in deps:
            if d is op or id(d) in seen:
                continue
            if d.eng == eng and not d.dma and not dma:
                if eng == "pe":
                    continue
                if kind != "raw":
                    continue
            seen.add(id(d))
            op.waits.append(d)
            d.flag = True
        for r in reads:
            if dma:
                r.rd.append(op)
            else:
                r.r[eng] = op
        for r in writes:
            r.w = op
            r.r = {}
            r.rd = []
        self.ops[eng].append(op)
        self.all.append(op)
        return op

    def finalize(self, sems, dsems):
        cnt = {e: 0 for e in ENGS}
        nd = len(dsems)
        di = 0
        last_on = [None] * nd
        for op in self.all:
            if op.dma:
                j = di % nd
                op.sem = dsems[j]
                op.val = 16 * (di // nd + 1)
                op.flag = True
                if last_on[j] is not None:
                    op.waits.append(last_on[j])
                last_on[j] = op
                di += 1
            elif op.flag:
                cnt[op.eng] += 1
                op.sem = sems[op.eng]
                op.val = cnt[op.eng]
        return cnt

    def replay(self, eng, e):
        seen = {}
        for op in self.ops[eng]:
            for w in op.waits:
                key = id(w.sem)
                if seen.get(key, 0) >= w.val:
                    continue
                seen[key] = w.val
                e.wait_ge(w.sem, w.val)
            if op.fn is None:
                continue
            ins = op.fn(e)
            if op.flag:
                ins.then_inc(op.sem, 16 if op.dma else 1)


class StopBuild(Exception):
    pass


def build(NSEQ, NT, L):
    import os
    STOP = int(os.environ.get("KSTOP", "99"))
    PIPE = int(os.environ.get("KPIPE", "1"))

    def stage_(n):
        if STOP == n:
            raise StopBuild()
    S_LEN = NT * 128
    nc = bass.Bass("TRN2", target_bir_lowering=False)
    NCONST = NCONST_BASE + L * 8 + L * 2 + L + L * 64 + L * 128
    o_nw = NCONST_BASE
    o_qn = o_nw + L * 8
    o_kv = o_qn + L * 2
    o_gn = o_kv + L
    o_bg = o_gn + L * 64

    def din(name, shape, dt=F32):
        return nc.dram_tensor(name, list(shape), dt, kind="ExternalInput").ap()

    x_d = din("x", [NSEQ, S_LEN, D])
    cT_d = din("cT", [128, 8 * NSEQ])
    pos_d = din("pos", [128, NSEQ * NT], I32)
    consts_d = din("consts", [128, NCONST])
    fnorm_d = din("fnorm", [128, D])
    ada_w_d = din("ada_w", [L, D, 3 * D])
    ada_b_d = din("ada_b", [NSEQ, L * 3 * D])
    w_in_d = din("w_in", [L, D, INC])
    w_uq_d = din("w_uq", [L, 256, 768])
    w_ukv_d = din("w_ukv", [L, 128, 1024])
    w_g2_d = din("w_g2", [L, 16, 128])
    b_g2_d = din("b_g2", [L, 128])
    w_out_d = din("w_out", [L, D, D])
    y_d = nc.dram_tensor("y", [NSEQ, S_LEN, D], F32, kind="ExternalOutput").ap()
    mods_d = nc.dram_tensor("mods", [L, NSEQ, 3 * D], F32, kind="Internal").ap()

    S = Sched()
    with ExitStack() as es:
        def sb(name, shape, dt=F32):
            return es.enter_context(nc.sbuf_tensor("sb_" + name, list(shape), dt))

        def R(name):
            return Res(name)

        cst = sb("cst", [128, NCONST]); r_cst = R("cst")
        fnorm = sb("fnorm", [128, D]); r_fnorm = R("fnorm")
        ident_b = sb("ident_b", [128, 128], BF16); r_ident = R("ident")
        w_in_b = sb("w_in_b", [128, 8, INC], BF16); r_win = R("win")
        w_out_b = sb("w_out_b", [128, 8, D], BF16); r_wout = R("wout")
        w_uq_b = sb("w_uq_b", [128, 2, 768], BF16); r_wuq = R("wuq")
        w_ukv_b = sb("w_ukv_b", [128, 1024], BF16); r_wukv = R("wukv")
        w_g2_b = sb("w_g2_b", [32, 128], BF16); r_wg2 = R("wg2")
        KT = sb("KT", [96, 8, S_LEN], BF16); r_KT = [R("KT%d" % t) for t in range(NT)]
        VA = sb("VA", [128, NT, 8, 65], BF16); r_VA = [R("VA%d" % t) for t in range(NT)]
        stage = sb("stage", [128, 1024]); r_stage = R("stage")
        gate_bc = sb("gate_bc", [128, D], BF16); r_gate = R("gate")
        modc = sb("modc", [128, 16]); r_modc = R("modc")
        gp = sb("gp", [128, 8]); r_gp = R("gp")
        modg = sb("modg", [NSEQ, 128]); r_modg = R("modg")
        r_modsd = [[R("modsd%d_%d" % (l_, g_)) for g_ in range(24)] for l_ in range(L)]
        adab = sb("adab", [NSEQ, 128]); r_adab = R("adab")
        cact = sb("cact", [128, 8 * NSEQ]); r_cact = R("cact")
        ctmp = sb("ctmp", [128, 8 * NSEQ]); r_ctmp = R("ctmp")
        cin = sb("cin", [128, 8 * NSEQ]); r_cin = R("cin")
        pos_i = sb("pos_i", [128, NSEQ * NT], I32); r_posi = R("posi")
        CH = min(2, NT)
        pos_f = sb("pos_f", [128, NT]); r_posf = R("posf")
        ang = sb("ang", [128, CH, 48]); r_ang = R("ang")
        rr = sb("rr", [128, CH, 48]); r_rr = R("rr")
        kk_i = sb("kk_i", [128, CH, 48], I32); r_kki = R("kki")
        kk_f = sb("kk_f", [128, CH, 48]); r_kkf = R("kkf")
        mm_ = sb("mm_", [128, CH, 48]); r_mm = R("mm")
        rc = sb("rc", [128, CH, 48]); r_rc = R("rc")
        Tsin = sb("Tsin", [128, NT, 48]); r_Tsin = R("Tsin")
        Tcos = sb("Tcos", [128, NT, 48]); r_Tcos = R("Tcos")
        xt = [sb("xt%d" % i, [128, D]) for i in range(3)]; r_xt = [R("xt0"), R("xt1"), R("xt2")]
        st_ = sb("st_", [128, 16]); r_st = R("st")
        st_r = sb("st_r", [128, 16]); r_str = R("str")
        st_g = sb("st_g", [128, 16]); r_stg = R("stg")
        st_m = sb("st_m", [128, 16]); r_stm = R("stm")
        xn = sb("xn", [128, D], BF16); r_xn = R("xn")
        hTs = [sb("hT%d" % i, [128, 8, 128], BF16) for i in range(2)]
        r_hTs = [[R("hT%da" % i), R("hT%db" % i)] for i in range(2)]
        qkf = sb("qkf", [128, 512]); r_qkf = R("qkf")
        tA = sb("tA", [128, 256]); r_tA = R("tA")
        tB = sb("tB", [128, 256]); r_tB = R("tB")
        qkrot = sb("qkrot", [128, 8, 64], BF16); r_qkrot = R("qkrot")
        kdec = sb("kdec", [128, 4, 64], BF16); r_kdec = R("kdec")
        qT_m = sb("qT_m", [128, 4, 128], BF16); r_qTm = R("qTm")
        qTd_m = sb("qTd_m", [128, 4, 128], BF16); r_qTdm = R("qTdm")
        kTr = sb("kTr", [128, 2, 128], BF16); r_kTr = R("kTr")
        STm = sb("STm", [128, 4, 128], BF16); r_STm = R("STm")
        vb_r = sb("vb_r", [128, 256], BF16); r_vbr = R("vbr")
        S32 = sb("S32", [128, 2, 64]); r_S32 = R("S32")
        S_b = sb("S_b", [128, 4, 64], BF16); r_Sb = R("Sb")
        glow_b = sb("glow_b", [128, 32], BF16); r_glowb = R("glowb")
        glowT = sb("glowT", [32, 128], BF16); r_glowT = R("glowT")
        lf = sb("lf", [128, 128]); r_lf = R("lf")
        e3s = [sb("e3_%d" % i, [128, 3, 128]) for i in range(2)]; r_e3s = [R("e3_0"), R("e3_1")]
        qk4 = sb("qk4", [128, 4, 128], BF16); r_qk4 = R("qk4")
        kdg = sb("kdg", [128, 128], BF16); r_kdg = R("kdg")
        qpT_m = sb("qpT_m", [128, 4, 128], BF16); r_qpTm = R("qpTm")
        qnT_m = sb("qnT_m", [128, 4, 128], BF16); r_qnTm = R("qnTm")
        kTg = sb("kTg", [128, 2, 128], BF16); r_kTg = R("kTg")
        A1 = sb("A1", [128, 4, 128], BF16); r_A1 = R("A1")
        A2 = sb("A2", [128, 4, 128], BF16); r_A2 = R("A2")
        vb_g = sb("vb_g", [128, 256], BF16); r_vbg = R("vbg")
        Sg32 = sb("Sg32", [128, 64]); r_Sg32 = R("Sg32")
        Sg_b = sb("Sg_b", [128, 64], BF16); r_Sgb = R("Sgb")
        edecs = [sb("edec%d" % i, [128, 1]) for i in range(2)]; r_edecs = [R("edec0"), R("edec1")]
        latn = sb("latn", [128, 384], BF16); r_latn = R("latn")
        latT = sb("latT", [128, 3, 128], BF16); r_latT = R("latT")
        Qb = sb("Qb", [128, 8, 96], BF16); r_Qb = R("Qb")
        Kfull = sb("Kfull", [128, 8, 96], BF16); r_Kfull = R("Kfull")
        kpe = sb("kpe", [128, 32], BF16); r_kpe = R("kpe")
        krf = sb("krf", [128, 32]); r_krf = R("krf")
        QT2 = [sb("QT%d" % i, [96, 8, 128], BF16) for i in range(2)]; r_QT2 = [R("QT0"), R("QT1")]
        PT = [sb("PT%d" % i, [128, 8, 128], BF16) for i in range(2)]; r_PT = [[R("PT0a"), R("PT0b")], [R("PT1a"), R("PT1b")]]
        rs = sb("rs", [128, 8]); r_rs = R("rs")
        eg = sb("eg", [128, 256]); r_eg = R("eg")
        sg = sb("sg", [128, 256]); r_sg = R("sg")
        gA = sb("gA", [128, 256], BF16); r_gA = R("gA")
        gA_g = sb("gA_g", [128, 256], BF16); r_gAg = R("gAg")
        gB = sb("gB", [128, 256], BF16); r_gB = R("gB")
        gB_gs = [sb("gB_g%d" % i, [128, 256], BF16) for i in range(2)]; r_gBgs = [R("gBg0"), R("gBg1")]
        mixed2 = [sb("mixed%d" % i, [128, D], BF16) for i in range(2)]; r_mixed2 = [R("mixed0"), R("mixed1")]
        gBm = [sb("gBm%d" % i, [128, 512], BF16) for i in range(2)]; r_gBm = [R("gBm0"), R("gBm1")]
        gAm = sb("gAm", [128, 512]); r_gAm = R("gAm")
        st_b = sb("st_b", [128, 4]); r_stb = R("stb")
        mixT = sb("mixT", [128, 8, 128], BF16); r_mixT = R("mixT")

        PP = [es.enter_context(nc.psum_tensor("pp%d" % i, [128, 1024], F32)) for i in range(4)]
        r_bank = [[Res("b%d_%d" % (i, j), excl=True) for j in range(2)] for i in range(4)]
        gen_banks = [(0, 0), (0, 1), (2, 0), (2, 1)]
        gstate = {"i": 0, "sc": 0}

        def gbank():
            i, j = gen_banks[gstate["i"] % 4]
            gstate["i"] += 1
            return PP[i][:, j * 512:(j + 1) * 512], r_bank[i][j]

        sems = {e: es.enter_context(nc.semaphore("s_" + e)) for e in ENGS}
        dsems = [es.enter_context(nc.semaphore("d%d" % i)) for i in range(24)]

        def C(name):
            a, b = _off[name]
            return cst[:, a:b]

        def dma(out, in_, reads, writes, eng="sp", **kw):
            return S.add(eng, lambda e: e.dma_start(out=out, in_=in_, **kw), reads, writes, dma=True)

        def mm(out, lhsT, rhs, start, reads, writes):
            return S.add("pe", lambda e: e.matmul(out, lhsT=lhsT, rhs=rhs, start=start, stop=True,
                                                  skip_group_check=True), reads, writes)

        def tr(out, in_, reads, writes):
            return S.add("pe", lambda e: e.transpose(out, in_, ident_b[:]), list(reads) + [r_ident], writes)

        def act(out, in_, func, reads, writes, scale=1.0, bias=0.0, accum=None, eng="act"):
            if accum is None:
                return S.add(eng, lambda e: e.activation(out=out, in_=in_, func=func, scale=scale, bias=bias),
                             reads, writes)
            return S.add(eng, lambda e: e.activation(out=out, in_=in_, func=func, scale=scale, bias=bias,
                                                     accum_out=accum), reads, writes)

        def tt(out, a, b, op, reads, writes, eng="dve"):
            return S.add(eng, lambda e: e.tensor_tensor(out=out, in0=a, in1=b, op=op), reads, writes)

        def ts(out, a, s1, s2, op0, op1, reads, writes, eng="dve"):
            if op1 is None:
                return S.add(eng, lambda e: e.tensor_scalar(out=out, in0=a, scalar1=s1, scalar2=None, op0=op0),
                             reads, writes)
            return S.add(eng, lambda e: e.tensor_scalar(out=out, in0=a, scalar1=s1, scalar2=s2, op0=op0, op1=op1),
                         reads, writes)

        def stt(out, a, scalar, b, op0, op1, reads, writes):
            return S.add("dve", lambda e: e.scalar_tensor_tensor(out=out, in0=a, scalar=scalar, in1=b,
                                                                 op0=op0, op1=op1), reads, writes)

        def cp(out, in_, reads, writes, eng="dve"):
            if eng == "act":
                return act(out, in_, AF.Copy, reads, writes)
            return S.add(eng, lambda e: e.tensor_copy(out=out, in_=in_), reads, writes)

        def memset(ap, val, writes, eng="dve"):
            return S.add(eng, lambda e: e.memset(ap, val), (), writes)

        def rstd_from(ssq_ap, out_ap, n, reads, writes_tmp, tmp_ap):
            act(tmp_ap, ssq_ap, AF.Ln, reads, writes_tmp, scale=1.0 / n, bias=EPS)
            act(out_ap, tmp_ap, AF.Exp, writes_tmp, writes_tmp, scale=-0.5)

        def sigmoid_from(z_ap, z_reads, width):
            act(eg[:, 0:width], z_ap, AF.Exp, z_reads, [r_eg], scale=-1.0)
            act(sg[:, 0:width], eg[:, 0:width], AF.Ln, [r_eg], [r_sg], bias=1.0)
            act(sg[:, 0:width], sg[:, 0:width], AF.Exp, [r_sg], [r_sg], scale=-1.0)

        try:
            dma(cst[:], consts_d[:, :], [], [r_cst])
            dma(fnorm[:], fnorm_d[:, :], [], [r_fnorm])
            dma(cin[:], cT_d[:, :], [], [r_cin])
            dma(pos_i[:], pos_d[:, :], [], [r_posi])
            cp(ident_b[:], C("ident"), [r_cst], [r_ident])
            for buf, rb in ((qT_m, r_qTm), (qTd_m, r_qTdm), (qpT_m, r_qpTm), (qnT_m, r_qnTm), (S_b, r_Sb)):
                memset(buf[:], 0.0, [rb])
            memset(VA[:], 1.0, r_VA)
            memset(w_g2_b[:], 0.0, [r_wg2])
            act(ctmp[:], cin[:], AF.Exp, [r_cin], [r_ctmp], scale=-1.0)
            ts(ctmp[:], ctmp[:], 1.0, None, ALU.add, None, [r_ctmp], [r_ctmp])
            S.add("dve", lambda e: e.reciprocal(out=ctmp[:], in_=ctmp[:]), [r_ctmp], [r_ctmp])
            tt(cact[:], cin[:], ctmp[:], ALU.mult, [r_cin, r_ctmp], [r_cact])
            cact3 = cact[:].rearrange("p (k s) -> p k s", s=NSEQ)
            def ada_group(l, g, bk, rbk):
                st3 = stage[:, 0:1024].rearrange("p (k c) -> p k c", c=128)
                dma(st3, ada_w_d[l, :, g * 128:(g + 1) * 128].rearrange("(k p) c -> p k c", p=128),
                    [], [r_stage])
                dma(adab[:], ada_b_d[:, l * 3 * D + g * 128: l * 3 * D + (g + 1) * 128], [], [r_adab])
                for k in range(8):
                    mm(bk[0:NSEQ, 0:128], cact3[:, k, :], st3[:, k, :], k == 0, [r_cact, r_stage], [rbk])
                tt(modg[:], bk[0:NSEQ, 0:128], adab[:], ALU.add, [rbk, r_adab], [r_modg])
                dma(mods_d[l, :, g * 128:(g + 1) * 128], modg[:], [r_modg], [r_modsd[l][g]])

            for g in range(24):
                bk, rbk = gbank()
                ada_group(0, g, bk, rbk)
            ada_todo = [(l_, g_) for l_ in range(1, L) for g_ in range(24)]

            stage_(1)
            r_y = [[R("y%d_%d" % (s, t)) for t in range(NT)] for s in range(NSEQ)]
            loaded_layer = {"l": None}
            xslot = {"i": 0}

            def rope_tables(s):
                cp(pos_f[:], pos_i[:, s * NT:(s + 1) * NT], [r_posi], [r_posf])
                for c0 in range(0, NT, CH):
                    tt(ang[:], pos_f[:, c0:c0 + CH].unsqueeze(2).to_broadcast([128, CH, 48]),
                       C("inv48").unsqueeze(1).to_broadcast([128, CH, 48]), ALU.mult, [r_posf, r_cst], [r_ang])
                    ts(rr[:], ang[:], 1.0 / TWO_PI, None, ALU.mult, None, [r_ang], [r_rr])
                    cp(kk_i[:], rr[:], [r_rr], [r_kki])
                    cp(kk_f[:], kk_i[:], [r_kki], [r_kkf])
                    stt(rr[:], kk_f[:], -C1, ang[:], ALU.mult, ALU.add, [r_kkf, r_ang], [r_rr])
                    stt(rr[:], kk_f[:], -C2, rr[:], ALU.mult, ALU.add, [r_kkf, r_rr], [r_rr])

                    def wrap(buf, rb):
                        ts(mm_[:], buf[:], PI, None, ALU.is_gt, None, [rb], [r_mm])
                        stt(buf[:], mm_[:], -TWO_PI, buf[:], ALU.mult, ALU.add, [r_mm, rb], [rb])
                        ts(mm_[:], buf[:], -PI, None, ALU.is_lt, None, [rb], [r_mm])
                        stt(buf[:], mm_[:], TWO_PI, buf[:], ALU.mult, ALU.add, [r_mm, rb], [rb])
                        ts(buf[:], buf[:], -PI_LO, PI_LO, ALU.max, ALU.min, [rb], [rb])

                    wrap(rr, r_rr)
                    ts(rc[:], rr[:], PI / 2, None, ALU.add, None, [r_rr], [r_rc])
                    wrap(rc, r_rc)
                    act(Tsin[:, c0:c0 + CH, :], rr[:], AF.Sin, [r_rr], [r_Tsin])
                    act(Tcos[:, c0:c0 + CH, :], rc[:], AF.Sin, [r_rc], [r_Tcos])

            for l in range(L):
                while ada_todo and ada_todo[0][0] <= l:
                    l_, g_ = ada_todo.pop(0)
                    bk, rbk = gbank()
                    ada_group(l_, g_, bk, rbk)
                for s in range(NSEQ):
                    rope_tables(s)
                    last_layer = (l == L - 1)
                    dma(modc[:, 0:8], mods_d[l, s, 0:D].rearrange("(k p) -> p k", p=128), r_modsd[l][0:8], [r_modc],
                        allow_slow_non_contiguous=True)
                    dma(modc[:, 8:16], mods_d[l, s, D:2 * D].rearrange("(k p) -> p k", p=128), r_modsd[l][8:16], [r_modc],
                        allow_slow_non_contiguous=True)
                    dma(stage[0:1, 0:D], mods_d[l, s:s + 1, 2 * D:3 * D], r_modsd[l][16:24], [r_stage])
                    stt(gp[:], modc[:, 8:16], 1.0, cst[:, o_nw + l * 8:o_nw + (l + 1) * 8], ALU.add, ALU.mult,
                        [r_modc, r_cst], [r_gp])
                    for hf in range(2):
                        bk, rbk = gbank()
                        mm(bk, C("ones")[0:1, :], stage[0:1, hf * 512:(hf + 1) * 512], True, [r_cst, r_stage], [rbk])
                        cp(gate_bc[:, hf * 512:(hf + 1) * 512], bk, [rbk], [r_gate], eng="act")
                    if loaded_layer["l"] != l:
                        loaded_layer["l"] = l
                        for k in range(8):
                            for (a, b) in ((0, 1368), (1368, INC)):
                                dma(w_in_b[:, k, a:b], w_in_d[l, k * 128:(k + 1) * 128, a:b], [], [r_win], eng="pool")
                            dma(w_out_b[:, k, :], w_out_d[l, k * 128:(k + 1) * 128, :], [], [r_wout], eng="pool")
                        dma(w_g2_b[0:16, :], w_g2_d[l, :, :], [], [r_wg2], eng="pool")
                        for k in range(2):
                            dma(stage[:, 0:768], w_uq_d[l, k * 128:(k + 1) * 128, :], [], [r_stage])
                            ts(w_uq_b[:, k, :], stage[:, 0:768], cst[:, o_qn + l * 2 + k:o_qn + l * 2 + k + 1], None,
                               ALU.mult, None, [r_stage, r_cst], [r_wuq])
                        dma(stage[:, 0:1024], w_ukv_d[l, :, :], [], [r_stage])
                        ts(w_ukv_b[:], stage[:, 0:1024], cst[:, o_kv + l:o_kv + l + 1], None, ALU.mult, None,
                           [r_stage, r_cst], [r_wukv])
                    memset(S32[:], 0.0, [r_S32])
                    for sl in range(4):
                        hh = sl % 2
                        memset(S_b[64 * hh:64 * hh + 64, sl, :], 0.0, [r_Sb])
                    memset(Sg32[:], 0.0, [r_Sg32])
                    memset(Sg_b[:], 0.0, [r_Sgb])

                    stage_(3)
                    bank_R = (PP[0][:, 0:512], r_bank[0][0])
                    bank_G = (PP[0][:, 512:1024], r_bank[0][1])
                    bank_M = [(PP[2][:, 0:512], r_bank[2][0]), (PP[2][:, 512:1024], r_bank[2][1])]
                    mstate = {"i": 0}

                    def nextM():
                        return bank_M[0]

                    def inproj_to(bk_, rbk_, c0, c1, t):
                        hT = hTs[t % 2]
                        for k in range(8):
                            mm(bk_[:, 0:c1 - c0], hT[:, k, :], w_in_b[:, k, c0:c1], k == 0, r_hTs[t % 2] + [r_win], [rbk_])

                    def norm_gate(bo, rbo, stx, rstx, gAx, rgAx, gBx, rgBx, out_ap, rout):
                        act(eg[:, 0:256], bo[:, 0:256], AF.Square, [rbo], [r_eg])
                        S.add("dve", lambda e: e.tensor_reduce(out=stx[:, 4:8], in_=eg[:, 0:256].rearrange("p (h d) -> p h d", d=64),
                                                                axis=AX.X, op=ALU.add), [r_eg], [rstx])
                        rstd_from(stx[:, 4:8], stx[:, 8:12], 64, [rstx], [rstx], stx[:, 12:16])
                        tt(gAx[:, 0:256].rearrange("p (h d) -> p h d", d=64), bo[:, 0:256].rearrange("p (h d) -> p h d", d=64),
                           stx[:, 8:12].unsqueeze(2).to_broadcast([128, 4, 64]), ALU.mult, [rbo, rstx], [rgAx])
                        tt(out_ap, gAx[:, 0:256], gBx[:, 0:256], ALU.mult, [rgAx, rgBx], [rout], eng="pool")

                    def hT_stage(t, X, rX):
                        hT = hTs[t % 2]
                        rh = r_hTs[t % 2]
                        src = x_d if l == 0 else y_d
                        dma(X[:], src[s, t * 128:(t + 1) * 128, :], [r_y[s][t]] if l > 0 else [], [rX])
                        act(xn[:], X[:], AF.Square, [rX], [r_xn, r_st], accum=st_[:, 0:1])
                        rstd_from(st_[:, 0:1], st_[:, 2:3], D, [r_st], [r_st], st_[:, 1:2])
                        S.add("pool", lambda e: e.tensor_scalar(out=xn[:], in0=X[:], scalar1=st_[:, 2:3], scalar2=1.0,
                                                                 op0=ALU.mult, op1=ALU.mult), [rX, r_st], [r_xn])
                        yield
                        bA, rbA = bank_R
                        bAb = bA.bitcast(BF16)
                        for k in range(8):
                            tr(bAb[:, k * 128:(k + 1) * 128], xn[:, k * 128:(k + 1) * 128], [r_xn], [rbA])
                        for k in range(4):
                            ts(hT[:, k, :], bAb[:, k * 128:(k + 1) * 128], gp[:, k:k + 1], modc[:, k:k + 1],
                               ALU.mult, ALU.add, [rbA, r_gp, r_modc], [rh[0]])
                        for k in range(4, 8):
                            act(hT[:, k, :], bAb[:, k * 128:(k + 1) * 128], AF.Identity, [rbA, r_gp, r_modc], [rh[1]],
                                scale=gp[:, k:k + 1], bias=modc[:, k:k + 1])

                    def ret_stream(t):
                        bR, rbR = bank_R
                        mx, rmx = mixed2[t % 2], r_mixed2[t % 2]
                        inproj_to(bR, rbR, 0, 512, t)
                        cp(qkf[:], bR, [rbR], [r_qkf], eng="act")
                        yield
                        q3 = qkf[:].rearrange("p (h d) -> p h d", d=64)
                        x1 = q3[:, :, 0:32]
                        x2 = q3[:, :, 32:64]
                        cosb = Tcos[:, t, 0:32].unsqueeze(1).to_broadcast([128, 8, 32])
                        sinb = Tsin[:, t, 0:32].unsqueeze(1).to_broadcast([128, 8, 32])
                        tA3 = tA[:].rearrange("p (h d) -> p h d", d=32)
                        tB3 = tB[:].rearrange("p (h d) -> p h d", d=32)
                        tt(tA3, x1, cosb, ALU.mult, [r_qkf, r_Tcos], [r_tA], eng="pool")
                        tt(tB3, x2, sinb, ALU.mult, [r_qkf, r_Tsin], [r_tB], eng="pool")
                        tt(qkrot[:, :, 0:32], tA3, tB3, ALU.subtract, [r_tA, r_tB], [r_qkrot], eng="pool")
                        tt(tA3, x2, cosb, ALU.mult, [r_qkf, r_Tcos], [r_tA], eng="pool")
                        tt(tB3, x1, sinb, ALU.mult, [r_qkf, r_Tsin], [r_tB], eng="pool")
                        tt(qkrot[:, :, 32:64], tA3, tB3, ALU.add, [r_tA, r_tB], [r_qkrot], eng="pool")
                        tt(kdec[:], qkrot[:, 4:8, :], C("wk").unsqueeze(2).to_broadcast([128, 4, 64]), ALU.mult,
                           [r_qkrot, r_cst], [r_kdec], eng="pool")
                        yield
                        qkflat = qkrot[:].rearrange("p h d -> p (h d)")
                        bRb = bR.bitcast(BF16)
                        for j in range(4):
                            tr(bRb[:, j * 128:(j + 1) * 128], qkflat[:, j * 128:(j + 1) * 128], [r_qkrot], [rbR])
                        for pair in range(2):
                            for hh in range(2):
                                cp(qT_m[64 * hh:64 * hh + 64, 2 * pair + hh, :],
                                   bRb[64 * hh:64 * hh + 64, pair * 128:(pair + 1) * 128], [rbR], [r_qTm], eng="act")
                        cp(kTr[:], bRb[:, 256:512].rearrange("p (a b) -> p a b", b=128), [rbR], [r_kTr])
                        wq3 = C("wqT").rearrange("p (a b) -> p a b", b=128)
                        for pair in range(2):
                            for hh in range(2):
                                tt(qTd_m[64 * hh:64 * hh + 64, 2 * pair + hh, :], qT_m[64 * hh:64 * hh + 64, 2 * pair + hh, :],
                                   wq3[64 * hh:64 * hh + 64, pair, :], ALU.mult, [r_qTm, r_cst], [r_qTdm], eng="pool")
                        yield
                        for hd in range(4):
                            mm(bR[:, hd * 128:(hd + 1) * 128], kTr[:, hd // 2, :], qT_m[:, hd, :], hd == 0,
                               [r_kTr, r_qTm], [rbR])
                        tt(STm[:].rearrange("p a b -> p (a b)"), bR, C("maskR"), ALU.mult, [rbR, r_cst], [r_STm])
                        yield
                        inproj_to(bR, rbR, 512, 1024, t)
                        cp(vb_r[:], bR[:, 0:256], [rbR], [r_vbr], eng="act")
                        sigmoid_from(bR[:, 256:512], [rbR], 256)
                        tt(gB[:, 0:256], bR[:, 256:512], sg[:, 0:256], ALU.mult, [rbR, r_sg], [r_gB])
                        yield
                        for hd in range(4):
                            mm(bR[:, hd * 64:(hd + 1) * 64], STm[:, hd, :], vb_r[:, hd * 64:(hd + 1) * 64], hd == 0,
                               [r_STm, r_vbr], [rbR])
                            mm(bR[:, hd * 64:(hd + 1) * 64], qTd_m[:, hd, :], S_b[:, hd, :], False, [r_qTdm, r_Sb], [rbR])
                        norm_gate(bR, rbR, st_r, r_str, gA, r_gA, gB, r_gB, mx[:, 0:256], rmx)
                        yield
                        kd2 = kdec[:].rearrange("p h d -> p (h d)")
                        for pair in range(2):
                            mm(bR[:, pair * 128:(pair + 1) * 128], kd2[:, pair * 128:(pair + 1) * 128],
                               vb_r[:, pair * 128:(pair + 1) * 128], pair == 0, [r_kdec, r_vbr], [rbR])
                        for pair in range(2):
                            for hh in range(2):
                                p0 = 64 * hh
                                stt(S32[p0:p0 + 64, pair, :], S32[p0:p0 + 64, pair, :],
                                    cst[p0:p0 + 64, _off["g128"][0] + pair:_off["g128"][0] + pair + 1],
                                    bR[p0:p0 + 64, pair * 128 + hh * 64: pair * 128 + hh * 64 + 64],
                                    ALU.mult, ALU.add, [r_S32, r_cst, rbR], [r_S32])
                                cp(S_b[p0:p0 + 64, 2 * pair + hh, :], S32[p0:p0 + 64, pair, :], [r_S32], [r_Sb], eng="pool")

                    def gla_gate(t):
                        bG, rbG = bank_G
                        bGb = bG.bitcast(BF16)
                        e3, r_e3 = e3s[t % 2], r_e3s[t % 2]
                        edec, r_edec = edecs[t % 2], r_edecs[t % 2]
                        gB_g, r_gBg = gB_gs[t % 2], r_gBgs[t % 2]
                        inproj_to(bG, rbG, 2464, 2736, t)
                        cp(glow_b[:], bG[:, 0:32], [rbG], [r_glowb], eng="act")
                        sigmoid_from(bG[:, 16:272], [rbG], 256)
                        tt(gB_g[:, 0:256], bG[:, 16:272], sg[:, 0:256], ALU.mult, [rbG, r_sg], [r_gBg])
                        tt(gB_g[:, 0:256].rearrange("p (h d) -> p h d", d=64), gB_g[:, 0:256].rearrange("p (h d) -> p h d", d=64),
                           cst[:, o_gn + l * 64:o_gn + (l + 1) * 64].unsqueeze(1).to_broadcast([128, 4, 64]), ALU.mult,
                           [r_gBg, r_cst], [r_gBg], eng="pool")
                        yield
                        tr(bGb[0:32, 0:128], glow_b[:], [r_glowb], [rbG])
                        cp(glowT[:], bGb[0:32, 0:128], [rbG], [r_glowT])
                        yield
                        mm(bG[:, 0:128], glowT[:], w_g2_b[:], True, [r_glowT, r_wg2], [rbG])
                        tt(lf[:], bG[:, 0:128], cst[:, o_bg + l * 128:o_bg + (l + 1) * 128], ALU.add, [rbG, r_cst], [r_lf])
                        act(lf[:], lf[:], AF.Exp, [r_lf], [r_lf], scale=-1.0)
                        act(lf[:], lf[:], AF.Ln, [r_lf], [r_lf], bias=1.0)
                        yield
                        mm(bG[:, 0:128], C("tri"), lf[:], True, [r_cst, r_lf], [rbG])
                        mm(bG[:, 128:256], C("trimo"), lf[:], False, [r_cst, r_lf], [rbG])
                        mm(bG[:, 256:258], lf[:], C("ones")[:, 0:2], False, [r_cst, r_lf], [rbG])
                        act(e3[:, 0, :], bG[:, 0:128], AF.Exp, [rbG], [r_e3], scale=-1.0 / 16)
                        act(e3[:, 1, :], bG[:, 0:128], AF.Exp, [rbG], [r_e3], scale=1.0 / 16)
                        act(e3[:, 2, :], bG[:, 128:256], AF.Exp, [rbG], [r_e3], scale=1.0 / 16,
                            bias=float(math.log(32 ** -0.5)))
                        act(edec[:], bG[:, 256:257], AF.Exp, [rbG], [r_edec], scale=-1.0 / 16)

                    def gla_stream(t):
                        bG, rbG = bank_G
                        mx, rmx = mixed2[t % 2], r_mixed2[t % 2]
                        bGb = bG.bitcast(BF16)
                        hm0 = _off["hm"][0]
                        e3, r_e3 = e3s[t % 2], r_e3s[t % 2]
                        edec, r_edec = edecs[t % 2], r_edecs[t % 2]
                        gB_g, r_gBg = gB_gs[t % 2], r_gBgs[t % 2]
                        inproj_to(bG, rbG, 1952, 2464, t)
                        cp(vb_g[:], bG[:, 256:512], [rbG], [r_vbg], eng="act")
                        tt(qk4[:, 0, :], bG[:, 0:128], e3[:, 0, :], ALU.mult, [rbG, r_e3], [r_qk4])
                        tt(qk4[:, 1, :], bG[:, 0:128], e3[:, 1, :], ALU.mult, [rbG, r_e3], [r_qk4])
                        tt(qk4[:, 2, :], bG[:, 128:256], e3[:, 1, :], ALU.mult, [rbG, r_e3], [r_qk4])
                        tt(qk4[:, 3, :], bG[:, 128:256], e3[:, 0, :], ALU.mult, [rbG, r_e3], [r_qk4])
                        tt(kdg[:], bG[:, 128:256], e3[:, 2, :], ALU.mult, [rbG, r_e3], [r_kdg])
                        yield
                        for j in range(4):
                            tr(bGb[:, j * 128:(j + 1) * 128], qk4[:, j, :], [r_qk4], [rbG])
                        for hd in range(4):
                            ts(qpT_m[:, hd, :], bGb[:, 0:128], cst[:, hm0 + hd:hm0 + hd + 1], None, ALU.mult, None,
                               [rbG, r_cst], [r_qpTm])
                            ts(qnT_m[:, hd, :], bGb[:, 128:256], cst[:, hm0 + hd:hm0 + hd + 1], None, ALU.mult, None,
                               [rbG, r_cst], [r_qnTm])
                        cp(kTg[:], bGb[:, 256:512].rearrange("p (a b) -> p a b", b=128), [rbG], [r_kTg])
                        yield
                        m1b = C("M1s").unsqueeze(1).to_broadcast([128, 4, 128])
                        m2b = C("M2s").unsqueeze(1).to_broadcast([128, 4, 128])
                        for hd in range(4):
                            mm(bG[:, hd * 128:(hd + 1) * 128], kTg[:, 0, :], qpT_m[:, hd, :], hd == 0, [r_kTg, r_qpTm], [rbG])
                        tt(A1[:], bG.rearrange("p (a b) -> p a b", b=128), m1b, ALU.mult, [rbG, r_cst], [r_A1])
                        yield
                        for hd in range(4):
                            mm(bG[:, hd * 128:(hd + 1) * 128], kTg[:, 1, :], qnT_m[:, hd, :], hd == 0, [r_kTg, r_qnTm], [rbG])
                        tt(A2[:], bG.rearrange("p (a b) -> p a b", b=128), m2b, ALU.mult, [rbG, r_cst], [r_A2])
                        yield
                        for hd in range(4):
                            o_ = bG[:, hd * 64:(hd + 1) * 64]
                            v_ = vb_g[:, hd * 64:(hd + 1) * 64]
                            mm(o_, A1[:, hd, :], v_, hd == 0, [r_A1, r_vbg], [rbG])
                            mm(o_, A2[:, hd, :], v_, False, [r_A2, r_vbg], [rbG])
                            mm(o_, qpT_m[:, hd, :], Sg_b[:], False, [r_qpTm, r_Sgb], [rbG])
                        norm_gate(bG, rbG, st_g, r_stg, gA_g, r_gAg, gB_g, r_gBg, mx[:, 768:1024], rmx)
                        yield
                        mm(bG[:, 0:256], kdg[:], vb_g[:], True, [r_kdg, r_vbg], [rbG])
                        ts(Sg32[:], Sg32[:], edec[:, 0:1], None, ALU.mult, None, [r_Sg32, r_edec], [r_Sg32])
                        for hd in range(4):
                            stt(Sg32[:], bG[:, hd * 64:(hd + 1) * 64], cst[:, hm0 + hd:hm0 + hd + 1], Sg32[:],
                                ALU.mult, ALU.add, [r_Sg32, r_cst, rbG], [r_Sg32])
                        cp(Sg_b[:], Sg32[:], [r_Sg32], [r_Sgb], eng="pool")

                    def mla_stream(t):
                        par = t % 2
                        g3, rg3 = nextM()
                        inproj_to(g3, rg3, 1440, 1952, t)
                        for hz in range(2):
                            sigmoid_from(g3[:, hz * 256:(hz + 1) * 256], [rg3], 256)
                            tt(gBm[par][:, hz * 256:(hz + 1) * 256], g3[:, hz * 256:(hz + 1) * 256], sg[:, 0:256], ALU.mult,
                               [rg3, r_sg], [r_gBm[par]])
                        yield
                        g2, rg2 = nextM()
                        inproj_to(g2, rg2, 1024, 1440, t)
                        act(latn[:, 0:256], g2[:, 0:256], AF.Square, [rg2], [r_latn, r_stm], accum=st_m[:, 4:5])
                        act(latn[:, 256:384], g2[:, 256:384], AF.Square, [rg2], [r_latn, r_stm], accum=st_m[:, 5:6])
                        cp(krf[:], g2[:, 384:416], [rg2], [r_krf], eng="act")
                        rstd_from(st_m[:, 4:5], st_m[:, 8:9], 256, [r_stm], [r_stm], st_m[:, 12:13])
                        rstd_from(st_m[:, 5:6], st_m[:, 9:10], 128, [r_stm], [r_stm], st_m[:, 13:14])
                        ts(latn[:, 0:256], g2[:, 0:256], st_m[:, 8:9], None, ALU.mult, None, [rg2, r_stm], [r_latn])
                        ts(latn[:, 256:384], g2[:, 256:384], st_m[:, 9:10], None, ALU.mult, None, [rg2, r_stm], [r_latn])
                        cm = Tcos[:, t, 32:48]
                        sm = Tsin[:, t, 32:48]
                        tt(tA[:, 0:16], krf[:, 0:16], cm, ALU.mult, [r_krf, r_Tcos], [r_tA], eng="pool")
                        tt(tB[:, 0:16], krf[:, 16:32], sm, ALU.mult, [r_krf, r_Tsin], [r_tB], eng="pool")
                        tt(kpe[:, 0:16], tA[:, 0:16], tB[:, 0:16], ALU.subtract, [r_tA, r_tB], [r_kpe], eng="pool")
                        tt(tA[:, 0:16], krf[:, 16:32], cm, ALU.mult, [r_krf, r_Tcos], [r_tA], eng="pool")
                        tt(tB[:, 0:16], krf[:, 0:16], sm, ALU.mult, [r_krf, r_Tsin], [r_tB], eng="pool")
                        tt(kpe[:, 16:32], tA[:, 0:16], tB[:, 0:16], ALU.add, [r_tA, r_tB], [r_kpe], eng="pool")
                        cp(Kfull[:, :, 64:96], kpe[:].unsqueeze(1).to_broadcast([128, 8, 32]), [r_kpe], [r_Kfull], eng="pool")
                        yield
                        bk, rbk = nextM()
                        bkb = bk.bitcast(BF16)
                        for j in range(3):
                            tr(bkb[:, j * 128:(j + 1) * 128], latn[:, j * 128:(j + 1) * 128], [r_latn], [rbk])
                        cp(latT[:], bkb[:, 0:384].rearrange("p (a b) -> p a b", b=128), [rbk], [r_latT])
                        yield
                        for hf in range(2):
                            bq, rbq = nextM()
                            for k in range(2):
                                mm(bq[:, 0:384], latT[:, k, :], w_uq_b[:, k, hf * 384:(hf + 1) * 384], k == 0,
                                   [r_latT, r_wuq], [rbq])
                            q3 = bq[:, 0:384].rearrange("p (h d) -> p h d", d=96)
                            Qh = Qb[:, hf * 4:(hf + 1) * 4, :]
                            cp(Qh[:, :, 0:64], q3[:, :, 0:64], [rbq], [r_Qb], eng="act")
                            cm4 = cm.unsqueeze(1).to_broadcast([128, 4, 16])
                            sm4 = sm.unsqueeze(1).to_broadcast([128, 4, 16])
                            tA4 = tA[:, 0:64].rearrange("p (h d) -> p h d", d=16)
                            tB4 = tB[:, 0:64].rearrange("p (h d) -> p h d", d=16)
                            tt(tA4, q3[:, :, 64:80], cm4, ALU.mult, [rbq, r_Tcos], [r_tA])
                            tt(tB4, q3[:, :, 80:96], sm4, ALU.mult, [rbq, r_Tsin], [r_tB])
                            tt(Qh[:, :, 64:80], tA4, tB4, ALU.subtract, [r_tA, r_tB], [r_Qb])
                            tt(tA4, q3[:, :, 80:96], cm4, ALU.mult, [rbq, r_Tcos], [r_tA])
                            tt(tB4, q3[:, :, 64:80], sm4, ALU.mult, [rbq, r_Tsin], [r_tB])
                            tt(Qh[:, :, 80:96], tA4, tB4, ALU.add, [r_tA, r_tB], [r_Qb])
                        yield
                        bk, rbk = nextM()
                        bkb = bk.bitcast(BF16)
                        for hd in range(8):
                            tr(bkb[0:96, hd * 128:(hd + 1) * 128], Qb[:, hd, :], [r_Qb], [rbk])
                        cp(QT2[par][:], bkb[0:96, :].rearrange("p (a b) -> p a b", b=128), [rbk], [r_QT2[par]])
                        yield
                        for hf in range(2):
                            bq, rbq = nextM()
                            mm(bq, latT[:, 2, :], w_ukv_b[:, hf * 512:(hf + 1) * 512], True, [r_latT, r_wukv], [rbq])
                            kv3 = bq.rearrange("p (h d) -> p h d", d=128)
                            cp(Kfull[:, hf * 4:(hf + 1) * 4, 0:64], kv3[:, :, 0:64], [rbq], [r_Kfull], eng="act")
                            cp(VA[:, t, hf * 4:(hf + 1) * 4, 0:64], kv3[:, :, 64:128], [rbq], [r_VA[t]], eng="act")
                        yield
                        bk, rbk = nextM()
                        bkb = bk.bitcast(BF16)
                        for hd in range(8):
                            tr(bkb[0:96, hd * 128:(hd + 1) * 128], Kfull[:, hd, :], [r_Kfull], [rbk])
                        cp(KT[:, :, t * 128:(t + 1) * 128], bkb[0:96, :].rearrange("p (a b) -> p a b", b=128), [rbk], [r_KT[t]],
                           eng="act")
                        if ada_todo:
                            yield
                            l_, g_ = ada_todo.pop(0)
                            bk, rbk = nextM()
                            ada_group(l_, g_, bk, rbk)

                    def phaseB(t, X, rX):
                        par = t % 2
                        QTt, rQTt = QT2[par], r_QT2[par]
                        mx, rmx = mixed2[par], r_mixed2[par]
                        sc_scale = float(96 ** -0.5)
                        acc = PP[1]
                        racc = r_bank[1]
                        scp = PP[3]
                        rsc = r_bank[3]
                        rot = [(scp[:, 0:512], rsc[0]), (scp[:, 512:1024], rsc[1]), bank_M[1]]

                        def qk_exp(j):
                            kt, hf = j // 2, j % 2
                            sb_, rsb_ = rot[j % 3]
                            P_ = PT[kt % 2]
                            rP = r_PT[kt % 2]
                            for i4 in range(4):
                                hd = 4 * hf + i4
                                mm(sb_[:, i4 * 128:(i4 + 1) * 128], KT[:, hd, kt * 128:(kt + 1) * 128], QTt[:, hd, :],
                                   i4 == 0, [r_KT[kt], rQTt], [rsb_])
                            act(P_[:, 4 * hf:4 * hf + 4, :].rearrange("p a b -> p (a b)"), sb_, AF.Exp, [rsb_], [rP[hf]],
                                scale=sc_scale)
                            if kt == t:
                                memset(P_[64:128, 4 * hf:4 * hf + 4, 0:64], 0.0, [rP[hf]], eng="pool")

                        def pv(j):
                            kt, hf = j // 2, j % 2
                            P_ = PT[kt % 2]
                            rP = r_PT[kt % 2]
                            for i4 in range(4):
                                hd = 4 * hf + i4
                                o_ = acc[:, hf * 512 + i4 * 65: hf * 512 + i4 * 65 + 65]
                                mm(o_, P_[:, hd, :], VA[:, kt, hd, :], (kt == 0 and i4 == 0), [rP[hf], r_VA[kt]], [racc[hf]])

                        nj = 2 * (t + 1)
                        qk_exp(0)
                        qk_exp(1)
                        for j in range(nj):
                            if j + 2 < nj:
                                qk_exp(j + 2)
                            pv(j)
                            if j % 2 == 1:
                                yield
                        for hf in range(2):
                            a3 = acc[:, hf * 512: hf * 512 + 260].rearrange("p (h d) -> p h d", d=65)
                            S.add("dve", lambda e, a3=a3, hf=hf: e.reciprocal(out=rs[:, hf * 4:(hf + 1) * 4].unsqueeze(2),
                                                                             in_=a3[:, :, 64:65]), [racc[hf]], [r_rs])
                            tt(gAm[:, hf * 256:(hf + 1) * 256].rearrange("p (h d) -> p h d", d=64), a3[:, :, 0:64],
                               rs[:, hf * 4:(hf + 1) * 4].unsqueeze(2).to_broadcast([128, 4, 64]), ALU.mult,
                               [racc[hf], r_rs], [r_gAm])
                        tt(mx[:, 256:768], gAm[:], gBm[par][:], ALU.mult, [r_gAm, r_gBm[par]], [rmx], eng="pool")
                        yield
                        stage_(8)
                        bk = scp[:, 0:512]
                        rbk = rsc[0]
                        bkb = bk.bitcast(BF16)
                        for k in range(8):
                            tr(bkb[:, k * 128:(k + 1) * 128], mx[:, k * 128:(k + 1) * 128], [rmx], [rbk])
                        cp(mixT[:], bkb.rearrange("p (a b) -> p a b", b=128), [rbk], [r_mixT], eng="act")
                        yield
                        for hf in range(2):
                            bo = scp[:, (1 - hf) * 512:(2 - hf) * 512]
                            rbo = rsc[1 - hf]
                            for k in range(8):
                                mm(bo, mixT[:, k, :], w_out_b[:, k, hf * 512:(hf + 1) * 512], k == 0, [r_mixT, r_wout], [rbo])
                            tt(gAm[:], bo, gate_bc[:, hf * 512:(hf + 1) * 512], ALU.mult, [rbo, r_gate], [r_gAm])
                            tt(X[:, hf * 512:(hf + 1) * 512], X[:, hf * 512:(hf + 1) * 512], gAm[:], ALU.add,
                               [rX, r_gAm], [rX], eng="pool")
                            yield
                        if last_layer:
                            act(mixT[:].rearrange("p a b -> p (a b)"), X[:], AF.Square, [rX], [r_mixT, r_stb], accum=st_b[:, 0:1])
                            rstd_from(st_b[:, 0:1], st_b[:, 2:3], D, [r_stb], [r_stb], st_b[:, 1:2])
                            stt(X[:], X[:], st_b[:, 2:3], fnorm[:], ALU.mult, ALU.mult, [rX, r_stb, r_fnorm], [rX])
                        dma(y_d[s, t * 128:(t + 1) * 128, :], X[:], [rX], [r_y[s][t]])

                    def drain(g):
                        for _ in g:
                            pass

                    def interleave(ga, gb, na, nb):
                        ia = ib = 0
                        da = db = False
                        while not (da and db):
                            if db or (not da and ia * nb <= ib * na):
                                try:
                                    next(ga)
                                    ia += 1
                                except StopIteration:
                                    da = True
                            else:
                                try:
                                    next(gb)
                                    ib += 1
                                except StopIteration:
                                    db = True

                    def round_robin(gens, weights):
                        active = list(range(len(gens)))
                        while active:
                            for i in list(active):
                                for _ in range(weights[i]):
                                    try:
                                        next(gens[i])
                                    except StopIteration:
                                        active.remove(i)
                                        break

                    Xs = {}

                    def alloc_x(t):
                        xi = xslot["i"] % 3
                        xslot["i"] += 1
                        Xs[t] = (xt[xi], r_xt[xi])
                        return Xs[t]

                    def chain2(g1, g2):
                        for _ in g1:
                            yield
                        yield
                        for _ in g2:
                            yield

                    prevB = None
                    drain(hT_stage(0, *alloc_x(0)))
                    drain(gla_gate(0))
                    for t in range(NT):
                        gens = [gla_stream(t), mla_stream(t), ret_stream(t)]
                        weights = [1, 1, 1]
                        if not PIPE:
                            if prevB is not None:
                                drain(prevB)
                            for g_ in gens:
                                drain(g_)
                            if t + 1 < NT:
                                drain(hT_stage(t + 1, *alloc_x(t + 1)))
                                drain(gla_gate(t + 1))
                        else:
                            if t + 1 < NT:
                                gens.append(chain2(hT_stage(t + 1, *alloc_x(t + 1)), gla_gate(t + 1)))
                                weights.append(1)
                            if prevB is not None:
                                gens.insert(0, prevB)
                                weights.insert(0, max(1, -(-(t + 4) // 4)))
                            round_robin(gens, weights)
                        prevB = phaseB(t, *Xs[t])
                    drain(prevB)

        except StopBuild:
            pass
        S.add("sp", None, [r_y[s][t] for s in range(NSEQ) for t in range(NT)], [])

        if os.environ.get("KDBG"):
            print("SBUF remaining", nc.sbuf_bytes_remaining, "ops", {e: len(v) for e, v in S.ops.items()})
        S.finalize(sems, dsems)
        with nc.Block() as block:
            @block.sync
            def _(e):
                S.replay("sp", e)

            @block.gpsimd
            def _(e):
                S.replay("pool", e)

            @block.vector
            def _(e):
                S.replay("dve", e)

            @block.scalar
            def _(e):
                S.replay("act", e)

            @block.tensor
            def _(e):
                S.replay("pe", e)
    return nc


def run(inputs, NSEQ, NT, L, ncores=NCORES, trace=False):
    f = np.float32
    x = np.asarray(inputs["x"], f)
    c = np.asarray(inputs["c"], f)
    pos = np.asarray(inputs["positions"], np.int32)
    consts = _const_pack(L, np.asarray(inputs["norm_w"], f), np.asarray(inputs["mla_q_norm"], f),
                         np.asarray(inputs["mla_kv_norm"], f), np.asarray(inputs["gla_norm"], f),
                         np.asarray(inputs["gla_b_g2"], f))
    fnorm = np.ascontiguousarray(np.broadcast_to(np.asarray(inputs["final_norm"], f)[None, :], (128, D)))
    ada_b = np.asarray(inputs["ada_b"], f).reshape(1, L * 3 * D)
    ada_b_rep = np.ascontiguousarray(np.broadcast_to(ada_b, (NSEQ, L * 3 * D)))
    shared = {
        "consts": consts, "fnorm": fnorm, "ada_w": np.ascontiguousarray(inputs["ada_w"], f), "ada_b": ada_b_rep,
        "w_in": np.ascontiguousarray(inputs["w_in"], f), "w_uq": np.ascontiguousarray(inputs["w_uq"], f),
        "w_ukv": np.ascontiguousarray(inputs["w_ukv"], f), "w_g2": np.ascontiguousarray(inputs["gla_w_g2"], f),
        "b_g2": np.ascontiguousarray(inputs["gla_b_g2"], f), "w_out": np.ascontiguousarray(inputs["w_out"], f),
    }
    in_maps = []
    for ci in range(ncores):
        sl = slice(ci * NSEQ, (ci + 1) * NSEQ)
        xc = np.ascontiguousarray(x[sl])
        cc = c[sl]
        cT = np.ascontiguousarray(cc.reshape(NSEQ, 8, 128).transpose(2, 1, 0).reshape(128, 8 * NSEQ))
        pc = pos[sl]
        pT = np.ascontiguousarray(pc.reshape(NSEQ, NT, 128).transpose(2, 0, 1).reshape(128, NSEQ * NT))
        m = {"x": xc, "cT": cT, "pos": pT}
        m.update(shared)
        in_maps.append(m)
    nc = build(NSEQ, NT, L)
    res = run_bass_kernel_spmd(nc, in_maps, core_ids=list(range(ncores)), trace=trace)
    out = np.concatenate([r["y"] for r in res.results], axis=0)
    return out, res


def kernel(**inputs):
    out, _ = run(inputs, NSEQ=4, NT=16, L=2)
    return out.astype(np.float32)
```

```python
import math
from contextlib import ExitStack

import numpy as np
import concourse.bass as bass
import concourse.mybir as mybir
from concourse.bass_utils import run_bass_kernel_spmd

F32 = mybir.dt.float32
BF16 = mybir.dt.bfloat16
I32 = mybir.dt.int32
AF = mybir.ActivationFunctionType
ALU = mybir.AluOpType
AX = mybir.AxisListType

D = 1024
INC = 2736
EPS = 1e-6
NCORES = 8
PI = math.pi
TWO_PI = 2.0 * math.pi
C1 = 6.28125
C2 = TWO_PI - 6.28125
PI_LO = 3.1415925

_off = {}
_cur = 0


def _reg(name, n):
    global _cur
    _off[name] = (_cur, _cur + n)
    _cur += n


_reg("ident", 128)
_reg("maskR", 512)
_reg("tri", 128)
_reg("ones", 128)
_reg("trimo", 128)
_reg("M1s", 128)
_reg("M2s", 128)
_reg("wqT", 256)
_reg("wk", 4)
_reg("g128", 2)
_reg("inv48", 48)
_reg("hm", 4)
NCONST_BASE = _cur


def _const_pack(L, norm_w, mla_q_norm, mla_kv_norm, gla_norm, b_g2):
    f = np.float32
    h = np.arange(4, dtype=f)
    log_gamma = np.log1p(-np.exp2(-5.0 - h)).astype(f)
    idx = np.arange(128, dtype=f)
    cj = (np.arange(128) // 64)
    maskR = np.zeros((128, 4, 128), f)
    for hh in range(4):
        dist = np.abs(idx[None, :] - idx[:, None])
        dec = np.exp(log_gamma[hh] * dist).astype(f)
        same = cj[:, None] == cj[None, :]
        past = cj[:, None] < cj[None, :]
        maskR[:, hh, :] = np.where(same | past, dec, 0.0) * f(0.125)
    tri = (np.arange(128)[:, None] <= np.arange(128)[None, :]).astype(f)
    ones = np.ones((128, 128), f)
    trimo = tri - ones
    sc = f(32 ** -0.5)
    M1s = tri * sc
    M2s = ((np.arange(128)[:, None] > np.arange(128)[None, :]) & (cj[:, None] == cj[None, :])).astype(f) * sc
    wqT = np.zeros((128, 2, 128), f)
    g128 = np.zeros((128, 2), f)
    for pair in range(2):
        for hh in range(2):
            hd = 2 * pair + hh
            wqT[64 * hh:64 * hh + 64, pair, :] = np.exp(log_gamma[hd] * (idx + 1.0))[None, :]
            g128[64 * hh:64 * hh + 64, pair] = np.exp(log_gamma[hd] * 128.0)
    wk = np.exp(log_gamma[None, :] * (127.0 - idx)[:, None]).astype(f) * f(0.125)
    inv_r = (10000.0 ** (-np.arange(32, dtype=f) / 32)).astype(f)
    inv_m = (10000.0 ** (-np.arange(16, dtype=f) / 16)).astype(f)
    inv48 = np.concatenate([inv_r, inv_m])[None, :].repeat(128, 0)
    parts = [np.eye(128, dtype=f), maskR.reshape(128, 512), tri, ones, trimo, M1s, M2s,
             wqT.reshape(128, 256), wk, g128, inv48,
             (np.arange(128)[:, None] // 32 == np.arange(4)[None, :]).astype(f)]
    nwT = norm_w.reshape(L, 8, 128).transpose(2, 0, 1).reshape(128, L * 8)
    qnT = mla_q_norm.reshape(L, 2, 128).transpose(2, 0, 1).reshape(128, L * 2)
    kvT = mla_kv_norm.reshape(L, 1, 128).transpose(2, 0, 1).reshape(128, L)
    gn = np.broadcast_to(gla_norm.reshape(1, L * 64), (128, L * 64))
    bg = np.broadcast_to(b_g2.reshape(1, L * 128), (128, L * 128))
    parts += [nwT, qnT, kvT, gn, bg]
    return np.ascontiguousarray(np.concatenate([p.astype(f) for p in parts], axis=1))


class Res:
    __slots__ = ("name", "w", "r", "rd", "excl")

    def __init__(self, name, excl=False):
        self.name = name
        self.excl = excl
        self.w = None
        self.r = {}
        self.rd = []


class Op:
    __slots__ = ("eng", "fn", "waits", "flag", "sem", "val", "dma")

    def __init__(self, eng, fn, dma):
        self.eng = eng
        self.fn = fn
        self.waits = []
        self.flag = False
        self.sem = None
        self.val = 0
        self.dma = dma


ENGS = ("pe", "act", "dve", "pool", "sp")


class Sched:
    def __init__(self):
        self.ops = {e: [] for e in ENGS}
        self.all = []

    def add(self, eng, fn, reads=(), writes=(), dma=False):
        op = Op(eng, fn, dma)
        deps = []
        for r in reads:
            if r.w is not None:
                deps.append((r.w, "raw"))
            if r.excl:
                for en, o in r.r.items():
                    if en != eng:
                        deps.append((o, "war"))
        for r in writes:
            if r.w is not None:
                deps.append((r.w, "waw"))
            for o in r.r.values():
                deps.append((o, "war"))
            for o in r.rd:
                deps.append((o, "war"))
        seen = set()
        for d, kind in deps:
            if d is op or id(d) in seen:
                continue
            if d.eng == eng and not d.dma and not dma:
                if eng == "pe":
                    continue
                if kind != "raw":
                    continue
            seen.add(id(d))
            op.waits.append(d)
            d.flag = True
        for r in reads:
            if dma:
                r.rd.append(op)
            else:
                r.r[eng] = op
        for r in writes:
            r.w = op
            r.r = {}
            r.rd = []
        self.ops[eng].append(op)
        self.all.append(op)
        return op

    def finalize(self, sems, dsems):
        cnt = {e: 0 for e in ENGS}
        nd = len(dsems)
        di = 0
        last_on = [None] * nd
        for op in self.all:
            if op.dma:
                j = di % nd
                op.sem = dsems[j]
                op.val = 16 * (di // nd + 1)
                op.flag = True
                if last_on[j] is not None:
                    op.waits.append(last_on[j])
                last_on[j] = op
                di += 1
            elif op.flag:
                cnt[op.eng] += 1
                op.sem = sems[op.eng]
                op.val = cnt[op.eng]
        return cnt

    def replay(self, eng, e):
        seen = {}
        for op in self.ops[eng]:
            for w in op.waits:
                key = id(w.sem)
                if seen.get(key, 0) >= w.val:
                    continue
                seen[key] = w.val
                e.wait_ge(w.sem, w.val)
            if op.fn is None:
                continue
            ins = op.fn(e)
            if op.flag:
                ins.then_inc(op.sem, 16 if op.dma else 1)


class StopBuild(Exception):
    pass


def build(NSEQ, NT, L):
    import os
    STOP = int(os.environ.get("KSTOP", "99"))
    PIPE = int(os.environ.get("KPIPE", "1"))

    def stage_(n):
        if STOP == n:
            raise StopBuild()
    S_LEN = NT * 128
    nc = bass.Bass("TRN2", target_bir_lowering=False)
    NCONST = NCONST_BASE + L * 8 + L * 2 + L + L * 64 + L * 128
    o_nw = NCONST_BASE
    o_qn = o_nw + L * 8
    o_kv = o_qn + L * 2
    o_gn = o_kv + L
    o_bg = o_gn + L * 64

    def din(name, shape, dt=F32):
        return nc.dram_tensor(name, list(shape), dt, kind="ExternalInput").ap()

    x_d = din("x", [NSEQ, S_LEN, D])
    cT_d = din("cT", [128, 8 * NSEQ])
    pos_d = din("pos", [128, NSEQ * NT], I32)
    consts_d = din("consts", [128, NCONST])
    fnorm_d = din("fnorm", [128, D])
    ada_w_d = din("ada_w", [L, D, 3 * D])
    ada_b_d = din("ada_b", [NSEQ, L * 3 * D])
    w_in_d = din("w_in", [L, D, INC])
    w_uq_d = din("w_uq", [L, 256, 768])
    w_ukv_d = din("w_ukv", [L, 128, 1024])
    w_g2_d = din("w_g2", [L, 16, 128])
    b_g2_d = din("b_g2", [L, 128])
    w_out_d = din("w_out", [L, D, D])
    y_d = nc.dram_tensor("y", [NSEQ, S_LEN, D], F32, kind="ExternalOutput").ap()
    mods_d = nc.dram_tensor("mods", [L, NSEQ, 3 * D], F32, kind="Internal").ap()

    S = Sched()
    with ExitStack() as es:
        def sb(name, shape, dt=F32):
            return es.enter_context(nc.sbuf_tensor("sb_" + name, list(shape), dt))

        def R(name):
            return Res(name)

        cst = sb("cst", [128, NCONST]); r_cst = R("cst")
        fnorm = sb("fnorm", [128, D]); r_fnorm = R("fnorm")
        ident_b = sb("ident_b", [128, 128], BF16); r_ident = R("ident")
        w_in_b = sb("w_in_b", [128, 8, INC], BF16); r_win = R("win")
        w_out_b = sb("w_out_b", [128, 8, D], BF16); r_wout = R("wout")
        w_uq_b = sb("w_uq_b", [128, 2, 768], BF16); r_wuq = R("wuq")
        w_ukv_b = sb("w_ukv_b", [128, 1024], BF16); r_wukv = R("wukv")
        w_g2_b = sb("w_g2_b", [32, 128], BF16); r_wg2 = R("wg2")
        KT = sb("KT", [96, 8, S_LEN], BF16); r_KT = [R("KT%d" % t) for t in range(NT)]
        VA = sb("VA", [128, NT, 8, 65], BF16); r_VA = [R("VA%d" % t) for t in range(NT)]
        stage = sb("stage", [128, 1024]); r_stage = R("stage")
        gate_bc = sb("gate_bc", [128, D], BF16); r_gate = R("gate")
        modc = sb("modc", [128, 16]); r_modc = R("modc")
        gp = sb("gp", [128, 8]); r_gp = R("gp")
        modg = sb("modg", [NSEQ, 128]); r_modg = R("modg")
        r_modsd = [[R("modsd%d_%d" % (l_, g_)) for g_ in range(24)] for l_ in range(L)]
        adab = sb("adab", [NSEQ, 128]); r_adab = R("adab")
        cact = sb("cact", [128, 8 * NSEQ]); r_cact = R("cact")
        ctmp = sb("ctmp", [128, 8 * NSEQ]); r_ctmp = R("ctmp")
        cin = sb("cin", [128, 8 * NSEQ]); r_cin = R("cin")
        pos_i = sb("pos_i", [128, NSEQ * NT], I32); r_posi = R("posi")
        CH = min(2, NT)
        pos_f = sb("pos_f", [128, NT]); r_posf = R("posf")
        ang = sb("ang", [128, CH, 48]); r_ang = R("ang")
        rr = sb("rr", [128, CH, 48]); r_rr = R("rr")
        kk_i = sb("kk_i", [128, CH, 48], I32); r_kki = R("kki")
        kk_f = sb("kk_f", [128, CH, 48]); r_kkf = R("kkf")
        mm_ = sb("mm_", [128, CH, 48]); r_mm = R("mm")
        rc = sb("rc", [128, CH, 48]); r_rc = R("rc")
        Tsin = sb("Tsin", [128, NT, 48]); r_Tsin = R("Tsin")
        Tcos = sb("Tcos", [128, NT, 48]); r_Tcos = R("Tcos")
        xt = [sb("xt%d" % i, [128, D]) for i in range(3)]; r_xt = [R("xt0"), R("xt1"), R("xt2")]
        st_ = sb("st_", [128, 16]); r_st = R("st")
        st_r = sb("st_r", [128, 16]); r_str = R("str")
        st_g = sb("st_g", [128, 16]); r_stg = R("stg")
        st_m = sb("st_m", [128, 16]); r_stm = R("stm")
        xn = sb("xn", [128, D], BF16); r_xn = R("xn")
        hTs = [sb("hT%d" % i, [128, 8, 128], BF16) for i in range(2)]
        r_hTs = [[R("hT%da" % i), R("hT%db" % i)] for i in range(2)]
        qkf = sb("qkf", [128, 512]); r_qkf = R("qkf")
        tA = sb("tA", [128, 256]); r_tA = R("tA")
        tB = sb("tB", [128, 256]); r_tB = R("tB")
        qkrot = sb("qkrot", [128, 8, 64], BF16); r_qkrot = R("qkrot")
        kdec = sb("kdec", [128, 4, 64], BF16); r_kdec = R("kdec")
        qT_m = sb("qT_m", [128, 4, 128], BF16); r_qTm = R("qTm")
        qTd_m = sb("qTd_m", [128, 4, 128], BF16); r_qTdm = R("qTdm")
        kTr = sb("kTr", [128, 2, 128], BF16); r_kTr = R("kTr")
        STm = sb("STm", [128, 4, 128], BF16); r_STm = R("STm")
        vb_r = sb("vb_r", [128, 256], BF16); r_vbr = R("vbr")
        S32 = sb("S32", [128, 2, 64]); r_S32 = R("S32")
        S_b = sb("S_b", [128, 4, 64], BF16); r_Sb = R("Sb")
        glow_b = sb("glow_b", [128, 32], BF16); r_glowb = R("glowb")
        glowT = sb("glowT", [32, 128], BF16); r_glowT = R("glowT")
        lf = sb("lf", [128, 128]); r_lf = R("lf")
        e3s = [sb("e3_%d" % i, [128, 3, 128]) for i in range(2)]; r_e3s = [R("e3_0"), R("e3_1")]
        qk4 = sb("qk4", [128, 4, 128], BF16); r_qk4 = R("qk4")
        kdg = sb("kdg", [128, 128], BF16); r_kdg = R("kdg")
        qpT_m = sb("qpT_m", [128, 4, 128], BF16); r_qpTm = R("qpTm")
        qnT_m = sb("qnT_m", [128, 4, 128], BF16); r_qnTm = R("qnTm")
        kTg = sb("kTg", [128, 2, 128], BF16); r_kTg = R("kTg")
        A1 = sb("A1", [128, 4, 128], BF16); r_A1 = R("A1")
        A2 = sb("A2", [128, 4, 128], BF16); r_A2 = R("A2")
        vb_g = sb("vb_g", [128, 256], BF16); r_vbg = R("vbg")
        Sg32 = sb("Sg32", [128, 64]); r_Sg32 = R("Sg32")
        Sg_b = sb("Sg_b", [128, 64], BF16); r_Sgb = R("Sgb")
        edecs = [sb("edec%d" % i, [128, 1]) for i in range(2)]; r_edecs = [R("edec0"), R("edec1")]
        latn = sb("latn", [128, 384], BF16); r_latn = R("latn")
        latT = sb("latT", [128, 3, 128], BF16); r_latT = R("latT")
        Qb = sb("Qb", [128, 8, 96], BF16); r_Qb = R("Qb")
        Kfull = sb("Kfull", [128, 8, 96], BF16); r_Kfull = R("Kfull")
        kpe = sb("kpe", [128, 32], BF16); r_kpe = R("kpe")
        krf = sb("krf", [128, 32]); r_krf = R("krf")
        QT2 = [sb("QT%d" % i, [96, 8, 128], BF16) for i in range(2)]; r_QT2 = [R("QT0"), R("QT1")]
        PT = [sb("PT%d" % i, [128, 8, 128], BF16) for i in range(2)]; r_PT = [[R("PT0a"), R("PT0b")], [R("PT1a"), R("PT1b")]]
        rs = sb("rs", [128, 8]); r_rs = R("rs")
        eg = sb("eg", [128, 256]); r_eg = R("eg")
        sg = sb("sg", [128, 256]); r_sg = R("sg")
        gA = sb("gA", [128, 256], BF16); r_gA = R("gA")
        gA_g = sb("gA_g", [128, 256], BF16); r_gAg = R("gAg")
        gB = sb("gB", [128, 256], BF16); r_gB = R("gB")
        gB_gs = [sb("gB_g%d" % i, [128, 256], BF16) for i in range(2)]; r_gBgs = [R("gBg0"), R("gBg1")]
        mixed2 = [sb("mixed%d" % i, [128, D], BF16) for i in range(2)]; r_mixed2 = [R("mixed0"), R("mixed1")]
        gBm = [sb("gBm%d" % i, [128, 512], BF16) for i in range(2)]; r_gBm = [R("gBm0"), R("gBm1")]
        gAm = sb("gAm", [128, 512]); r_gAm = R("gAm")
        st_b = sb("st_b", [128, 4]); r_stb = R("stb")
        mixT = sb("mixT", [128, 8, 128], BF16); r_mixT = R("mixT")

        PP = [es.enter_context(nc.psum_tensor("pp%d" % i, [128, 1024], F32)) for i in range(4)]
        r_bank = [[Res("b%d_%d" % (i, j), excl=True) for j in range(2)] for i in range(4)]
        gen_banks = [(0, 0), (0, 1), (2, 0), (2, 1)]
        gstate = {"i": 0, "sc": 0}

        def gbank():
            i, j = gen_banks[gstate["i"] % 4]
            gstate["i"] += 1
            return PP[i][:, j * 512:(j + 1) * 512], r_bank[i][j]

        sems = {e: es.enter_context(nc.semaphore("s_" + e)) for e in ENGS}
        dsems = [es.enter_context(nc.semaphore("d%d" % i)) for i in range(24)]

        def C(name):
            a, b = _off[name]
            return cst[:, a:b]

        def dma(out, in_, reads, writes, eng="sp", **kw):
            return S.add(eng, lambda e: e.dma_start(out=out, in_=in_, **kw), reads, writes, dma=True)

        def mm(out, lhsT, rhs, start, reads, writes):
            return S.add("pe", lambda e: e.matmul(out, lhsT=lhsT, rhs=rhs, start=start, stop=True,
                                                  skip_group_check=True), reads, writes)

        def tr(out, in_, reads, writes):
            return S.add("pe", lambda e: e.transpose(out, in_, ident_b[:]), list(reads) + [r_ident], writes)

        def act(out, in_, func, reads, writes, scale=1.0, bias=0.0, accum=None, eng="act"):
            if accum is None:
                return S.add(eng, lambda e: e.activation(out=out, in_=in_, func=func, scale=scale, bias=bias),
                             reads, writes)
            return S.add(eng, lambda e: e.activation(out=out, in_=in_, func=func, scale=scale, bias=bias,
                                                     accum_out=accum), reads, writes)

        def tt(out, a, b, op, reads, writes, eng="dve"):
            return S.add(eng, lambda e: e.tensor_tensor(out=out, in0=a, in1=b, op=op), reads, writes)

        def ts(out, a, s1, s2, op0, op1, reads, writes, eng="dve"):
            if op1 is None:
                return S.add(eng, lambda e: e.tensor_scalar(out=out, in0=a, scalar1=s1, scalar2=None, op0=op0),
                             reads, writes)
            return S.add(eng, lambda e: e.tensor_scalar(out=out, in0=a, scalar1=s1, scalar2=s2, op0=op0, op1=op1),
                         reads, writes)

        def stt(out, a, scalar, b, op0, op1, reads, writes):
            return S.add("dve", lambda e: e.scalar_tensor_tensor(out=out, in0=a, scalar=scalar, in1=b,
                                                                 op0=op0, op1=op1), reads, writes)

        def cp(out, in_, reads, writes, eng="dve"):
            if eng == "act":
                return act(out, in_, AF.Copy, reads, writes)
            return S.add(eng, lambda e: e.tensor_copy(out=out, in_=in_), reads, writes)

        def memset(ap, val, writes, eng="dve"):
            return S.add(eng, lambda e: e.memset(ap, val), (), writes)

        def rstd_from(ssq_ap, out_ap, n, reads, writes_tmp, tmp_ap):
            act(tmp_ap, ssq_ap, AF.Ln, reads, writes_tmp, scale=1.0 / n, bias=EPS)
            act(out_ap, tmp_ap, AF.Exp, writes_tmp, writes_tmp, scale=-0.5)

        def sigmoid_from(z_ap, z_reads, width):
            act(eg[:, 0:width], z_ap, AF.Exp, z_reads, [r_eg], scale=-1.0)
            act(sg[:, 0:width], eg[:, 0:width], AF.Ln, [r_eg], [r_sg], bias=1.0)
            act(sg[:, 0:width], sg[:, 0:width], AF.Exp, [r_sg], [r_sg], scale=-1.0)

        try:
            dma(cst[:], consts_d[:, :], [], [r_cst])
            dma(fnorm[:], fnorm_d[:, :], [], [r_fnorm])
            dma(cin[:], cT_d[:, :], [], [r_cin])
            dma(pos_i[:], pos_d[:, :], [], [r_posi])
            cp(ident_b[:], C("ident"), [r_cst], [r_ident])
            for buf, rb in ((qT_m, r_qTm), (qTd_m, r_qTdm), (qpT_m, r_qpTm), (qnT_m, r_qnTm), (S_b, r_Sb)):
                memset(buf[:], 0.0, [rb])
            memset(VA[:], 1.0, r_VA)
            memset(w_g2_b[:], 0.0, [r_wg2])
            act(ctmp[:], cin[:], AF.Exp, [r_cin], [r_ctmp], scale=-1.0)
            ts(ctmp[:], ctmp[:], 1.0, None, ALU.add, None, [r_ctmp], [r_ctmp])
            S.add("dve", lambda e: e.reciprocal(out=ctmp[:], in_=ctmp[:]), [r_ctmp], [r_ctmp])
            tt(cact[:], cin[:], ctmp[:], ALU.mult, [r_cin, r_ctmp], [r_cact])
            cact3 = cact[:].rearrange("p (k s) -> p k s", s=NSEQ)
            def ada_group(l, g, bk, rbk):
                st3 = stage[:, 0:1024].rearrange("p (k c) -> p k c", c=128)
                dma(st3, ada_w_d[l, :, g * 128:(g + 1) * 128].rearrange("(k p) c -> p k c", p=128),
                    [], [r_stage])
                dma(adab[:], ada_b_d[:, l * 3 * D + g * 128: l * 3 * D + (g + 1) * 128], [], [r_adab])
                for k in range(8):
                    mm(bk[0:NSEQ, 0:128], cact3[:, k, :], st3[:, k, :], k == 0, [r_cact, r_stage], [rbk])
                tt(modg[:], bk[0:NSEQ, 0:128], adab[:], ALU.add, [rbk, r_adab], [r_modg])
                dma(mods_d[l, :, g * 128:(g + 1) * 128], modg[:], [r_modg], [r_modsd[l][g]])

            for g in range(24):
                bk, rbk = gbank()
                ada_group(0, g, bk, rbk)
            ada_todo = [(l_, g_) for l_ in range(1, L) for g_ in range(24)]

            stage_(1)
            r_y = [[R("y%d_%d" % (s, t)) for t in range(NT)] for s in range(NSEQ)]
            loaded_layer = {"l": None}
            xslot = {"i": 0}

            def rope_tables(s):
                cp(pos_f[:], pos_i[:, s * NT:(s + 1) * NT], [r_posi], [r_posf])
                for c0 in range(0, NT, CH):
                    tt(ang[:], pos_f[:, c0:c0 + CH].unsqueeze(2).to_broadcast([128, CH, 48]),
                       C("inv48").unsqueeze(1).to_broadcast([128, CH, 48]), ALU.mult, [r_posf, r_cst], [r_ang])
                    ts(rr[:], ang[:], 1.0 / TWO_PI, None, ALU.mult, None, [r_ang], [r_rr])
                    cp(kk_i[:], rr[:], [r_rr], [r_kki])
                    cp(kk_f[:], kk_i[:], [r_kki], [r_kkf])
                    stt(rr[:], kk_f[:], -C1, ang[:], ALU.mult, ALU.add, [r_kkf, r_ang], [r_rr])
                    stt(rr[:], kk_f[:], -C2, rr[:], ALU.mult, ALU.add, [r_kkf, r_rr], [r_rr])

                    def wrap(buf, rb):
                        ts(mm_[:], buf[:], PI, None, ALU.is_gt, None, [rb], [r_mm])
                        stt(buf[:], mm_[:], -TWO_PI, buf[:], ALU.mult, ALU.add, [r_mm, rb], [rb])
                        ts(mm_[:], buf[:], -PI, None, ALU.is_lt, None, [rb], [r_mm])
                        stt(buf[:], mm_[:], TWO_PI, buf[:], ALU.mult, ALU.add, [r_mm, rb], [rb])
                        ts(buf[:], buf[:], -PI_LO, PI_LO, ALU.max, ALU.min, [rb], [rb])

                    wrap(rr, r_rr)
                    ts(rc[:], rr[:], PI / 2, None, ALU.add, None, [r_rr], [r_rc])
                    wrap(rc, r_rc)
                    act(Tsin[:, c0:c0 + CH, :], rr[:], AF.Sin, [r_rr], [r_Tsin])
                    act(Tcos[:, c0:c0 + CH, :], rc[:], AF.Sin, [r_rc], [r_Tcos])

            for l in range(L):
                while ada_todo and ada_todo[0][0] <= l:
                    l_, g_ = ada_todo.pop(0)
                    bk, rbk = gbank()
                    ada_group(l_, g_, bk, rbk)
                for s in range(NSEQ):
                    rope_tables(s)
                    last_layer = (l == L - 1)
                    dma(modc[:, 0:8], mods_d[l, s, 0:D].rearrange("(k p) -> p k", p=128), r_modsd[l][0:8], [r_modc],
                        allow_slow_non_contiguous=True)
                    dma(modc[:, 8:16], mods_d[l, s, D:2 * D].rearrange("(k p) -> p k", p=128), r_modsd[l][8:16], [r_modc],
                        allow_slow_non_contiguous=True)
                    dma(stage[0:1, 0:D], mods_d[l, s:s + 1, 2 * D:3 * D], r_modsd[l][16:24], [r_stage])
                    stt(gp[:], modc[:, 8:16], 1.0, cst[:, o_nw + l * 8:o_nw + (l + 1) * 8], ALU.add, ALU.mult,
                        [r_modc, r_cst], [r_gp])
                    for hf in range(2):
                        bk, rbk = gbank()
                        mm(bk, C("ones")[0:1, :], stage[0:1, hf * 512:(hf + 1) * 512], True, [r_cst, r_stage], [rbk])
                        cp(gate_bc[:, hf * 512:(hf + 1) * 512], bk, [rbk], [r_gate], eng="act")
                    if loaded_layer["l"] != l:
                        loaded_layer["l"] = l
                        for k in range(8):
                            for (a, b) in ((0, 1368), (1368, INC)):
                                dma(w_in_b[:, k, a:b], w_in_d[l, k * 128:(k + 1) * 128, a:b], [], [r_win], eng="pool")
                            dma(w_out_b[:, k, :], w_out_d[l, k * 128:(k + 1) * 128, :], [], [r_wout], eng="pool")
                        dma(w_g2_b[0:16, :], w_g2_d[l, :, :], [], [r_wg2], eng="pool")
                        for k in range(2):
                            dma(stage[:, 0:768], w_uq_d[l, k * 128:(k + 1) * 128, :], [], [r_stage])
                            ts(w_uq_b[:, k, :], stage[:, 0:768], cst[:, o_qn + l * 2 + k:o_qn + l * 2 + k + 1], None,
                               ALU.mult, None, [r_stage, r_cst], [r_wuq])
                        dma(stage[:, 0:1024], w_ukv_d[l, :, :], [], [r_stage])
                        ts(w_ukv_b[:], stage[:, 0:1024], cst[:, o_kv + l:o_kv + l + 1], None, ALU.mult, None,
                           [r_stage, r_cst], [r_wukv])
                    memset(S32[:], 0.0, [r_S32])
                    for sl in range(4):
                        hh = sl % 2
                        memset(S_b[64 * hh:64 * hh + 64, sl, :], 0.0, [r_Sb])
                    memset(Sg32[:], 0.0, [r_Sg32])
                    memset(Sg_b[:], 0.0, [r_Sgb])

                    stage_(3)
                    bank_R = (PP[0][:, 0:512], r_bank[0][0])
                    bank_G = (PP[0][:, 512:1024], r_bank[0][1])
                    bank_M = [(PP[2][:, 0:512], r_bank[2][0]), (PP[2][:, 512:1024], r_bank[2][1])]
                    mstate = {"i": 0}

                    def nextM():
                        return bank_M[0]

                    def inproj_to(bk_, rbk_, c0, c1, t):
                        hT = hTs[t % 2]
                        for k in range(8):
                            mm(bk_[:, 0:c1 - c0], hT[:, k, :], w_in_b[:, k, c0:c1], k == 0, r_hTs[t % 2] + [r_win], [rbk_])

                    def norm_gate(bo, rbo, stx, rstx, gAx, rgAx, gBx, rgBx, out_ap, rout):
                        act(eg[:, 0:256], bo[:, 0:256], AF.Square, [rbo], [r_eg])
                        S.add("dve", lambda e: e.tensor_reduce(out=stx[:, 4:8], in_=eg[:, 0:256].rearrange("p (h d) -> p h d", d=64),
                                                                axis=AX.X, op=ALU.add), [r_eg], [rstx])
                        rstd_from(stx[:, 4:8], stx[:, 8:12], 64, [rstx], [rstx], stx[:, 12:16])
                        tt(gAx[:, 0:256].rearrange("p (h d) -> p h d", d=64), bo[:, 0:256].rearrange("p (h d) -> p h d", d=64),
                           stx[:, 8:12].unsqueeze(2).to_broadcast([128, 4, 64]), ALU.mult, [rbo, rstx], [rgAx])
                        tt(out_ap, gAx[:, 0:256], gBx[:, 0:256], ALU.mult, [rgAx, rgBx], [rout], eng="pool")

                    def hT_stage(t, X, rX):
                        hT = hTs[t % 2]
                        rh = r_hTs[t % 2]
                        src = x_d if l == 0 else y_d
                        dma(X[:], src[s, t * 128:(t + 1) * 128, :], [r_y[s][t]] if l > 0 else [], [rX])
                        act(xn[:], X[:], AF.Square, [rX], [r_xn, r_st], accum=st_[:, 0:1])
                        rstd_from(st_[:, 0:1], st_[:, 2:3], D, [r_st], [r_st], st_[:, 1:2])
                        S.add("pool", lambda e: e.tensor_scalar(out=xn[:], in0=X[:], scalar1=st_[:, 2:3], scalar2=1.0,
                                                                 op0=ALU.mult, op1=ALU.mult), [rX, r_st], [r_xn])
                        yield
                        bA, rbA = bank_R
                        bAb = bA.bitcast(BF16)
                        for k in range(8):
                            tr(bAb[:, k * 128:(k + 1) * 128], xn[:, k * 128:(k + 1) * 128], [r_xn], [rbA])
                        for k in range(4):
                            ts(hT[:, k, :], bAb[:, k * 128:(k + 1) * 128], gp[:, k:k + 1], modc[:, k:k + 1],
                               ALU.mult, ALU.add, [rbA, r_gp, r_modc], [rh[0]])
                        for k in range(4, 8):
                            act(hT[:, k, :], bAb[:, k * 128:(k + 1) * 128], AF.Identity, [rbA, r_gp, r_modc], [rh[1]],
                                scale=gp[:, k:k + 1], bias=modc[:, k:k + 1])

                    def ret_stream(t):
                        bR, rbR = bank_R
                        mx, rmx = mixed2[t % 2], r_mixed2[t % 2]
                        inproj_to(bR, rbR, 0, 512, t)
                        cp(qkf[:], bR, [rbR], [r_qkf], eng="act")
                        yield
                        q3 = qkf[:].rearrange("p (h d) -> p h d", d=64)
                        x1 = q3[:, :, 0:32]
                        x2 = q3[:, :, 32:64]
                        cosb = Tcos[:, t, 0:32].unsqueeze(1).to_broadcast([128, 8, 32])
                        sinb = Tsin[:, t, 0:32].unsqueeze(1).to_broadcast([128, 8, 32])
                        tA3 = tA[:].rearrange("p (h d) -> p h d", d=32)
                        tB3 = tB[:].rearrange("p (h d) -> p h d", d=32)
                        tt(tA3, x1, cosb, ALU.mult, [r_qkf, r_Tcos], [r_tA], eng="pool")
                        tt(tB3, x2, sinb, ALU.mult, [r_qkf, r_Tsin], [r_tB], eng="pool")
                        tt(qkrot[:, :, 0:32], tA3, tB3, ALU.subtract, [r_tA, r_tB], [r_qkrot], eng="pool")
                        tt(tA3, x2, cosb, ALU.mult, [r_qkf, r_Tcos], [r_tA], eng="pool")
                        tt(tB3, x1, sinb, ALU.mult, [r_qkf, r_Tsin], [r_tB], eng="pool")
                        tt(qkrot[:, :, 32:64], tA3, tB3, ALU.add, [r_tA, r_tB], [r_qkrot], eng="pool")
                        tt(kdec[:], qkrot[:, 4:8, :], C("wk").unsqueeze(2).to_broadcast([128, 4, 64]), ALU.mult,
                           [r_qkrot, r_cst], [r_kdec], eng="pool")
                        yield
                        qkflat = qkrot[:].rearrange("p h d -> p (h d)")
                        bRb = bR.bitcast(BF16)
                        for j in range(4):
                            tr(bRb[:, j * 128:(j + 1) * 128], qkflat[:, j * 128:(j + 1) * 128], [r_qkrot], [rbR])
                        for pair in range(2):
                            for hh in range(2):
                                cp(qT_m[64 * hh:64 * hh + 64, 2 * pair + hh, :],
                                   bRb[64 * hh:64 * hh + 64, pair * 128:(pair + 1) * 128], [rbR], [r_qTm], eng="act")
                        cp(kTr[:], bRb[:, 256:512].rearrange("p (a b) -> p a b", b=128), [rbR], [r_kTr])
                        wq3 = C("wqT").rearrange("p (a b) -> p a b", b=128)
                        for pair in range(2):
                            for hh in range(2):
                                tt(qTd_m[64 * hh:64 * hh + 64, 2 * pair + hh, :], qT_m[64 * hh:64 * hh + 64, 2 * pair + hh, :],
                                   wq3[64 * hh:64 * hh + 64, pair, :], ALU.mult, [r_qTm, r_cst], [r_qTdm], eng="pool")
                        yield
                        for hd in range(4):
                            mm(bR[:, hd * 128:(hd + 1) * 128], kTr[:, hd // 2, :], qT_m[:, hd, :], hd == 0,
                               [r_kTr, r_qTm], [rbR])
                        tt(STm[:].rearrange("p a b -> p (a b)"), bR, C("maskR"), ALU.mult, [rbR, r_cst], [r_STm])
                        yield
                        inproj_to(bR, rbR, 512, 1024, t)
                        cp(vb_r[:], bR[:, 0:256], [rbR], [r_vbr], eng="act")
                        sigmoid_from(bR[:, 256:512], [rbR], 256)
                        tt(gB[:, 0:256], bR[:, 256:512], sg[:, 0:256], ALU.mult, [rbR, r_sg], [r_gB])
                        yield
                        for hd in range(4):
                            mm(bR[:, hd * 64:(hd + 1) * 64], STm[:, hd, :], vb_r[:, hd * 64:(hd + 1) * 64], hd == 0,
                               [r_STm, r_vbr], [rbR])
                            mm(bR[:, hd * 64:(hd + 1) * 64], qTd_m[:, hd, :], S_b[:, hd, :], False, [r_qTdm, r_Sb], [rbR])
                        norm_gate(bR, rbR, st_r, r_str, gA, r_gA, gB, r_gB, mx[:, 0:256], rmx)
                        yield
                        kd2 = kdec[:].rearrange("p h d -> p (h d)")
                        for pair in range(2):
                            mm(bR[:, pair * 128:(pair + 1) * 128], kd2[:, pair * 128:(pair + 1) * 128],
                               vb_r[:, pair * 128:(pair + 1) * 128], pair == 0, [r_kdec, r_vbr], [rbR])
                        for pair in range(2):
                            for hh in range(2):
                                p0 = 64 * hh
                                stt(S32[p0:p0 + 64, pair, :], S32[p0:p0 + 64, pair, :],
                                    cst[p0:p0 + 64, _off["g128"][0] + pair:_off["g128"][0] + pair + 1],
                                    bR[p0:p0 + 64, pair * 128 + hh * 64: pair * 128 + hh * 64 + 64],
                                    ALU.mult, ALU.add, [r_S32, r_cst, rbR], [r_S32])
                                cp(S_b[p0:p0 + 64, 2 * pair + hh, :], S32[p0:p0 + 64, pair, :], [r_S32], [r_Sb], eng="pool")

                    def gla_gate(t):
                        bG, rbG = bank_G
                        bGb = bG.bitcast(BF16)
                        e3, r_e3 = e3s[t % 2], r_e3s[t % 2]
                        edec, r_edec = edecs[t % 2], r_edecs[t % 2]
                        gB_g, r_gBg = gB_gs[t % 2], r_gBgs[t % 2]
                        inproj_to(bG, rbG, 2464, 2736, t)
                        cp(glow_b[:], bG[:, 0:32], [rbG], [r_glowb], eng="act")
                        sigmoid_from(bG[:, 16:272], [rbG], 256)
                        tt(gB_g[:, 0:256], bG[:, 16:272], sg[:, 0:256], ALU.mult, [rbG, r_sg], [r_gBg])
                        tt(gB_g[:, 0:256].rearrange("p (h d) -> p h d", d=64), gB_g[:, 0:256].rearrange("p (h d) -> p h d", d=64),
                           cst[:, o_gn + l * 64:o_gn + (l + 1) * 64].unsqueeze(1).to_broadcast([128, 4, 64]), ALU.mult,
                           [r_gBg, r_cst], [r_gBg], eng="pool")
                        yield
                        tr(bGb[0:32, 0:128], glow_b[:], [r_glowb], [rbG])
                        cp(glowT[:], bGb[0:32, 0:128], [rbG], [r_glowT])
                        yield
                        mm(bG[:, 0:128], glowT[:], w_g2_b[:], True, [r_glowT, r_wg2], [rbG])
                        tt(lf[:], bG[:, 0:128], cst[:, o_bg + l * 128:o_bg + (l + 1) * 128], ALU.add, [rbG, r_cst], [r_lf])
                        act(lf[:], lf[:], AF.Exp, [r_lf], [r_lf], scale=-1.0)
                        act(lf[:], lf[:], AF.Ln, [r_lf], [r_lf], bias=1.0)
                        yield
                        mm(bG[:, 0:128], C("tri"), lf[:], True, [r_cst, r_lf], [rbG])
                        mm(bG[:, 128:256], C("trimo"), lf[:], False, [r_cst, r_lf], [rbG])
                        mm(bG[:, 256:258], lf[:], C("ones")[:, 0:2], False, [r_cst, r_lf], [rbG])
                        act(e3[:, 0, :], bG[:, 0:128], AF.Exp, [rbG], [r_e3], scale=-1.0 / 16)
                        act(e3[:, 1, :], bG[:, 0:128], AF.Exp, [rbG], [r_e3], scale=1.0 / 16)
                        act(e3[:, 2, :], bG[:, 128:256], AF.Exp, [rbG], [r_e3], scale=1.0 / 16,
                            bias=float(math.log(32 ** -0.5)))
                        act(edec[:], bG[:, 256:257], AF.Exp, [rbG], [r_edec], scale=-1.0 / 16)

                    def gla_stream(t):
                        bG, rbG = bank_G
                        mx, rmx = mixed2[t % 2], r_mixed2[t % 2]
                        bGb = bG.bitcast(BF16)
                        hm0 = _off["hm"][0]
                        e3, r_e3 = e3s[t % 2], r_e3s[t % 2]
                        edec, r_edec = edecs[t % 2], r_edecs[t % 2]
                        gB_g, r_gBg = gB_gs[t % 2], r_gBgs[t % 2]
                        inproj_to(bG, rbG, 1952, 2464, t)
                        cp(vb_g[:], bG[:, 256:512], [rbG], [r_vbg], eng="act")
                        tt(qk4[:, 0, :], bG[:, 0:128], e3[:, 0, :], ALU.mult, [rbG, r_e3], [r_qk4])
                        tt(qk4[:, 1, :], bG[:, 0:128], e3[:, 1, :], ALU.mult, [rbG, r_e3], [r_qk4])
                        tt(qk4[:, 2, :], bG[:, 128:256], e3[:, 1, :], ALU.mult, [rbG, r_e3], [r_qk4])
                        tt(qk4[:, 3, :], bG[:, 128:256], e3[:, 0, :], ALU.mult, [rbG, r_e3], [r_qk4])
                        tt(kdg[:], bG[:, 128:256], e3[:, 2, :], ALU.mult, [rbG, r_e3], [r_kdg])
                        yield
                        for j in range(4):
                            tr(bGb[:, j * 128:(j + 1) * 128], qk4[:, j, :], [r_qk4], [rbG])
                        for hd in range(4):
                            ts(qpT_m[:, hd, :], bGb[:, 0:128], cst[:, hm0 + hd:hm0 + hd + 1], None, ALU.mult, None,
                               [rbG, r_cst], [r_qpTm])
                            ts(qnT_m[:, hd, :], bGb[:, 128:256], cst[:, hm0 + hd:hm0 + hd + 1], None, ALU.mult, None,
                               [rbG, r_cst], [r_qnTm])
                        cp(kTg[:], bGb[:, 256:512].rearrange("p (a b) -> p a b", b=128), [rbG], [r_kTg])
                        yield
                        m1b = C("M1s").unsqueeze(1).to_broadcast([128, 4, 128])
                        m2b = C("M2s").unsqueeze(1).to_broadcast([128, 4, 128])
                        for hd in range(4):
                            mm(bG[:, hd * 128:(hd + 1) * 128], kTg[:, 0, :], qpT_m[:, hd, :], hd == 0, [r_kTg, r_qpTm], [rbG])
                        tt(A1[:], bG.rearrange("p (a b) -> p a b", b=128), m1b, ALU.mult, [rbG, r_cst], [r_A1])
                        yield
                        for hd in range(4):
                            mm(bG[:, hd * 128:(hd + 1) * 128], kTg[:, 1, :], qnT_m[:, hd, :], hd == 0, [r_kTg, r_qnTm], [rbG])
                        tt(A2[:], bG.rearrange("p (a b) -> p a b", b=128), m2b, ALU.mult, [rbG, r_cst], [r_A2])
                        yield
                        for hd in range(4):
                            o_ = bG[:, hd * 64:(hd + 1) * 64]
                            v_ = vb_g[:, hd * 64:(hd + 1) * 64]
                            mm(o_, A1[:, hd, :], v_, hd == 0, [r_A1, r_vbg], [rbG])
                            mm(o_, A2[:, hd, :], v_, False, [r_A2, r_vbg], [rbG])
                            mm(o_, qpT_m[:, hd, :], Sg_b[:], False, [r_qpTm, r_Sgb], [rbG])
                        norm_gate(bG, rbG, st_g, r_stg, gA_g, r_gAg, gB_g, r_gBg, mx[:, 768:1024], rmx)
                        yield
                        mm(bG[:, 0:256], kdg[:], vb_g[:], True, [r_kdg, r_vbg], [rbG])
                        ts(Sg32[:], Sg32[:], edec[:, 0:1], None, ALU.mult, None, [r_Sg32, r_edec], [r_Sg32])
                        for hd in range(4):
                            stt(Sg32[:], bG[:, hd * 64:(hd + 1) * 64], cst[:, hm0 + hd:hm0 + hd + 1], Sg32[:],
                                ALU.mult, ALU.add, [r_Sg32, r_cst, rbG], [r_Sg32])
                        cp(Sg_b[:], Sg32[:], [r_Sg32], [r_Sgb], eng="pool")

                    def mla_stream(t):
                        par = t % 2
                        g3, rg3 = nextM()
                        inproj_to(g3, rg3, 1440, 1952, t)
                        for hz in range(2):
                            sigmoid_from(g3[:, hz * 256:(hz + 1) * 256], [rg3], 256)
                            tt(gBm[par][:, hz * 256:(hz + 1) * 256], g3[:, hz * 256:(hz + 1) * 256], sg[:, 0:256], ALU.mult,
                               [rg3, r_sg], [r_gBm[par]])
                        yield
                        g2, rg2 = nextM()
                        inproj_to(g2, rg2, 1024, 1440, t)
                        act(latn[:, 0:256], g2[:, 0:256], AF.Square, [rg2], [r_latn, r_stm], accum=st_m[:, 4:5])
                        act(latn[:, 256:384], g2[:, 256:384], AF.Square, [rg2], [r_latn, r_stm], accum=st_m[:, 5:6])
                        cp(krf[:], g2[:, 384:416], [rg2], [r_krf], eng="act")
                        rstd_from(st_m[:, 4:5], st_m[:, 8:9], 256, [r_stm], [r_stm], st_m[:, 12:13])
                        rstd_from(st_m[:, 5:6], st_m[:, 9:10], 128, [r_stm], [r_stm], st_m[:, 13:14])
                        ts(latn[:, 0:256], g2[:, 0:256], st_m[:, 8:9], None, ALU.mult, None, [rg2, r_stm], [r_latn])
                        ts(latn[:, 256:384], g2[:, 256:384], st_m[:, 9:10], None, ALU.mult, None, [rg2, r_stm], [r_latn])
                        cm = Tcos[:, t, 32:48]
                        sm = Tsin[:, t, 32:48]
                        tt(tA[:, 0:16], krf[:, 0:16], cm, ALU.mult, [r_krf, r_Tcos], [r_tA], eng="pool")
                        tt(tB[:, 0:16], krf[:, 16:32], sm, ALU.mult, [r_krf, r_Tsin], [r_tB], eng="pool")
                        tt(kpe[:, 0:16], tA[:, 0:16], tB[:, 0:16], ALU.subtract, [r_tA, r_tB], [r_kpe], eng="pool")
                        tt(tA[:, 0:16], krf[:, 16:32], cm, ALU.mult, [r_krf, r_Tcos], [r_tA], eng="pool")
                        tt(tB[:, 0:16], krf[:, 0:16], sm, ALU.mult, [r_krf, r_Tsin], [r_tB], eng="pool")
                        tt(kpe[:, 16:32], tA[:, 0:16], tB[:, 0:16], ALU.add, [r_tA, r_tB], [r_kpe], eng="pool")
                        cp(Kfull[:, :, 64:96], kpe[:].unsqueeze(1).to_broadcast([128, 8, 32]), [r_kpe], [r_Kfull], eng="pool")
                        yield
                        bk, rbk = nextM()
                        bkb = bk.bitcast(BF16)
                        for j in range(3):
                            tr(bkb[:, j * 128:(j + 1) * 128], latn[:, j * 128:(j + 1) * 128], [r_latn], [rbk])
                        cp(latT[:], bkb[:, 0:384].rearrange("p (a b) -> p a b", b=128), [rbk], [r_latT])
                        yield
                        for hf in range(2):
                            bq, rbq = nextM()
                            for k in range(2):
                                mm(bq[:, 0:384], latT[:, k, :], w_uq_b[:, k, hf * 384:(hf + 1) * 384], k == 0,
                                   [r_latT, r_wuq], [rbq])
                            q3 = bq[:, 0:384].rearrange("p (h d) -> p h d", d=96)
                            Qh = Qb[:, hf * 4:(hf + 1) * 4, :]
                            cp(Qh[:, :, 0:64], q3[:, :, 0:64], [rbq], [r_Qb], eng="act")
                            cm4 = cm.unsqueeze(1).to_broadcast([128, 4, 16])
                            sm4 = sm.unsqueeze(1).to_broadcast([128, 4, 16])
                            tA4 = tA[:, 0:64].rearrange("p (h d) -> p h d", d=16)
                            tB4 = tB[:, 0:64].rearrange("p (h d) -> p h d", d=16)
                            tt(tA4, q3[:, :, 64:80], cm4, ALU.mult, [rbq, r_Tcos], [r_tA])
                            tt(tB4, q3[:, :, 80:96], sm4, ALU.mult, [rbq, r_Tsin], [r_tB])
                            tt(Qh[:, :, 64:80], tA4, tB4, ALU.subtract, [r_tA, r_tB], [r_Qb])
                            tt(tA4, q3[:, :, 80:96], cm4, ALU.mult, [rbq, r_Tcos], [r_tA])
                            tt(tB4, q3[:, :, 64:80], sm4, ALU.mult, [rbq, r_Tsin], [r_tB])
                            tt(Qh[:, :, 80:96], tA4, tB4, ALU.add, [r_tA, r_tB], [r_Qb])
                        yield
                        bk, rbk = nextM()
                        bkb = bk.bitcast(BF16)
                        for hd in range(8):
                            tr(bkb[0:96, hd * 128:(hd + 1) * 128], Qb[:, hd, :], [r_Qb], [rbk])
                        cp(QT2[par][:], bkb[0:96, :].rearrange("p (a b) -> p a b", b=128), [rbk], [r_QT2[par]])
                        yield
                        for hf in range(2):
                            bq, rbq = nextM()
                            mm(bq, latT[:, 2, :], w_ukv_b[:, hf * 512:(hf + 1) * 512], True, [r_latT, r_wukv], [rbq])
                            kv3 = bq.rearrange("p (h d) -> p h d", d=128)
                            cp(Kfull[:, hf * 4:(hf + 1) * 4, 0:64], kv3[:, :, 0:64], [rbq], [r_Kfull], eng="act")
                            cp(VA[:, t, hf * 4:(hf + 1) * 4, 0:64], kv3[:, :, 64:128], [rbq], [r_VA[t]], eng="act")
                        yield
                        bk, rbk = nextM()
                        bkb = bk.bitcast(BF16)
                        for hd in range(8):
                            tr(bkb[0:96, hd * 128:(hd + 1) * 128], Kfull[:, hd, :], [r_Kfull], [rbk])
                        cp(KT[:, :, t * 128:(t + 1) * 128], bkb[0:96, :].rearrange("p (a b) -> p a b", b=128), [rbk], [r_KT[t]],
                           eng="act")
                        if ada_todo:
                            yield
                            l_, g_ = ada_todo.pop(0)
                            bk, rbk = nextM()
                            ada_group(l_, g_, bk, rbk)

                    def phaseB(t, X, rX):
                        par = t % 2
                        QTt, rQTt = QT2[par], r_QT2[par]
                        mx, rmx = mixed2[par], r_mixed2[par]
                        sc_scale = float(96 ** -0.5)
                        acc = PP[1]
                        racc = r_bank[1]
                        scp = PP[3]
                        rsc = r_bank[3]
                        rot = [(scp[:, 0:512], rsc[0]), (scp[:, 512:1024], rsc[1]), bank_M[1]]

                        def qk_exp(j):
                            kt, hf = j // 2, j % 2
                            sb_, rsb_ = rot[j % 3]
                            P_ = PT[kt % 2]
                            rP = r_PT[kt % 2]
                            for i4 in range(4):
                                hd = 4 * hf + i4
                                mm(sb_[:, i4 * 128:(i4 + 1) * 128], KT[:, hd, kt * 128:(kt + 1) * 128], QTt[:, hd, :],
                                   i4 == 0, [r_KT[kt], rQTt], [rsb_])
                            act(P_[:, 4 * hf:4 * hf + 4, :].rearrange("p a b -> p (a b)"), sb_, AF.Exp, [rsb_], [rP[hf]],
                                scale=sc_scale)
                            if kt == t:
                                memset(P_[64:128, 4 * hf:4 * hf + 4, 0:64], 0.0, [rP[hf]], eng="pool")

                        def pv(j):
                            kt, hf = j // 2, j % 2
                            P_ = PT[kt % 2]
                            rP = r_PT[kt % 2]
                            for i4 in range(4):
                                hd = 4 * hf + i4
                                o_ = acc[:, hf * 512 + i4 * 65: hf * 512 + i4 * 65 + 65]
                                mm(o_, P_[:, hd, :], VA[:, kt, hd, :], (kt == 0 and i4 == 0), [rP[hf], r_VA[kt]], [racc[hf]])

                        nj = 2 * (t + 1)
                        qk_exp(0)
                        qk_exp(1)
                        for j in range(nj):
                            if j + 2 < nj:
                                qk_exp(j + 2)
                            pv(j)
                            if j % 2 == 1:
                                yield
                        for hf in range(2):
                            a3 = acc[:, hf * 512: hf * 512 + 260].rearrange("p (h d) -> p h d", d=65)
                            S.add("dve", lambda e, a3=a3, hf=hf: e.reciprocal(out=rs[:, hf * 4:(hf + 1) * 4].unsqueeze(2),
                                                                             in_=a3[:, :, 64:65]), [racc[hf]], [r_rs])
                            tt(gAm[:, hf * 256:(hf + 1) * 256].rearrange("p (h d) -> p h d", d=64), a3[:, :, 0:64],
                               rs[:, hf * 4:(hf + 1) * 4].unsqueeze(2).to_broadcast([128, 4, 64]), ALU.mult,
                               [racc[hf], r_rs], [r_gAm])
                        tt(mx[:, 256:768], gAm[:], gBm[par][:], ALU.mult, [r_gAm, r_gBm[par]], [rmx], eng="pool")
                        yield
                        stage_(8)
                        bk = scp[:, 0:512]
                        rbk = rsc[0]
                        bkb = bk.bitcast(BF16)
                        for k in range(8):
                            tr(bkb[:, k * 128:(k + 1) * 128], mx[:, k * 128:(k + 1) * 128], [rmx], [rbk])
                        cp(mixT[:], bkb.rearrange("p (a b) -> p a b", b=128), [rbk], [r_mixT], eng="act")
                        yield
                        for hf in range(2):
                            bo = scp[:, (1 - hf) * 512:(2 - hf) * 512]
                            rbo = rsc[1 - hf]
                            for k in range(8):
                                mm(bo, mixT[:, k, :], w_out_b[:, k, hf * 512:(hf + 1) * 512], k == 0, [r_mixT, r_wout], [rbo])
                            tt(gAm[:], bo, gate_bc[:, hf * 512:(hf + 1) * 512], ALU.mult, [rbo, r_gate], [r_gAm])
                            tt(X[:, hf * 512:(hf + 1) * 512], X[:, hf * 512:(hf + 1) * 512], gAm[:], ALU.add,
                               [rX, r_gAm], [rX], eng="pool")
                            yield
                        if last_layer:
                            act(mixT[:].rearrange("p a b -> p (a b)"), X[:], AF.Square, [rX], [r_mixT, r_stb], accum=st_b[:, 0:1])
                            rstd_from(st_b[:, 0:1], st_b[:, 2:3], D, [r_stb], [r_stb], st_b[:, 1:2])
                            stt(X[:], X[:], st_b[:, 2:3], fnorm[:], ALU.mult, ALU.mult, [rX, r_stb, r_fnorm], [rX])
                        dma(y_d[s, t * 128:(t + 1) * 128, :], X[:], [rX], [r_y[s][t]])

                    def drain(g):
                        for _ in g:
                            pass

                    def interleave(ga, gb, na, nb):
                        ia = ib = 0
                        da = db = False
                        while not (da and db):
                            if db or (not da and ia * nb <= ib * na):
                                try:
                                    next(ga)
                                    ia += 1
                                except StopIteration:
                                    da = True
                            else:
                                try:
                                    next(gb)
                                    ib += 1
                                except StopIteration:
                                    db = True

                    def round_robin(gens, weights):
                        active = list(range(len(gens)))
                        while active:
                            for i in list(active):
                                for _ in range(weights[i]):
                                    try:
                                        next(gens[i])
                                    except StopIteration:
                                        active.remove(i)
                                        break

                    Xs = {}

                    def alloc_x(t):
                        xi = xslot["i"] % 3
                        xslot["i"] += 1
                        Xs[t] = (xt[xi], r_xt[xi])
                        return Xs[t]

                    def chain2(g1, g2):
                        for _ in g1:
                            yield
                        yield
                        for _ in g2:
                            yield

                    prevB = None
                    drain(hT_stage(0, *alloc_x(0)))
                    drain(gla_gate(0))
                    for t in range(NT):
                        gens = [gla_stream(t), mla_stream(t), ret_stream(t)]
                        weights = [1, 1, 1]
                        if not PIPE:
                            if prevB is not None:
                                drain(prevB)
                            for g_ in gens:
                                drain(g_)
                            if t + 1 < NT:
                                drain(hT_stage(t + 1, *alloc_x(t + 1)))
                                drain(gla_gate(t + 1))
                        else:
                            if t + 1 < NT:
                                gens.append(chain2(hT_stage(t + 1, *alloc_x(t + 1)), gla_gate(t + 1)))
                                weights.append(1)
                            if prevB is not None:
                                gens.insert(0, prevB)
                                weights.insert(0, max(1, -(-(t + 4) // 8)))
                            round_robin(gens, weights)
                        prevB = phaseB(t, *Xs[t])
                    drain(prevB)

        except StopBuild:
            pass
        S.add("sp", None, [r_y[s][t] for s in range(NSEQ) for t in range(NT)], [])

        if os.environ.get("KDBG"):
            print("SBUF remaining", nc.sbuf_bytes_remaining, "ops", {e: len(v) for e, v in S.ops.items()})
        S.finalize(sems, dsems)
        with nc.Block() as block:
            @block.sync
            def _(e):
                S.replay("sp", e)

            @block.gpsimd
            def _(e):
                S.replay("pool", e)

            @block.vector
            def _(e):
                S.replay("dve", e)

            @block.scalar
            def _(e):
                S.replay("act", e)

            @block.tensor
            def _(e):
                S.replay("pe", e)
    return nc


def run(inputs, NSEQ, NT, L, ncores=NCORES, trace=False):
    f = np.float32
    x = np.asarray(inputs["x"], f)
    c = np.asarray(inputs["c"], f)
    pos = np.asarray(inputs["positions"], np.int32)
    consts = _const_pack(L, np.asarray(inputs["norm_w"], f), np.asarray(inputs["mla_q_norm"], f),
                         np.asarray(inputs["mla_kv_norm"], f), np.asarray(inputs["gla_norm"], f),
                         np.asarray(inputs["gla_b_g2"], f))
    fnorm = np.ascontiguousarray(np.broadcast_to(np.asarray(inputs["final_norm"], f)[None, :], (128, D)))
    ada_b = np.asarray(inputs["ada_b"], f).reshape(1, L * 3 * D)
    ada_b_rep = np.ascontiguousarray(np.broadcast_to(ada_b, (NSEQ, L * 3 * D)))
    shared = {
        "consts": consts, "fnorm": fnorm, "ada_w": np.ascontiguousarray(inputs["ada_w"], f), "ada_b": ada_b_rep,
        "w_in": np.ascontiguousarray(inputs["w_in"], f), "w_uq": np.ascontiguousarray(inputs["w_uq"], f),
        "w_ukv": np.ascontiguousarray(inputs["w_ukv"], f), "w_g2": np.ascontiguousarray(inputs["gla_w_g2"], f),
        "b_g2": np.ascontiguousarray(inputs["gla_b_g2"], f), "w_out": np.ascontiguousarray(inputs["w_out"], f),
    }
    in_maps = []
    for ci in range(ncores):
        sl = slice(ci * NSEQ, (ci + 1) * NSEQ)
        xc = np.ascontiguousarray(x[sl])
        cc = c[sl]
        cT = np.ascontiguousarray(cc.reshape(NSEQ, 8, 128).transpose(2, 1, 0).reshape(128, 8 * NSEQ))
        pc = pos[sl]
        pT = np.ascontiguousarray(pc.reshape(NSEQ, NT, 128).transpose(2, 0, 1).reshape(128, NSEQ * NT))
        m = {"x": xc, "cT": cT, "pos": pT}
        m.update(shared)
        in_maps.append(m)
    nc = build(NSEQ, NT, L)
    res = run_bass_kernel_spmd(nc, in_maps, core_ids=list(range(ncores)), trace=trace)
    out = np.concatenate([r["y"] for r in res.results], axis=0)
    return out, res


def kernel(**inputs):
    out, _ = run(inputs, NSEQ=4, NT=16, L=2)
    return out.astype(np.float32)
```
